# Optimizing a Trainium2 kernel written in Bass

```python
import math
import jax, jax.numpy as jnp
from jax import lax
import numpy as np

D_MODEL = 1024
BATCH = 4
SEQ = 4096
DEPTH = 2

CHUNK = 64
Q_BLOCK = 128
HEAD_DIM = 64
D_MIX = D_MODEL
D_FF = 4 * D_MODEL
EPS = 1e-6
ROPE_THETA = 500000.0
ROPE_DIM = HEAD_DIM // 4

A_HEADS = 4
A_QK_DIM = HEAD_DIM
A_V_DIM = 2 * HEAD_DIM
A_WIDTH = A_HEADS * A_V_DIM
B_HEADS = 4
B_LEFT_CHUNKS = 8
B_BAND = B_LEFT_CHUNKS + 1
REL_CLIP = 128
B_WIDTH = B_HEADS * HEAD_DIM
C_HEADS = 4
C_WIDTH = C_HEADS * HEAD_DIM

IN_SPLIT_SIZES = [A_HEADS * 2 * A_QK_DIM, A_HEADS * 2 * A_QK_DIM, A_WIDTH,
                  B_WIDTH, B_WIDTH, B_WIDTH,
                  C_WIDTH, C_WIDTH, C_WIDTH]
D_IN = sum(IN_SPLIT_SIZES)
IN_SPLITS = [int(v) for v in np.cumsum(IN_SPLIT_SIZES)[:-1]]

kernel_name = "hybrid_diff_chunkrel_stickbreak_block"


def rms_norm(x, g):
    xf = x.astype(jnp.float32)
    y = xf * lax.rsqrt(jnp.mean(xf * xf, axis=-1, keepdims=True) + EPS)
    return (y * g.astype(jnp.float32)).astype(x.dtype)


def head_rms_norm(o, g, n_heads):
    bsz, s, w = o.shape
    d = w // n_heads
    y = rms_norm(o.reshape(bsz, s, n_heads, d), g.reshape(n_heads, d))
    return y.reshape(bsz, s, w)


def rope_tables(seq):
    pos = jnp.arange(seq, dtype=jnp.float32)
    inv_freq = ROPE_THETA ** (-jnp.arange(0, ROPE_DIM, 2, dtype=jnp.float32) / ROPE_DIM)
    ang = pos[:, None] * inv_freq[None, :]
    return jnp.cos(ang), jnp.sin(ang)


def partial_rope(x, cos, sin):
    s = x.shape[1]
    shp = (1, s) + (1,) * (x.ndim - 3) + (ROPE_DIM // 2,)
    c = cos.reshape(shp).astype(x.dtype)
    sn = sin.reshape(shp).astype(x.dtype)
    x1 = x[..., : ROPE_DIM // 2]
    x2 = x[..., ROPE_DIM // 2: ROPE_DIM]
    rot = jnp.concatenate([x1 * c - x2 * sn, x2 * c + x1 * sn], axis=-1)
    return jnp.concatenate([rot, x[..., ROPE_DIM:]], axis=-1)


def diff_attention(q, k, v, lam, lam_init, subln_g):
    bsz, s = q.shape[0], q.shape[1]
    scale = A_QK_DIM ** -0.5
    pos = jnp.arange(s)
    outs = []
    for b0 in range(0, s, Q_BLOCK):
        e = b0 + Q_BLOCK
        sc = jnp.einsum('bqhnd,bkhnd->bhnqk', q[:, b0:e], k[:, :e]).astype(jnp.float32) * scale
        mask = (pos[:e][None, :] // CHUNK) <= (pos[b0:e][:, None] // CHUNK)
        sc = jnp.where(mask[None, None, None], sc, -jnp.inf)
        p = jax.nn.softmax(sc, axis=-1)
        w = p[:, :, 0] - lam * p[:, :, 1]
        outs.append(jnp.einsum('bhqk,bkhe->bqhe', w.astype(v.dtype), v[:, :e]))
    o = jnp.concatenate(outs, axis=1)
    o = rms_norm(o, subln_g) * (1.0 - lam_init)
    return o.reshape(bsz, s, A_WIDTH)


def chunk_rel_attention(q, k, v, rel_bias):
    bsz, s, h, d = q.shape
    nc = s // CHUNK
    qc = q.reshape(bsz, nc, CHUNK, h, d)
    pad = [(0, 0), (B_LEFT_CHUNKS * CHUNK, 0), (0, 0), (0, 0)]
    kp = jnp.pad(k, pad).reshape(bsz, nc + B_LEFT_CHUNKS, CHUNK, h, d)
    vp = jnp.pad(v, pad).reshape(bsz, nc + B_LEFT_CHUNKS, CHUNK, h, d)
    kband = jnp.concatenate([kp[:, j:j + nc] for j in range(B_BAND)], axis=2)
    vband = jnp.concatenate([vp[:, j:j + nc] for j in range(B_BAND)], axis=2)
    sc = jnp.einsum('bcqhd,bckhd->bchqk', qc, kband).astype(jnp.float32) * (d ** -0.5)
    qi = jnp.arange(CHUNK)
    kj = jnp.arange(B_BAND * CHUNK)
    rel = qi[:, None] + B_LEFT_CHUNKS * CHUNK - kj[None, :]
    idx = jnp.clip(rel, -REL_CLIP, REL_CLIP) + REL_CLIP
    bias = rel_bias[:, idx].astype(jnp.float32)
    ci = jnp.arange(nc)
    valid = (ci[:, None] - B_LEFT_CHUNKS + kj[None, :] // CHUNK) >= 0
    sc = jnp.where(valid[None, :, None, None, :], sc + bias[None, None], -jnp.inf)
    p = jax.nn.softmax(sc, axis=-1)
    o = jnp.einsum('bchqk,bckhd->bcqhd', p.astype(v.dtype), vband)
    return o.reshape(bsz, s, h * d)


def stick_breaking_attention(q, k, v):
    bsz, s, h, d = q.shape
    scale = d ** -0.5
    pos = jnp.arange(s)
    outs = []
    for b0 in range(0, s, Q_BLOCK):
        e = b0 + Q_BLOCK
        z = jnp.einsum('bqhd,bkhd->bhqk', q[:, b0:e], k[:, :e]).astype(jnp.float32) * scale
        causal = (pos[:e][None, :] < pos[b0:e][:, None])[None, None]
        log_beta = jax.nn.log_sigmoid(z)
        log_stay = jnp.where(causal, jax.nn.log_sigmoid(-z), 0.0)
        rem = lax.cumsum(log_stay, axis=3, reverse=True) - log_stay
        a = jnp.where(causal, jnp.exp(log_beta + rem), 0.0)
        outs.append(jnp.einsum('bhqk,bkhd->bqhd', a.astype(v.dtype), v[:, :e]))
    return jnp.concatenate(outs, axis=1).reshape(bsz, s, h * d)


def setup_inputs(seed: int = 0) -> dict:
    key = jax.random.key(seed)
    ks = jax.random.split(key, 20)
    f32 = jnp.float32
    nrm = lambda k, shp, sc: jax.random.normal(k, shp, f32) * sc
    gain = lambda k, n: 1.0 + 0.05 * jax.random.normal(k, (DEPTH, n), f32)
    return {
        "x": jax.random.normal(ks[0], (BATCH, SEQ, D_MODEL), f32),
        "norm_pre_mix": gain(ks[1], D_MODEL),
        "w_in": nrm(ks[2], (DEPTH, D_MODEL, D_IN), D_MODEL ** -0.5),
        "lam_q1": nrm(ks[3], (DEPTH, A_QK_DIM), 0.1),
        "lam_k1": nrm(ks[4], (DEPTH, A_QK_DIM), 0.1),
        "lam_q2": nrm(ks[5], (DEPTH, A_QK_DIM), 0.1),
        "lam_k2": nrm(ks[6], (DEPTH, A_QK_DIM), 0.1),
        "subln_a": gain(ks[7], A_V_DIM),
        "rel_bias": nrm(ks[8], (DEPTH, B_HEADS, 2 * REL_CLIP + 1), 0.2),
        "gn_b": gain(ks[9], B_WIDTH),
        "gn_c": gain(ks[10], C_WIDTH),
        "w_out": nrm(ks[11], (DEPTH, D_MIX, D_MODEL), D_MIX ** -0.5),
        "norm_post_mix": gain(ks[12], D_MODEL),
        "norm_pre_mlp": gain(ks[13], D_MODEL),
        "w_up": nrm(ks[14], (DEPTH, D_MODEL, D_FF), D_MODEL ** -0.5),
        "w_down": nrm(ks[15], (DEPTH, D_FF, D_MODEL), D_FF ** -0.5),
        "norm_post_mlp": gain(ks[16], D_MODEL),
    }


def reference(x, norm_pre_mix, w_in, lam_q1, lam_k1, lam_q2, lam_k2, subln_a, rel_bias,
              gn_b, gn_c, w_out, norm_post_mix, norm_pre_mlp, w_up, w_down, norm_post_mlp):
    bsz, s, _ = x.shape
    cos, sin = rope_tables(s)
    for l in range(DEPTH):
        h = rms_norm(x, norm_pre_mix[l])
        proj = h @ w_in[l]
        a_q, a_k, a_v, b_q, b_k, b_v, c_q, c_k, c_v = jnp.split(proj, IN_SPLITS, axis=-1)

        lam_init = 0.8 - 0.6 * math.exp(-0.3 * l)
        lam = (jnp.exp(jnp.sum(lam_q1[l].astype(jnp.float32) * lam_k1[l].astype(jnp.float32)))
               - jnp.exp(jnp.sum(lam_q2[l].astype(jnp.float32) * lam_k2[l].astype(jnp.float32)))
               + lam_init)
        qa = partial_rope(a_q.reshape(bsz, s, A_HEADS, 2, A_QK_DIM), cos, sin)
        ka = partial_rope(a_k.reshape(bsz, s, A_HEADS, 2, A_QK_DIM), cos, sin)
        va = a_v.reshape(bsz, s, A_HEADS, A_V_DIM)
        o_a = diff_attention(qa, ka, va, lam, lam_init, subln_a[l])

        o_b = chunk_rel_attention(b_q.reshape(bsz, s, B_HEADS, HEAD_DIM),
                                  b_k.reshape(bsz, s, B_HEADS, HEAD_DIM),
                                  b_v.reshape(bsz, s, B_HEADS, HEAD_DIM), rel_bias[l])
        o_b = head_rms_norm(o_b, gn_b[l], B_HEADS)

        o_c = stick_breaking_attention(c_q.reshape(bsz, s, C_HEADS, HEAD_DIM),
                                       c_k.reshape(bsz, s, C_HEADS, HEAD_DIM),
                                       c_v.reshape(bsz, s, C_HEADS, HEAD_DIM))
        o_c = head_rms_norm(o_c, gn_c[l], C_HEADS)

        y = jnp.concatenate([o_a, o_b, o_c], axis=-1) @ w_out[l]
        x = x + rms_norm(y, norm_post_mix[l])

        h = rms_norm(x, norm_pre_mlp[l])
        m = jnp.square(jax.nn.relu(h @ w_up[l])) @ w_down[l]
        x = x + rms_norm(m, norm_post_mlp[l])
    return x
```

```python
import math
import types
import numpy as np
import ml_dtypes
import concourse.bass as bass
import concourse.mybir as mybir
from concourse.ap import AP
from concourse.bass_utils import run_bass_kernel_spmd
from contextlib import ExitStack

F32 = mybir.dt.float32
BF16 = mybir.dt.bfloat16
U8 = mybir.dt.uint8
ALU = mybir.AluOpType
AF = mybir.ActivationFunctionType
AX = mybir.AxisListType

D = 1024
DFF = 4096
DIN = 3072
EPS = 1e-6
SAME_ENG_RAW_SYNC = True


class Buf:
    __slots__ = ("name", "w", "r")

    def __init__(self, name):
        self.name = name
        self.w = {}
        self.r = {}


class Op:
    __slots__ = ("eng", "fn", "deps", "signal", "token", "is_dma", "dkey", "extra_waits")

    def __init__(self, eng, fn, is_dma=False, dkey=None):
        self.eng = eng
        self.fn = fn
        self.deps = set()
        self.signal = False
        self.token = None
        self.is_dma = is_dma
        self.dkey = dkey
        self.extra_waits = []


def _freeze(fn):
    if fn.__closure__ is None:
        return fn
    cells = []
    for c in fn.__closure__:
        try:
            cells.append(types.CellType(c.cell_contents))
        except ValueError:
            cells.append(c)
    return types.FunctionType(fn.__code__, fn.__globals__, fn.__name__, fn.__defaults__, tuple(cells))


class Sched:
    ENGS = ("sp", "act", "dve", "pool", "pe")

    def __init__(self):
        self.ops = {e: [] for e in self.ENGS}
        self.all = []
        self.last_op = {e: None for e in self.ENGS}
        self.dma_keys = {}
        self.dma_last = {}
        self.pending_barrier = {e: [] for e in self.ENGS}

    def _deps(self, op, reads, writes):
        for b in reads:
            for e, w in b.w.items():
                if w is op:
                    continue
                if w.is_dma or e != op.eng or op.is_dma or (SAME_ENG_RAW_SYNC and e != "pe"):
                    op.deps.add(w)
        for b in writes:
            for e, w in b.w.items():
                if w is op:
                    continue
                if w.is_dma and op.is_dma and w.dkey == op.dkey and w.eng == op.eng:
                    op.deps |= w.deps
                    continue
                if w.is_dma or e != op.eng or op.is_dma or e != "pe":
                    op.deps.add(w)
            for e, r in b.r.items():
                if r is op:
                    continue
                if r.is_dma or e != op.eng or op.is_dma or e != "pe":
                    op.deps.add(r)
        for b in reads:
            b.r[op.eng if not op.is_dma else ("dma", id(op))] = op
        for b in writes:
            if op.is_dma:
                prev = {k: v for k, v in b.w.items() if v.is_dma and v.dkey == op.dkey}
                b.w = prev
                b.w[("dma", id(op))] = op
            else:
                b.w = {op.eng: op}
            b.r = {}

    def op(self, eng, fn, reads=(), writes=()):
        o = Op(eng, _freeze(fn))
        self._deps(o, reads, writes)
        o.deps |= set(self.pending_barrier[eng])
        self.pending_barrier[eng] = []
        self.ops[eng].append(o)
        self.all.append(o)
        self.last_op[eng] = o
        return o

    def dma(self, fn, key, reads=(), writes=(), eng="sp"):
        o = Op(eng, _freeze(fn), is_dma=True, dkey=key)
        self._deps(o, reads, writes)
        o.deps |= set(self.pending_barrier[eng])
        self.pending_barrier[eng] = []
        self.dma_keys[key] = self.dma_keys.get(key, 0) + 1
        o.token = ("dma:" + key, 16 * self.dma_keys[key])
        o.signal = True
        self.dma_last[key] = o
        self.ops[eng].append(o)
        self.all.append(o)
        return o

    def barrier(self):
        toks = [o for o in self.last_op.values() if o is not None]
        toks += list(self.dma_last.values())
        for e in self.ENGS:
            self.pending_barrier[e] = list(toks)

    def finalize(self, nc, es):
        for o in self.all:
            for d in o.deps:
                d.signal = True
        for e in self.ENGS:
            for d in self.pending_barrier[e]:
                d.signal = True
        cnt = {e: 0 for e in self.ENGS}
        for e in self.ENGS:
            for o in self.ops[e]:
                if o.is_dma:
                    continue
                if o.signal:
                    cnt[e] += 1
                    o.token = ("eng:" + e, cnt[e])
        names = set()
        for o in self.all:
            if o.signal:
                names.add(o.token[0])
        self.sems = {}
        for n in sorted(names):
            self.sems[n] = es.enter_context(nc.semaphore(n.replace(":", "_")))
        final_waits = {}
        self.n_waits = 0
        for e in self.ENGS:
            waited = {}
            for o in self.ops[e]:
                need = {}
                for d in o.deps:
                    k, v = d.token
                    if need.get(k, 0) < v:
                        need[k] = v
                o.extra_waits = []
                for k, v in need.items():
                    if waited.get(k, 0) < v:
                        waited[k] = v
                        o.extra_waits.append((k, v))
                        self.n_waits += 1
            need = {}
            for d in self.pending_barrier[e]:
                k, v = d.token
                if need.get(k, 0) < v:
                    need[k] = v
            final_waits[e] = [(k, v) for k, v in need.items() if waited.get(k, 0) < v]
        block = es.enter_context(nc.Block())
        sems = self.sems

        def make(ename):
            ops = self.ops[ename]
            fw = final_waits[ename]

            def run(eng):
                for o in ops:
                    for k, v in o.extra_waits:
                        eng.wait_ge(sems[k], v)
                    ins = o.fn(eng)
                    if o.signal:
                        k, v = o.token
                        ins.then_inc(sems[k], 16 if o.is_dma else 1)
                for k, v in fw:
                    eng.wait_ge(sems[k], v)
            return run

        block.sync(make("sp"))
        block.scalar(make("act"))
        block.vector(make("dve"))
        block.gpsimd(make("pool"))
        block.tensor(make("pe"))


class Arena:
    def __init__(self, tensor_ap, nbytes):
        self.t = tensor_ap
        self.n = nbytes
        self.off = 0
        self.namecnt = {}
        self.peak = 0

    def mark(self):
        return self.off

    def reset(self, m):
        self.off = m

    def alloc(self, name, free_elems, dtype):
        esz = 4 if dtype == F32 else 2
        nb = free_elems * esz
        nb_al = (nb + 63) // 64 * 64
        assert self.off + nb_al <= self.n, f"arena overflow {name}: {self.off}+{nb_al}>{self.n}"
        v = self.t[:, self.off:self.off + nb].bitcast(dtype)
        self.off += nb_al
        self.peak = max(self.peak, self.off)
        idx = self.namecnt.get(name, 0)
        self.namecnt[name] = idx + 1
        return v, Buf(f"{name}_{idx}")


def bc_mid(ap2, m):
    a = ap2.ap
    return AP(ap2.tensor, ap2.offset, [list(a[0]), [0, m], list(a[1])])


def bc_last(ap2, n):
    a = ap2.ap
    return AP(ap2.tensor, ap2.offset, [list(a[0]), list(a[1]), [0, n]])


def build(S=4096, L=2, dbg=False):
    NT = S // 128
    NQ = S // 256
    NB = S // 512
    NM = S // 256
    nc = bass.Bass("TRN2", target_bir_lowering=False)

    def din(name, shape, dt=F32):
        return nc.dram_tensor(name, shape, dt, kind="ExternalInput").ap()

    def dscr(name, shape, dt):
        return nc.dram_tensor(name, shape, dt, kind="Internal").ap()

    x_d = din("x", [S, D])
    w_in_d = din("w_in", [L, D, DIN])
    w_out_d = din("w_out", [L, D, D])
    w_up_d = din("w_up", [L, D, DFF])
    w_down_d = din("w_down", [L, DFF, D])
    g_pre_mix_d = din("norm_pre_mix", [L, D])
    g_post_mix_d = din("norm_post_mix", [L, D])
    g_pre_mlp_d = din("norm_pre_mlp", [L, D])
    g_post_mlp_d = din("norm_post_mlp", [L, D])
    lam_d = [din(n, [L, 64]) for n in ("lam_q1", "lam_k1", "lam_q2", "lam_k2")]
    subln_d = din("subln_a", [L, 128])
    relb_d = din("rel_bias", [L, 4, 257])
    gnb_d = din("gn_b", [L, 256])
    gnc_d = din("gn_c", [L, 256])
    ident_d = din("c_ident", [128, 128], BF16)
    ropeC_d = din("c_ropeC", [128, S])
    ropeS_d = din("c_ropeS", [128, S])
    mcc_d = din("c_mask_cc", [128, 128], BF16)
    mtri_d = din("c_mask_tri", [128, 128], BF16)
    negU_d = din("c_negU", [128, 128], BF16)
    negOnes_d = din("c_negOnes", [128, 128], BF16)
    mB_d = din("c_maskB", [128, 2 * 128])
    antiI_d = din("c_antiI", [128, 128])
    out_d = nc.dram_tensor("out", [S, D], F32, kind="ExternalOutput").ap()
    ocat_d = dscr("ocat", [S, D], BF16)
    xa_d = dscr("xa", [S, D], F32)
    xb_d = dscr("xb", [S, D], F32)
    h2T_d = dscr("h2T", [NT, 128, 8 * 128], BF16)
    E_d = dscr("Eext", [4, 384], F32)
    dbg_d = {}
    if dbg:
        dbg_d["ocat"] = nc.dram_tensor("dbg_ocat", [S, D], BF16, kind="ExternalOutput").ap()
        dbg_d["xa"] = nc.dram_tensor("dbg_xa", [S, D], F32, kind="ExternalOutput").ap()

    Sd = Sched()
    es = ExitStack()
    ARENA_BYTES = 206 * 1024
    arena_t = es.enter_context(nc.sbuf_tensor("arena", [128, ARENA_BYTES], U8))
    A = Arena(arena_t, ARENA_BYTES)
    pbank = []
    for i in range(8):
        t = es.enter_context(nc.psum_tensor(f"pb{i}", [128, 512], F32))
        pbank.append((t, Buf(f"pb{i}")))

    def pbf(i):
        return pbank[i][0][:, 0:512].bitcast(BF16)

    def dump(name, ap, buf):
        if not dbg:
            return
        t = nc.dram_tensor("dbg_" + name, list(ap.shape), ap.dtype, kind="ExternalOutput").ap()
        Sd.dma(lambda e: e.dma_start(out=t, in_=ap), "dbgdump_" + name, reads=[buf])

    ident, ident_b = A.alloc("ident", 128, BF16)
    mcc, mcc_b = A.alloc("mcc", 128, BF16)
    mtri, mtri_b = A.alloc("mtri", 128, BF16)
    negU, negU_b = A.alloc("negU", 128, BF16)
    negOnes, negOnes_b = A.alloc("negOnes", 128, BF16)
    for (t, b, d) in ((ident, ident_b, ident_d), (mcc, mcc_b, mcc_d), (mtri, mtri_b, mtri_d),
                      (negU, negU_b, negU_d), (negOnes, negOnes_b, negOnes_d)):
        Sd.dma(lambda e, t=t, d=d: e.dma_start(out=t, in_=d), b.name, writes=[b])

    def rstd_ops(ssq, out, n, rb):
        Sd.op("act", lambda e: e.activation(out=out, in_=ssq, func=AF.Ln, scale=1.0 / n, bias=eps_t[:, 0:1]),
              reads=[rb, eps_b], writes=[rb])
        Sd.op("act", lambda e: e.activation(out=out, in_=out, func=AF.Exp, scale=-0.5), reads=[rb], writes=[rb])

    eps_t, eps_b = A.alloc("eps", 1, F32)
    Sd.op("pool", lambda e: e.memset(eps_t, EPS), writes=[eps_b])

    base_mark = A.mark()

    def layer(l, x_src, x_dst):
        lam_init = 0.8 - 0.6 * math.exp(-0.3 * l)
        A.reset(base_mark)
        A.namecnt = dict(base_names)
        hT, hT_b = A.alloc("hT", 8 * S, BF16)
        hT3 = hT.rearrange("p (k s) -> p k s", k=8)
        p12_mark = A.mark()

        xin = [A.alloc("xin", D, F32) for _ in range(4)]
        hb = [A.alloc("hb", D, BF16) for _ in range(3)]
        junks = [A.alloc("junk", D, BF16) for _ in range(4)]
        st = [A.alloc("st", 4, F32) for _ in range(4)]

        def p1_load(i):
            xt, xt_b = xin[i % 4]
            Sd.dma(lambda e: e.dma_start(out=xt, in_=x_src[i * 128:(i + 1) * 128, :]), xt_b.name, writes=[xt_b])

        def p1_sq(i):
            xt, xt_b = xin[i % 4]
            s_, s_b = st[i % 4]
            junk, junk_b = junks[i % 4]
            Sd.op("act", lambda e: e.activation(out=junk, in_=xt, func=AF.Square, accum_out=s_[:, 0:1]),
                  reads=[xt_b], writes=[junk_b, s_b])
            rstd_ops(s_[:, 0:1], s_[:, 1:2], D, s_b)

        def p1_scale(i):
            xt, xt_b = xin[i % 4]
            ht, ht_b = hb[i % 3]
            s_, s_b = st[i % 4]
            Sd.op("dve", lambda e: e.tensor_scalar(out=ht, in0=xt, scalar1=s_[:, 1:2], scalar2=None, op0=ALU.mult),
                  reads=[xt_b, s_b], writes=[ht_b])

        def p1_tr(i):
            ht, ht_b = hb[i % 3]
            pv = pbf(i % 2)

            def tr(e):
                ins = None
                for kc in range(8):
                    ins = e.transpose(out=pv[:, kc * 128:(kc + 1) * 128], in_=ht[:, kc * 128:(kc + 1) * 128], identity=ident)
                return ins
            Sd.op("pe", tr, reads=[ht_b, ident_b], writes=[pbank[i % 2][1]])

        def p1_evac(i):
            pv = pbf(i % 2)
            if i % 2:
                Sd.op("dve", lambda e: e.tensor_copy(out=hT3[:, :, i * 128:(i + 1) * 128], in_=pv.rearrange("p (k t) -> p k t", k=8)),
                      reads=[pbank[i % 2][1]], writes=[hT_b])
            else:
                Sd.op("act", lambda e: e.activation(out=hT3[:, :, i * 128:(i + 1) * 128], in_=pv.rearrange("p (k t) -> p k t", k=8), func=AF.Copy),
                      reads=[pbank[i % 2][1]], writes=[hT_b])

        for i in range(min(2, NT)):
            p1_load(i)
        for j in range(NT + 3):
            if 0 <= j - 3 < NT:
                p1_evac(j - 3)
            if j < NT:
                p1_sq(j)
            if 0 <= j - 2 < NT:
                p1_tr(j - 2)
            if 0 <= j - 1 < NT:
                p1_scale(j - 1)
            if j + 2 < NT:
                p1_load(j + 2)
        Sd.barrier()
        A.reset(p12_mark)

        gin, gin_b = A.alloc("gin", 8, F32)
        Sd.dma(lambda e: e.dma_start(out=gin, in_=g_pre_mix_d[l].rearrange("(k p) -> p k", p=128), allow_slow_non_contiguous=True),
               gin_b.name, writes=[gin_b])
        stage = [A.alloc("stage", 8 * 384, F32) for _ in range(2)]
        wb = [A.alloc("wb", 8 * 384, BF16) for _ in range(2)]
        wslot = [0]

        def load_w(col_groups):
            k = wslot[0] % 2
            wslot[0] += 1
            ntot = sum(n for _, n in col_groups)
            sg, sg_b = stage[k]
            w_, w_b = wb[k]
            sv = sg[:, 0:8 * ntot].rearrange("p (k n) -> p k n", k=8)
            wv = w_[:, 0:8 * ntot].rearrange("p (k n) -> p k n", k=8)
            o = 0
            for (c0, n) in col_groups:
                Sd.dma(lambda e, sv=sv, o=o, n=n, c0=c0: e.dma_start(
                    out=sv[:, :, o:o + n], in_=w_in_d[l][:, c0:c0 + n].rearrange("(k p) n -> p k n", p=128)),
                    sg_b.name, writes=[sg_b])
                o += n
            Sd.op("dve", lambda e, sv=sv, wv=wv, ntot=ntot: e.tensor_tensor(out=wv, in0=sv, in1=bc_last(gin, ntot), op=ALU.mult),
                  reads=[sg_b, gin_b], writes=[w_b])
            return wv, w_b

        ostg = [A.alloc("ostg", 256, BF16) for _ in range(4)]
        ostg_i = [0]
        ftmp = [A.alloc("ftmp", 256, F32) for _ in range(4)]
        fsm = [A.alloc("fsm", 16, F32) for _ in range(4)]
        fin_i = [0]

        def proj_fm(wv, w_b, c0, dstT, dst_b, tb, pi, eng, scale=None, rows=None):
            pt, pb_ = pbank[pi]

            def mm(e):
                ins = None
                for kc in range(8):
                    ins = e.matmul(pt[:, 0:512], lhsT=wv[:, kc, c0:c0 + 128], rhs=hT3[:, kc, tb * 512:(tb + 1) * 512],
                                   start=(kc == 0), stop=(kc == 7))
                return ins
            Sd.op("pe", mm, reads=[w_b, hT_b], writes=[pb_])
            return pt, pb_

        def evac_copy(eng, out, in_, reads, writes, scale=None):
            if eng == "act":
                if scale is None:
                    Sd.op("act", lambda e: e.activation(out=out, in_=in_, func=AF.Copy), reads=reads, writes=writes)
                else:
                    Sd.op("act", lambda e: e.activation(out=out, in_=in_, func=AF.Copy, scale=scale), reads=reads, writes=writes)
            else:
                if scale is None:
                    Sd.op("dve", lambda e: e.tensor_copy(out=out, in_=in_), reads=reads, writes=writes)
                else:
                    Sd.op("dve", lambda e: e.tensor_scalar(out=out, in0=in_, scalar1=scale, scalar2=None, op0=ALU.mult),
                          reads=reads, writes=writes)

        fin_pending = []

        def fin_submit(stages, k=None):
            for stl in list(fin_pending):
                if stl[0] == k:
                    for f in stl[1:]:
                        f()
                    fin_pending.remove(stl)
            fin_pending.append([k] + list(stages))

        def tick():
            for stl in list(fin_pending):
                f = stl.pop(1)
                f()
                if len(stl) == 1:
                    fin_pending.remove(stl)

        def fin_flush():
            while fin_pending:
                tick()

        def fin_slot():
            k = fin_i[0] % 4
            fin_i[0] += 1
            return k

        def norm_stages(k, o32, o32_b, nh, hd, gtile, g_b, tile_i, col0):
            sm, sm_b = fsm[k]
            sq, sq_b = fsqs[k]
            og, og_b = ostg[k]
            w = nh * hd
            o3 = o32[:, 0:w].rearrange("p (h d) -> p h d", h=nh)

            def s_sq():
                if nh == 1:
                    Sd.op("act", lambda e: e.activation(out=sq[:, 0:w], in_=o32[:, 0:w], func=AF.Square, accum_out=sm[:, 0:1]),
                          reads=[o32_b], writes=[sq_b, sm_b])
                else:
                    Sd.op("pool", lambda e: e.tensor_tensor(out=sq[:, 0:w], in0=o32[:, 0:w], in1=o32[:, 0:w], op=ALU.mult),
                          reads=[o32_b], writes=[sq_b])

            def s_red():
                if nh > 1:
                    Sd.op("dve", lambda e: e.tensor_reduce(out=sm[:, 0:nh], in_=sq[:, 0:w].rearrange("p (h d) -> p h d", h=nh), axis=AX.X, op=ALU.add),
                          reads=[sq_b], writes=[sm_b])

            def s_ln():
                Sd.op("act", lambda e: e.activation(out=sm[:, 4:4 + nh], in_=sm[:, 0:nh], func=AF.Ln, scale=1.0 / hd, bias=eps_t[:, 0:1]),
                      reads=[sm_b, eps_b], writes=[sm_b])

            def s_exp():
                Sd.op("act", lambda e: e.activation(out=sm[:, 4:4 + nh], in_=sm[:, 4:4 + nh], func=AF.Exp, scale=-0.5), reads=[sm_b], writes=[sm_b])

            def s_scale():
                if nh == 1:
                    Sd.op("dve", lambda e: e.scalar_tensor_tensor(out=og[:, 0:w], in0=o32[:, 0:w], scalar=sm[:, 4:5], in1=gtile[:, 0:w],
                                                                 op0=ALU.mult, op1=ALU.mult), reads=[o32_b, sm_b, g_b], writes=[og_b])
                else:
                    Sd.op("dve", lambda e: e.tensor_tensor(out=o3, in0=o3, in1=bc_last(sm[:, 4:4 + nh], hd), op=ALU.mult),
                          reads=[o32_b, sm_b], writes=[o32_b])
                    Sd.op("dve", lambda e: e.tensor_tensor(out=og[:, 0:w], in0=o32[:, 0:w], in1=gtile[:, 0:w], op=ALU.mult),
                          reads=[o32_b, g_b], writes=[og_b])

            def s_store():
                Sd.dma(lambda e: e.dma_start(out=ocat_d[tile_i * 128:(tile_i + 1) * 128, col0:col0 + w], in_=og[:, 0:w]),
                       og_b.name, reads=[og_b])
            return [s_sq, s_red, s_ln, s_exp, s_scale, s_store]

        fsqs = [A.alloc("fsq", 256, F32) for _ in range(4)]
        p2_mark = A.mark()

        lamt, lamt_b = A.alloc("lamt", 4 * 64, F32)
        lams, lams_b = A.alloc("lams", 8, F32)
        gA, gA_b = A.alloc("gA", 128, F32)
        for j in range(4):
            Sd.dma(lambda e, j=j: e.dma_start(out=lamt[:, j * 64:(j + 1) * 64], in_=lam_d[j][l].partition_broadcast(128)),
                   lamt_b.name, writes=[lamt_b])
        Sd.dma(lambda e: e.dma_start(out=gA, in_=subln_d[l].partition_broadcast(128)), gA_b.name, writes=[gA_b])
        Sd.op("dve", lambda e: e.tensor_tensor(out=lamt[:, 0:64], in0=lamt[:, 0:64], in1=lamt[:, 64:128], op=ALU.mult),
              reads=[lamt_b], writes=[lamt_b])
        Sd.op("dve", lambda e: e.tensor_tensor(out=lamt[:, 128:192], in0=lamt[:, 128:192], in1=lamt[:, 192:256], op=ALU.mult),
              reads=[lamt_b], writes=[lamt_b])
        Sd.op("dve", lambda e: e.tensor_reduce(out=lams[:, 0:1], in_=lamt[:, 0:64], axis=AX.X, op=ALU.add), reads=[lamt_b], writes=[lams_b])
        Sd.op("dve", lambda e: e.tensor_reduce(out=lams[:, 1:2], in_=lamt[:, 128:192], axis=AX.X, op=ALU.add), reads=[lamt_b], writes=[lams_b])
        Sd.op("act", lambda e: e.activation(out=lams[:, 2:4], in_=lams[:, 0:2], func=AF.Exp), reads=[lams_b], writes=[lams_b])
        Sd.op("dve", lambda e: e.tensor_tensor(out=lams[:, 4:5], in0=lams[:, 3:4], in1=lams[:, 2:3], op=ALU.subtract), reads=[lams_b], writes=[lams_b])
        Sd.op("dve", lambda e: e.tensor_scalar(out=lams[:, 4:5], in0=lams[:, 4:5], scalar1=-lam_init, scalar2=None, op0=ALU.add), reads=[lams_b], writes=[lams_b])
        Sd.op("dve", lambda e: e.tensor_scalar(out=gA, in0=gA, scalar1=(1.0 - lam_init), scalar2=None, op0=ALU.mult), reads=[gA_b], writes=[gA_b])

        qT, qT_b = A.alloc("qT", S, BF16)
        kT0, kT0_b = A.alloc("kT0", S, BF16)
        kT1, kT1_b = A.alloc("kT1", S, BF16)
        vA, vA_b = A.alloc("vA", NT * 129, BF16)
        vA3 = vA.rearrange("p (t e) -> p t e", t=NT)
        wP, wP_b = A.alloc("wP", 8 * 256, BF16)
        wP3 = wP.rearrange("p (k n) -> p k n", k=8)
        ropeC = [A.alloc("ropeC", 512, F32) for _ in range(2)]
        ropeS = [A.alloc("ropeS", 512, F32) for _ in range(2)]
        rt1 = [A.alloc("rt1", 512, F32) for _ in range(2)]
        rt2 = [A.alloc("rt2", 512, F32) for _ in range(2)]
        PT = [A.alloc("PT", 512, BF16) for _ in range(4)]
        Sd.op("pool", lambda e: e.memset(kT0[64:128, :], 0.0), writes=[kT0_b])
        Sd.op("pool", lambda e: e.memset(kT1[0:64, :], 0.0), writes=[kT1_b])
        Sd.op("pool", lambda e: e.memset(wP, 0.0), writes=[wP_b])
        Sd.op("pool", lambda e: e.memset(vA3[:, :, 128:129], 1.0), writes=[vA_b])

        for h in range(4):
            wv, w_b = load_w([(h * 128, 128), (512 + h * 128, 128), (1024 + h * 128, 128)])
            w4 = wv[:, :, 0:256].rearrange("p k (g d) -> p k g d", g=4)
            wP4 = wP3.rearrange("p k (g d) -> p k g d", g=4)
            for kc in range(8):
                Sd.op("pool", lambda e, kc=kc: e.tensor_copy(out=wP4[:, kc, :, 0:8], in_=w4[:, kc, :, 8:16]), reads=[w_b], writes=[wP_b])
                Sd.op("pool", lambda e, kc=kc: e.tensor_copy(out=wP4[:, kc, :, 8:16], in_=w4[:, kc, :, 0:8]), reads=[w_b], writes=[wP_b])
            for tb in range(NB):
                rc, rc_b = ropeC[tb % 2]
                rs, rs_b = ropeS[tb % 2]
                Sd.dma(lambda e, rc=rc, tb=tb: e.dma_start(out=rc, in_=ropeC_d[:, tb * 512:(tb + 1) * 512]), rc_b.name, writes=[rc_b])
                Sd.dma(lambda e, rs=rs, tb=tb: e.dma_start(out=rs, in_=ropeS_d[:, tb * 512:(tb + 1) * 512]), rs_b.name, writes=[rs_b])
                for qk in range(2):
                    p1, p1_b = proj_fm(wv, w_b, qk * 128, None, None, tb, 6, None)
                    p2, p2_b = proj_fm(wP3, wP_b, qk * 128, None, None, tb, 7, None)
                    t1, t1_b = rt1[qk]
                    t2, t2_b = rt2[qk]
                    Sd.op("dve", lambda e, t1=t1, p1=p1, rc=rc: e.tensor_tensor(out=t1, in0=p1[:, 0:512], in1=rc, op=ALU.mult),
                          reads=[p1_b, rc_b], writes=[t1_b])
                    Sd.op("dve", lambda e, t2=t2, p2=p2, rs=rs: e.tensor_tensor(out=t2, in0=p2[:, 0:512], in1=rs, op=ALU.mult),
                          reads=[p2_b, rs_b], writes=[t2_b])
                    sl = slice(tb * 512, (tb + 1) * 512)
                    if qk == 0:
                        Sd.op("dve", lambda e, t1=t1, t2=t2, sl=sl: e.tensor_tensor(out=qT[:, sl], in0=t1, in1=t2, op=ALU.add),
                              reads=[t1_b, t2_b], writes=[qT_b])
                    else:
                        Sd.op("dve", lambda e, t1=t1, t2=t2, sl=sl: e.tensor_tensor(out=kT0[0:64, sl], in0=t1[0:64, :], in1=t2[0:64, :], op=ALU.add),
                              reads=[t1_b, t2_b], writes=[kT0_b])
                        Sd.op("dve", lambda e, t1=t1, t2=t2, sl=sl: e.tensor_tensor(out=kT1[64:128, sl], in0=t1[64:128, :], in1=t2[64:128, :], op=ALU.add),
                              reads=[t1_b, t2_b], writes=[kT1_b])
                pt, pb_ = pbank[6 + (tb % 2)]

                def mmv(e, tb=tb, pt=pt, wv=wv):
                    ins = None
                    for j in range(4):
                        ti = tb * 4 + j
                        for kc in range(8):
                            ins = e.matmul(pt[:, j * 128:(j + 1) * 128], lhsT=hT3[:, kc, ti * 128:(ti + 1) * 128], rhs=wv[:, kc, 256:384],
                                           start=(kc == 0), stop=(kc == 7))
                    return ins
                Sd.op("pe", mmv, reads=[w_b, hT_b], writes=[pb_])
                Sd.op("act", lambda e, tb=tb, pt=pt: e.activation(out=vA3[:, tb * 4:(tb + 1) * 4, 0:128],
                                                                 in_=pt[:, 0:512].rearrange("p (j e) -> p j e", j=4), func=AF.Copy),
                      reads=[pb_], writes=[vA_b])

            if l == 0 and h in (0, 1, 2):
                dump(f"qT{h}", qT, qT_b)
                dump(f"kT0_{h}", kT0, kT0_b)
                dump(f"kT1_{h}", kT1, kT1_b)
                dump(f"vA{h}", vA, vA_b)
                dump(f"wP{h}", wP, wP_b)
            steps = []
            for Q in range(NQ):
                for kt in range(2 * Q + 2):
                    steps.append((Q, kt))

            def emit_S(si):
                Q, kt = steps[si]
                pt, pb_ = pbank[(0, 1, 6)[si % 3]]

                def mm(e, Q=Q, kt=kt, pt=pt):
                    e.matmul(pt[:, 0:256], lhsT=kT0[:, kt * 128:(kt + 1) * 128], rhs=qT[:, Q * 256:(Q + 1) * 256], start=True, stop=True)
                    return e.matmul(pt[:, 256:512], lhsT=kT1[:, kt * 128:(kt + 1) * 128], rhs=qT[:, Q * 256:(Q + 1) * 256], start=True, stop=True)
                Sd.op("pe", mm, reads=[kT0_b, kT1_b, qT_b], writes=[pb_])

            def emit_rest(si):
                Q, kt = steps[si]
                r = kt - 2 * Q
                pt, pb_ = pbank[(0, 1, 6)[si % 3]]
                P, P_b = PT[si % 4]
                Sd.op("act", lambda e: e.activation(out=P, in_=pt[:, 0:512], func=AF.Exp, scale=0.125), reads=[pb_], writes=[P_b])
                P3 = P.rearrange("p (n q) -> p n q", n=2)
                if r >= 0:
                    Sd.op("dve", lambda e: e.tensor_tensor(out=P3[:, :, r * 128:(r + 1) * 128], in0=P3[:, :, r * 128:(r + 1) * 128],
                                                           in1=bc_mid(mcc, 2), op=ALU.mult), reads=[P_b, mcc_b], writes=[P_b])
                first = (kt == 0)
                ab = 2 + 2 * (Q % 2)

                def pv(e):
                    ins = None
                    for n in range(2):
                        acc = pbank[ab + n][0]
                        for qs in range(max(r, 0), 2):
                            last = (kt == 2 * Q + qs)
                            ins = e.matmul(acc[:, qs * 129:(qs + 1) * 129], lhsT=P3[:, n, qs * 128:(qs + 1) * 128], rhs=vA3[:, kt, :],
                                           start=(first and qs == 0), stop=last, skip_group_check=True)
                    return ins
                Sd.op("pe", pv, reads=[P_b, vA_b], writes=[pbank[ab][1], pbank[ab + 1][1]])
                for qs in (range(2) if kt == 2 * Q + 1 else ()):
                    ti = Q * 2 + qs
                    k = fin_slot()
                    sm, sm_b = fsm[k]
                    o32, o32_b = ftmp[k]
                    a0, a0_b = pbank[ab]
                    a1, a1_b = pbank[ab + 1]
                    c = qs * 129

                    def s_comb(sm=sm, sm_b=sm_b, o32=o32, o32_b=o32_b, c=c):
                        Sd.op("dve", lambda e: e.reciprocal(out=sm[:, 8:9], in_=a0[:, c + 128:c + 129]), reads=[a0_b], writes=[sm_b])
                        Sd.op("dve", lambda e: e.reciprocal(out=sm[:, 9:10], in_=a1[:, c + 128:c + 129]), reads=[a1_b], writes=[sm_b])
                        Sd.op("dve", lambda e: e.tensor_tensor(out=sm[:, 9:10], in0=sm[:, 9:10], in1=lams[:, 4:5], op=ALU.mult),
                              reads=[sm_b, lams_b], writes=[sm_b])
                        Sd.op("dve", lambda e: e.tensor_scalar(out=o32[:, 0:128], in0=a0[:, c:c + 128], scalar1=sm[:, 8:9], scalar2=None, op0=ALU.mult),
                              reads=[a0_b, sm_b], writes=[o32_b])
                        Sd.op("dve", lambda e: e.scalar_tensor_tensor(out=o32[:, 0:128], in0=a1[:, c:c + 128], scalar=sm[:, 9:10], in1=o32[:, 0:128],
                                                                     op0=ALU.mult, op1=ALU.add), reads=[a1_b, sm_b, o32_b], writes=[o32_b])
                    fin_submit([s_comb] + norm_stages(k, o32, o32_b, 1, 128, gA, gA_b, ti, h * 128), k)

            n = len(steps)
            for si in range(n + 2):
                if si < n:
                    emit_S(si)
                if si >= 2:
                    emit_rest(si - 2)
                    tick()
        fin_flush()
        Sd.barrier()
        A.reset(p2_mark)

        qB, qB_b = A.alloc("qB", 2 * S, BF16)
        qB3 = qB.rearrange("p (f s) -> p f s", f=2)
        kB = [A.alloc("kB", S, BF16) for _ in range(4)]
        vB, vB_b = A.alloc("vB", NT * 4 * 65, BF16)
        vB4 = vB.rearrange("p (t h e) -> p t h e", t=NT, h=4)
        BT, BT_b = A.alloc("BT", 5 * 512, F32)
        BT3 = BT.rearrange("p (d x) -> p d x", d=5)
        Rt, Rt_b = A.alloc("Rt", 5 * 128, F32)
        antiI, antiI_b = A.alloc("antiI", 128, F32)
        mB, mB_b = A.alloc("mB", 256, F32)
        gnb, gnb_b = A.alloc("gnb", 256, F32)
        BTb, BTb_b = A.alloc("BTb", 5 * 512, BF16)
        BTb3 = BTb.rearrange("p (d x) -> p d x", d=5)
        PB = [A.alloc("PB", 512, BF16) for _ in range(3)]
        Sd.dma(lambda e: e.dma_start(out=antiI, in_=antiI_d), antiI_b.name, writes=[antiI_b])
        Sd.dma(lambda e: e.dma_start(out=mB, in_=mB_d), mB_b.name, writes=[mB_b])
        Sd.dma(lambda e: e.dma_start(out=gnb, in_=gnb_d[l].partition_broadcast(128)), gnb_b.name, writes=[gnb_b])
        E_b = Buf("E_dram")
        c4, c4_b = A.alloc("c4", 1, F32)
        ctile, ctile_b = A.alloc("ctile", 128, F32)
        c256, c256_b = A.alloc("c256", 4, F32)
        Sd.dma(lambda e: e.dma_start(out=c4[0:4, 0:1], in_=relb_d[l, :, 256:257], allow_slow_non_contiguous=True), c4_b.name, writes=[c4_b])
        Sd.dma(lambda e: e.dma_start(out=c256.rearrange("p (h o) -> p h o", o=1),
                                     in_=AP(relb_d.tensor, relb_d[l, 0:1, 256:257].offset, [[0, 128], [257, 4], [1, 1]]),
                                     allow_slow_non_contiguous=True),
               c256_b.name, writes=[c256_b])
        Sd.op("dve", lambda e: e.tensor_copy(out=ctile[0:4, :], in_=AP(c4.tensor, c4.offset, [[c4.ap[0][0], 4], [0, 128]])),
              reads=[c4_b], writes=[ctile_b])
        Sd.dma(lambda e: e.dma_start(out=E_d[:, 0:256], in_=relb_d[l, :, 1:257]), "E_dram", writes=[E_b])
        Sd.dma(lambda e: e.dma_start(out=E_d[:, 256:384], in_=ctile[0:4, :]), "E_dram", reads=[ctile_b], writes=[E_b])
        for hh in range(4):
            k_, k_b = kB[hh]
            if hh % 2 == 0:
                Sd.op("pool", lambda e, k_=k_: e.memset(k_[64:128, :], 0.0), writes=[k_b])
            else:
                Sd.op("pool", lambda e, k_=k_: e.memset(k_[0:64, :], 0.0), writes=[k_b])
        Sd.op("pool", lambda e: e.memset(vB4[:, :, :, 64:65], 1.0), writes=[vB_b])
        for hh in range(4):
            Sd.dma(lambda e, hh=hh: e.dma_start(out=Rt[:, 0:256].rearrange("p (d j) -> p d j", d=2),
                                                in_=AP(E_d.tensor, E_d[hh:hh + 1, 0:1].offset, [[1, 128], [128, 2], [1, 128]])),
                   Rt_b.name, reads=[E_b], writes=[Rt_b])
            pt, pb_ = pbank[4 + hh % 2]
            Sd.op("pe", lambda e, pt=pt: e.matmul(pt[:, 0:256], lhsT=antiI, rhs=Rt[:, 0:256], start=True, stop=True),
                  reads=[antiI_b, Rt_b], writes=[pb_])
            Sd.op("dve", lambda e, pt=pt, hh=hh: e.tensor_copy(
                out=BT3[:, 0:2, hh * 128:(hh + 1) * 128], in_=pt[:, 0:256].rearrange("p (d j) -> p d j", d=2)),
                reads=[pb_], writes=[BT_b])
        for d in range(2, 5):
            Sd.op("dve", lambda e, d=d: e.tensor_copy(out=BT3[:, d, :].rearrange("p (h j) -> p h j", h=4), in_=bc_last(c256, 128)),
                  reads=[c256_b], writes=[BT_b])
        Sd.op("dve", lambda e: e.tensor_tensor(out=BT3[:, 0, :].rearrange("p (h j) -> p h j", h=4), in0=BT3[:, 0, :].rearrange("p (h j) -> p h j", h=4),
                                               in1=bc_mid(mB[:, 0:128], 4), op=ALU.add), reads=[BT_b, mB_b], writes=[BT_b])
        Sd.op("dve", lambda e: e.tensor_tensor(out=BT3[:, 4, :].rearrange("p (h j) -> p h j", h=4), in0=BT3[:, 4, :].rearrange("p (h j) -> p h j", h=4),
                                               in1=bc_mid(mB[:, 128:256], 4), op=ALU.add), reads=[BT_b, mB_b], writes=[BT_b])
        Sd.op("dve", lambda e: e.tensor_copy(out=BTb, in_=BT), reads=[BT_b], writes=[BTb_b])
        wq, wq_b = load_w([(1536, 256)])
        for tb in range(NB):
            for ft in range(2):
                pt, pb_ = proj_fm(wq, wq_b, ft * 128, None, None, tb, 4 + ft, None)
                evac_copy("act" if ft else "dve", qB3[:, ft, tb * 512:(tb + 1) * 512], pt[:, 0:512], [pb_], [qB_b])
        wk, wk_b = load_w([(1792, 256)])
        for tb in range(NB):
            for ft in range(2):
                pt, pb_ = proj_fm(wk, wk_b, ft * 128, None, None, tb, 4 + ft, None)
                k0, k0_b = kB[ft * 2]
                k1, k1_b = kB[ft * 2 + 1]
                evac_copy("act", k0[0:64, tb * 512:(tb + 1) * 512], pt[0:64, 0:512], [pb_], [k0_b], scale=0.125)
                evac_copy("dve", k1[64:128, tb * 512:(tb + 1) * 512], pt[64:128, 0:512], [pb_], [k1_b], scale=0.125)
        wvv, wvv_b = load_w([(2048, 256)])
        for tp in range(NT // 2):
            pt, pb_ = pbank[4 + tp % 2]

            def mmv(e, tp=tp, pt=pt):
                ins = None
                for j in range(2):
                    ti = tp * 2 + j
                    for kc in range(8):
                        ins = e.matmul(pt[:, j * 256:(j + 1) * 256], lhsT=hT3[:, kc, ti * 128:(ti + 1) * 128], rhs=wvv[:, kc, 0:256],
                                       start=(kc == 0), stop=(kc == 7))
                return ins
            Sd.op("pe", mmv, reads=[wvv_b, hT_b], writes=[pb_])
            Sd.op("act" if tp % 2 else "dve",
                  (lambda e, tp=tp, pt=pt: e.activation(out=vB4[:, tp * 2:tp * 2 + 2, :, 0:64],
                                                        in_=pt[:, 0:512].rearrange("p (j h e) -> p j h e", j=2, h=4), func=AF.Copy))
                  if tp % 2 else
                  (lambda e, tp=tp, pt=pt: e.tensor_copy(out=vB4[:, tp * 2:tp * 2 + 2, :, 0:64],
                                                         in_=pt[:, 0:512].rearrange("p (j h e) -> p j h e", j=2, h=4))),
                  reads=[pb_], writes=[vB_b])
        stepsB = []
        for i in range(NT):
            for d in range(4, -1, -1):
                if i - d >= 0:
                    stepsB.append((i, d))

        def emitB_S(si):
            i, d = stepsB[si]
            j = i - d
            pt, pb_ = pbank[(0, 1, 4)[si % 3]]

            def mm(e):
                ins = None
                for hh in range(4):
                    ins = e.matmul(pt[:, hh * 128:(hh + 1) * 128], lhsT=kB[hh][0][:, j * 128:(j + 1) * 128],
                                   rhs=qB3[:, hh // 2, i * 128:(i + 1) * 128], start=(hh == 0), stop=False, skip_group_check=True)
                return e.matmul(pt[:, 0:512], lhsT=ident, rhs=BTb3[:, d, :], start=False, stop=True, skip_group_check=True)
            Sd.op("pe", mm, reads=[kB[0][1], kB[1][1], kB[2][1], kB[3][1], qB_b, BTb_b, ident_b], writes=[pb_])

        def emitB_rest(si):
            i, d = stepsB[si]
            j = i - d
            pt, pb_ = pbank[(0, 1, 4)[si % 3]]
            P, P_b = PB[si % 3]
            Sd.op("act", lambda e: e.activation(out=P, in_=pt[:, 0:512], func=AF.Exp), reads=[pb_], writes=[P_b])
            first = (d == min(4, i))
            acc, acc_b = pbank[2 + (i % 2)]

            def pv(e):
                ins = None
                for hh in range(4):
                    ins = e.matmul(acc[:, hh * 65:(hh + 1) * 65], lhsT=P[:, hh * 128:(hh + 1) * 128], rhs=vB4[:, j, hh, :],
                                   start=(first and hh == 0), stop=(d == 0), skip_group_check=True)
                return ins
            Sd.op("pe", pv, reads=[P_b, vB_b], writes=[acc_b])
            if d == 0:
                k = fin_slot()
                sm, sm_b = fsm[k]
                o32, o32_b = ftmp[k]
                a3 = acc[:, 0:260].rearrange("p (h e) -> p h e", h=4)

                def s_comb():
                    Sd.op("dve", lambda e: e.reciprocal(out=sm[:, 8:12], in_=a3[:, :, 64]), reads=[acc_b], writes=[sm_b])
                    Sd.op("dve", lambda e: e.tensor_tensor(out=o32.rearrange("p (h e) -> p h e", h=4), in0=a3[:, :, 0:64],
                                                           in1=bc_last(sm[:, 8:12], 64), op=ALU.mult), reads=[acc_b, sm_b], writes=[o32_b])
                fin_submit([s_comb] + norm_stages(k, o32, o32_b, 4, 64, gnb, gnb_b, i, 512), k)

        nB = len(stepsB)
        for si in range(nB + 2):
            if si < nB:
                emitB_S(si)
            if si >= 2:
                emitB_rest(si - 2)
                tick()
        fin_flush()
        Sd.barrier()
        A.reset(p2_mark)

        qC, qC_b = A.alloc("qC", S, BF16)
        kC = [A.alloc("kC", S, BF16) for _ in range(2)]
        vC, vC_b = A.alloc("vC", NT * 128, BF16)
        vC4 = vC.rearrange("p (t h e) -> p t h e", t=NT, h=2)
        gnc, gnc_b = A.alloc("gnc", 256, F32)
        Sd.dma(lambda e: e.dma_start(out=gnc, in_=gnc_d[l].partition_broadcast(128)), gnc_b.name, writes=[gnc_b])
        Ebuf = [A.alloc("Ebuf", 512, F32) for _ in range(2)]
        Sp = [[A.alloc("Sp", 512, BF16) for _ in range(2)] for _ in range(2)]
        SpSum = [A.alloc("SpSum", 512, BF16) for _ in range(2)]
        AT = [[A.alloc("AT", 512, BF16) for _ in range(2)] for _ in range(2)]
        Sd.op("pool", lambda e: e.memset(kC[0][0][64:128, :], 0.0), writes=[kC[0][1]])
        Sd.op("pool", lambda e: e.memset(kC[1][0][0:64, :], 0.0), writes=[kC[1][1]])
        zbank = [[0, 1], [6, 7]]
        for hp in range(2):
            wv, w_b = load_w([(2304 + hp * 128, 128), (2560 + hp * 128, 128), (2816 + hp * 128, 128)])
            for tb in range(NB):
                pt, pb_ = proj_fm(wv, w_b, 0, None, None, tb, 4, None)
                evac_copy("dve", qC[:, tb * 512:(tb + 1) * 512], pt[:, 0:512], [pb_], [qC_b])
                pt, pb_ = proj_fm(wv, w_b, 128, None, None, tb, 5, None)
                evac_copy("act", kC[0][0][0:64, tb * 512:(tb + 1) * 512], pt[0:64, 0:512], [pb_], [kC[0][1]], scale=0.125)
                evac_copy("dve", kC[1][0][64:128, tb * 512:(tb + 1) * 512], pt[64:128, 0:512], [pb_], [kC[1][1]], scale=0.125)
                pt, pb_ = pbank[5]

                def mmv(e, tb=tb, pt=pt, wv=wv):
                    ins = None
                    for j in range(4):
                        ti = tb * 4 + j
                        for kc in range(8):
                            ins = e.matmul(pt[:, j * 128:(j + 1) * 128], lhsT=hT3[:, kc, ti * 128:(ti + 1) * 128], rhs=wv[:, kc, 256:384],
                                           start=(kc == 0), stop=(kc == 7))
                    return ins
                Sd.op("pe", mmv, reads=[w_b, hT_b], writes=[pb_])
                Sd.op("act", lambda e, tb=tb, pt=pt: e.activation(out=vC4[:, tb * 4:(tb + 1) * 4, :, :],
                                                                 in_=pt[:, 0:512].rearrange("p (j h e) -> p j h e", j=4, h=2), func=AF.Copy),
                      reads=[pb_], writes=[vC_b])
            stepsC = []
            for Q in range(NB):
                for kt in range(4 * Q + 3, -1, -1):
                    stepsC.append((Q, kt))

            def cols(Q, kt):
                r = kt - 4 * Q
                return r, max(r, 0) * 128

            def emitC_QK(si):
                Q, kt = stepsC[si]
                r, c0 = cols(Q, kt)
                for s in range(2):
                    z, z_b = pbank[zbank[si % 2][s]]
                    Sd.op("pe", lambda e, z=z, s=s, kt=kt, Q=Q, c0=c0: e.matmul(
                        z[:, c0:512], lhsT=kC[s][0][:, kt * 128:(kt + 1) * 128], rhs=qC[:, Q * 512 + c0:(Q + 1) * 512],
                        start=True, stop=False, skip_group_check=True), reads=[kC[s][1], qC_b], writes=[z_b])

            def emitC_rest(si):
                Q, kt = stepsC[si]
                r, c0 = cols(Q, kt)
                firstQ = (kt == 4 * Q + 3)
                for s in range(2):
                    if firstQ:
                        Sd.op("pool", lambda e, s=s: e.memset(SpSum[s][0], 0.0), writes=[SpSum[s][1]])
                for s in range(2):
                    z, z_b = pbank[zbank[si % 2][s]]
                    E_, E_b2 = pbank[4 + s] if si % 2 == 0 else Ebuf[s]
                    sp_, sp_b = Sp[s][si % 2]
                    Sd.op("act", lambda e, z=z, E_=E_, c0=c0: e.activation(out=E_[:, c0:512], in_=z[:, c0:512], func=AF.Exp),
                          reads=[z_b], writes=[E_b2])
                    Sd.op("act", lambda e, E_=E_, sp_=sp_, c0=c0: e.activation(out=sp_[:, c0:512], in_=E_[:, c0:512], func=AF.Ln, bias=one_t[:, 0:1]),
                          reads=[E_b2, one_b], writes=[sp_b])
                    if r >= 0:
                        Sd.op("dve", lambda e, sp_=sp_, c0=c0: e.tensor_tensor(out=sp_[:, c0:c0 + 128], in0=sp_[:, c0:c0 + 128], in1=mtri, op=ALU.mult),
                              reads=[sp_b, mtri_b], writes=[sp_b])

                    def cum(e, z=z, sp_=sp_, s=s, c0=c0):
                        ins = e.matmul(z[:, c0:512], lhsT=negU, rhs=sp_[:, c0:512], start=False, stop=firstQ, skip_group_check=True)
                        if not firstQ:
                            ins = e.matmul(z[:, c0:512], lhsT=negOnes, rhs=SpSum[s][0][:, c0:512], start=False, stop=True, skip_group_check=True)
                        return ins
                    Sd.op("pe", cum, reads=[sp_b, negU_b, negOnes_b, SpSum[s][1]], writes=[z_b])
                for s in range(2):
                    z, z_b = pbank[zbank[si % 2][s]]
                    sp_, sp_b = Sp[s][si % 2]
                    a_, a_b = AT[s][si % 2]
                    Sd.op("act", lambda e, z=z, a_=a_, c0=c0: e.activation(out=a_[:, c0:512], in_=z[:, c0:512], func=AF.Exp),
                          reads=[z_b], writes=[a_b])
                    if r >= 0:
                        Sd.op("dve", lambda e, a_=a_, c0=c0: e.tensor_tensor(out=a_[:, c0:c0 + 128], in0=a_[:, c0:c0 + 128], in1=mtri, op=ALU.mult),
                              reads=[a_b, mtri_b], writes=[a_b])
                    if kt > 0:
                        Sd.op("dve", lambda e, s=s, sp_=sp_, c0=c0: e.tensor_tensor(out=SpSum[s][0][:, c0:512], in0=SpSum[s][0][:, c0:512],
                                                                                      in1=sp_[:, c0:512], op=ALU.add),
                              reads=[SpSum[s][1], sp_b], writes=[SpSum[s][1]])
                    acc, acc_b = pbank[2 + (Q % 2)]

                    def pv(e, a_=a_, s=s, acc=acc):
                        ins = None
                        for qs in range(max(r, 0), 4):
                            firstq = (kt == 4 * Q + 3) and s == 0 and qs == 3
                            ins = e.matmul(acc[:, s * 256 + qs * 64:s * 256 + (qs + 1) * 64], lhsT=a_[:, qs * 128:(qs + 1) * 128],
                                           rhs=vC4[:, kt, s, :], start=firstq, stop=(kt == 0), skip_group_check=True)
                        return ins
                    Sd.op("pe", pv, reads=[a_b, vC_b], writes=[acc_b])
                if kt == 0:
                    acc, acc_b = pbank[2 + (Q % 2)]
                    a4 = acc[:, 0:512].rearrange("p (s q e) -> p s q e", s=2, q=4)
                    for qs in range(4):
                        ti = Q * 4 + qs
                        k = fin_slot()
                        o32, o32_b = ftmp[k]

                        def s_comb(o32=o32, o32_b=o32_b, qs=qs, acc_b=acc_b, a4=a4):
                            Sd.op("dve", lambda e: e.tensor_copy(out=o32[:, 0:128].rearrange("p (s e) -> p s e", s=2), in_=a4[:, :, qs, :]),
                                  reads=[acc_b], writes=[o32_b])
                        fin_submit([s_comb] + norm_stages(k, o32, o32_b, 2, 64, gnc[:, hp * 128:(hp + 1) * 128], gnc_b, ti, 768 + hp * 128), k)

            nC = len(stepsC)
            for si in range(nC + 1):
                if si < nC:
                    emitC_QK(si)
                if si >= 1:
                    emitC_rest(si - 1)
                    tick()
        fin_flush()
        Sd.barrier()
        if dbg and l == 0:
            Sd.dma(lambda e: e.dma_start(out=dbg_d["ocat"], in_=ocat_d), "dbg1")
            Sd.barrier()

        A.reset(base_mark)
        wd, wd_b = A.alloc("wd", 32 * D, BF16)
        wd3 = wd.rearrange("p (k n) -> p k n", k=32)
        gpl, gpl_b = A.alloc("gpl", D, F32)
        gin2, gin2_b = A.alloc("gin2", 8, F32)
        Sd.dma(lambda e: e.dma_start(out=gpl, in_=g_post_mlp_d[l].partition_broadcast(128)), gpl_b.name, writes=[gpl_b])
        Sd.dma(lambda e: e.dma_start(out=gin2, in_=g_pre_mlp_d[l].rearrange("(k p) -> p k", p=128), allow_slow_non_contiguous=True),
               gin2_b.name, writes=[gin2_b])
        p4_mark = A.mark()
        stg4 = [A.alloc("stg4", 2048, F32) for _ in range(3)]
        ci = [0]

        def load_wd_chunk(fc2, engs=("dve", "act")):
            sg, sg_b = stg4[ci[0] % 3]
            Sd.dma(lambda e: e.dma_start(out=sg.rearrange("p (f n) -> p f n", f=2),
                                         in_=w_down_d[l][fc2 * 256:(fc2 + 1) * 256, :].rearrange("(f p) n -> p f n", p=128)),
                   sg_b.name, writes=[sg_b])
            eng = engs[ci[0] % len(engs)]
            dst = wd3[:, fc2 * 2:(fc2 + 1) * 2, :]
            src_ = sg.rearrange("p (f n) -> p f n", f=2)
            if eng == "act":
                Sd.op("act", lambda e: e.activation(out=dst, in_=src_, func=AF.Copy), reads=[sg_b], writes=[wd_b])
            else:
                Sd.op(eng, lambda e: e.tensor_copy(out=dst, in_=src_), reads=[sg_b], writes=[wd_b])
            ci[0] += 1

        wo, wo_b = A.alloc("wo", 8 * D, BF16)
        wo3 = wo.rearrange("p (k n) -> p k n", k=8)
        gpm, gpm_b = A.alloc("gpm", D, F32)
        Sd.dma(lambda e: e.dma_start(out=gpm, in_=g_post_mix_d[l].partition_broadcast(128)), gpm_b.name, writes=[gpm_b])
        for kc in range(8):
            sg, sg_b = stg4[ci[0] % 3]
            ci[0] += 1
            Sd.dma(lambda e, sg=sg, kc=kc: e.dma_start(out=sg[:, 0:1024], in_=w_out_d[l][kc * 128:(kc + 1) * 128, :]), sg_b.name, writes=[sg_b])
            evac_copy("dve" if kc % 2 else "act", wo3[:, kc, :], sg[:, 0:1024], [sg_b], [wo_b])
        NS = 5
        oc = [A.alloc("oc", D, BF16) for _ in range(NS)]
        x3 = [A.alloc("x3", D, F32) for _ in range(NS)]
        oT = [A.alloc("oT", D, BF16) for _ in range(2)]
        xn = [A.alloc("xn", D, F32) for _ in range(3)]
        h2 = [A.alloc("h2", D, BF16) for _ in range(2)]
        h2o = [A.alloc("h2o", D, BF16) for _ in range(2)]
        s3 = [A.alloc("s3", 8, F32) for _ in range(4)]
        junk3s = [A.alloc("junk3", D, BF16) for _ in range(4)]
        jc3 = [0]

        def next_junk3():
            jc3[0] += 1
            return junk3s[jc3[0] % 4]

        def p3_load(i):
            oc_, oc_b = oc[i % NS]
            x_, x_b = x3[i % NS]
            Sd.dma(lambda e: e.dma_start(out=oc_, in_=ocat_d[i * 128:(i + 1) * 128, :]), oc_b.name, writes=[oc_b])
            Sd.dma(lambda e: e.dma_start(out=x_, in_=x_src[i * 128:(i + 1) * 128, :]), x_b.name, writes=[x_b])

        def st_tr(i):
            k = i % 2
            oc_, oc_b = oc[i % NS]
            pv = pbf(k)

            def tr(e):
                ins = None
                for kc in range(8):
                    ins = e.transpose(out=pv[:, kc * 128:(kc + 1) * 128], in_=oc_[:, kc * 128:(kc + 1) * 128], identity=ident)
                return ins
            Sd.op("pe", tr, reads=[oc_b, ident_b], writes=[pbank[k][1]])

        def st_evac(i):
            k = i % 2
            oT_, oT_b = oT[k]
            pv = pbf(k)
            Sd.op("dve", lambda e: e.tensor_copy(out=oT_, in_=pv), reads=[pbank[k][1]], writes=[oT_b])

        def st_mm(i):
            k = i % 2
            oT_, oT_b = oT[k]
            oT3 = oT_.rearrange("p (k t) -> p k t", k=8)
            for hf in range(2):
                pt, pb_ = pbank[2 + k * 2 + hf]

                def mm(e, pt=pt, hf=hf):
                    ins = None
                    for kc in range(8):
                        ins = e.matmul(pt[:, 0:512], lhsT=oT3[:, kc, :], rhs=wo3[:, kc, hf * 512:(hf + 1) * 512], start=(kc == 0), stop=(kc == 7))
                    return ins
                Sd.op("pe", mm, reads=[oT_b, wo_b], writes=[pb_])

        def st_sqy(i):
            k = i % 2
            s_, s_b = s3[i % 4]
            for hf in range(2):
                pt, pb_ = pbank[2 + k * 2 + hf]
                junk3, junk3_b = next_junk3()
                Sd.op("act", lambda e, pt=pt, hf=hf, junk3=junk3: e.activation(out=junk3[:, 0:512], in_=pt[:, 0:512], func=AF.Square, accum_out=s_[:, hf:hf + 1]),
                      reads=[pb_], writes=[junk3_b, s_b])

        def st_ssqadd(i):
            s_, s_b = s3[i % 4]
            Sd.op("dve", lambda e: e.tensor_tensor(out=s_[:, 2:3], in0=s_[:, 0:1], in1=s_[:, 1:2], op=ALU.add), reads=[s_b], writes=[s_b])

        def st_rstd1(i):
            s_, s_b = s3[i % 4]
            rstd_ops(s_[:, 2:3], s_[:, 3:4], D, s_b)

        def st_xn(i):
            k = i % 2
            x_, x_b = x3[i % NS]
            xn_, xn_b = xn[i % 3]
            s_, s_b = s3[i % 4]
            for hf in range(2):
                pt, pb_ = pbank[2 + k * 2 + hf]
                Sd.op("dve", lambda e, pt=pt, hf=hf: e.scalar_tensor_tensor(
                    out=xn_[:, hf * 512:(hf + 1) * 512], in0=pt[:, 0:512], scalar=s_[:, 3:4], in1=gpm[:, hf * 512:(hf + 1) * 512],
                    op0=ALU.mult, op1=ALU.mult), reads=[pb_, s_b, gpm_b], writes=[xn_b])
            Sd.op("dve", lambda e: e.tensor_tensor(out=xn_, in0=xn_, in1=x_, op=ALU.add), reads=[xn_b, x_b], writes=[xn_b])
            Sd.dma(lambda e: e.dma_start(out=xa_d[i * 128:(i + 1) * 128, :], in_=xn_), xn_b.name + "s", reads=[xn_b])

        def st_sqx(i):
            xn_, xn_b = xn[i % 3]
            s_, s_b = s3[i % 4]
            junk3, junk3_b = next_junk3()
            Sd.op("act", lambda e: e.activation(out=junk3, in_=xn_, func=AF.Square, accum_out=s_[:, 4:5]),
                  reads=[xn_b], writes=[junk3_b, s_b])
            rstd_ops(s_[:, 4:5], s_[:, 5:6], D, s_b)

        def st_h2(i):
            xn_, xn_b = xn[i % 3]
            h2_, h2_b = h2[i % 2]
            s_, s_b = s3[i % 4]
            Sd.op("dve", lambda e: e.tensor_scalar(out=h2_, in0=xn_, scalar1=s_[:, 5:6], scalar2=None, op0=ALU.mult),
                  reads=[xn_b, s_b], writes=[h2_b])

        def st_tr2(i):
            k = i % 2
            h2_, h2_b = h2[k]
            pv2 = pbf(6 + k)

            def tr2(e):
                ins = None
                for kc in range(8):
                    ins = e.transpose(out=pv2[:, kc * 128:(kc + 1) * 128], in_=h2_[:, kc * 128:(kc + 1) * 128], identity=ident)
                return ins
            Sd.op("pe", tr2, reads=[h2_b, ident_b], writes=[pbank[6 + k][1]])

        def st_h2o(i):
            k = i % 2
            h2o_, h2o_b = h2o[k]
            pv2 = pbf(6 + k)
            Sd.op("act", lambda e: e.activation(out=h2o_, in_=pv2, func=AF.Copy), reads=[pbank[6 + k][1]], writes=[h2o_b])
            Sd.dma(lambda e: e.dma_start(out=h2T_d[i], in_=h2o_), h2o_b.name + "s", reads=[h2o_b])

        def ok(t):
            return 0 <= t < NT

        for i in range(min(3, NT)):
            p3_load(i)
        wd_next = 0
        for j in range(NT + 4):
            if ok(j):
                st_tr(j)
            if ok(j - 2):
                st_ssqadd(j - 2)
                st_rstd1(j - 2)
            if ok(j - 4):
                st_tr2(j - 4)
            if ok(j):
                st_evac(j)
            if ok(j - 3):
                st_sqx(j - 3)
            if ok(j - 1):
                st_mm(j - 1)
            if ok(j - 2):
                st_xn(j - 2)
            if ok(j - 4):
                st_h2o(j - 4)
            if ok(j - 3):
                st_h2(j - 3)
            if ok(j - 1):
                st_sqy(j - 1)
            if j + 3 < NT:
                p3_load(j + 3)
            if j % 2 == 1 and wd_next < 16:
                load_wd_chunk(wd_next)
                wd_next += 1
        while wd_next < 16:
            load_wd_chunk(wd_next, engs=("dve", "act"))
            wd_next += 1
        Sd.barrier()
        if dbg and l == 0:
            Sd.dma(lambda e: e.dma_start(out=dbg_d["xa"], in_=xa_d), "dbg2")
            Sd.barrier()

        A.reset(p4_mark)
        wu, wu_b = A.alloc("wu", 8 * DFF, BF16)
        wu3 = wu.rearrange("p (k n) -> p k n", k=8)
        m4 = A.mark()
        stg4 = [A.alloc("stg4b", 2048, F32) for _ in range(3)]
        ci = 0
        for kc in range(8):
            for c in range(2):
                sg, sg_b = stg4[ci % 3]
                Sd.dma(lambda e, sg=sg, kc=kc, c=c: e.dma_start(out=sg, in_=w_up_d[l][kc * 128:(kc + 1) * 128, c * 2048:(c + 1) * 2048]),
                       sg_b.name, writes=[sg_b])
                if ci % 2 == 0:
                    Sd.op("dve", lambda e, sg=sg, kc=kc, c=c: e.tensor_scalar(out=wu3[:, kc, c * 2048:(c + 1) * 2048], in0=sg, scalar1=gin2[:, kc:kc + 1],
                                                                             scalar2=None, op0=ALU.mult), reads=[sg_b, gin2_b], writes=[wu_b])
                else:
                    Sd.op("act", lambda e, sg=sg, kc=kc, c=c: e.activation(out=wu3[:, kc, c * 2048:(c + 1) * 2048], in_=sg, func=AF.Copy,
                                                                          scale=gin2[:, kc:kc + 1]), reads=[sg_b, gin2_b], writes=[wu_b])
                ci += 1
        Sd.barrier()
        A.reset(m4)
        TB = 4
        NMB = NT // TB
        uT, uT_b = A.alloc("uT", 32 * 512, BF16)
        uT3 = uT.rearrange("p (f t) -> p f t", f=32)
        uR = [A.alloc("uR", 512, BF16) for _ in range(3)]
        hblk, hblk_b = A.alloc("hblk", TB * 1024, BF16)
        hb4 = hblk.rearrange("p (t k x) -> p t k x", t=TB, k=8)
        x4 = [A.alloc("x4", D, F32) for _ in range(4)]
        o4 = [A.alloc("o4", D, F32) for _ in range(2)]
        s4 = [A.alloc("s4", 8, F32) for _ in range(2)]
        junk4s = [A.alloc("junk4", 512, BF16) for _ in range(3)]
        jc4 = [0]
        uTb = [Buf(f"uTpart{j}") for j in range(32)]

        def p4_load_h(b):
            Sd.dma(lambda e: e.dma_start(out=hblk.rearrange("p (t x) -> p t x", t=TB),
                                         in_=h2T_d[TB * b:TB * (b + 1)].rearrange("t p x -> p t x")), hblk_b.name, writes=[hblk_b])

        def p4_load_x(ti):
            x_, x_b = x4[ti % 4]
            Sd.dma(lambda e: e.dma_start(out=x_, in_=xa_d[ti * 128:(ti + 1) * 128, :]), x_b.name, writes=[x_b])

        p4_load_h(0)
        for ti in range(min(4, NT)):
            p4_load_x(ti)
        for b in range(NMB):
            for fc in range(32):
                pt, pb_ = pbank[fc % 2]

                def mm(e, pt=pt, fc=fc):
                    ins = None
                    for kc in range(8):
                        ins = e.matmul(pt[:, 0:512].rearrange("p (t x) -> p t x", t=TB), lhsT=wu3[:, kc, fc * 128:(fc + 1) * 128],
                                       rhs=hb4[:, :, kc, :], start=(kc == 0), stop=(kc == 7))
                    return ins
                Sd.op("pe", mm, reads=[wu_b, hblk_b], writes=[pb_])
                ur, ur_b = uR[fc % 3]
                Sd.op("dve", lambda e, pt=pt, ur=ur: e.tensor_scalar(out=ur, in0=pt[:, 0:512], scalar1=0.0, scalar2=None, op0=ALU.max),
                      reads=[pb_], writes=[ur_b])
                Sd.op("pool", lambda e, ur=ur, fc=fc: e.tensor_tensor(out=uT3[:, fc, :], in0=ur, in1=ur, op=ALU.mult),
                      reads=[ur_b], writes=[uTb[fc]])
            if b + 1 < NMB:
                p4_load_h(b + 1)
            for t in range(TB):
                ti = b * TB + t
                k = ti % 2
                x_, x_b = x4[ti % 4]
                o_, o_b = o4[k]
                s_, s_b = s4[k]
                for hf in range(2):
                    pt, pb_ = pbank[2 + k * 2 + hf]

                    def mm(e, pt=pt, hf=hf, t=t):
                        ins = None
                        for fc in range(32):
                            ins = e.matmul(pt[:, 0:512], lhsT=uT3[:, fc, t * 128:(t + 1) * 128], rhs=wd3[:, fc, hf * 512:(hf + 1) * 512],
                                           start=(fc == 0), stop=(fc == 31))
                        return ins
                    Sd.op("pe", mm, reads=uTb + [wd_b], writes=[pb_])
                    jc4[0] += 1
                    junk4, junk4_b = junk4s[jc4[0] % 3]
                    Sd.op("act", lambda e, pt=pt, s_=s_, hf=hf, junk4=junk4: e.activation(out=junk4, in_=pt[:, 0:512], func=AF.Square, accum_out=s_[:, hf:hf + 1]),
                          reads=[pb_], writes=[junk4_b, s_b])
                Sd.op("dve", lambda e, s_=s_: e.tensor_tensor(out=s_[:, 2:3], in0=s_[:, 0:1], in1=s_[:, 1:2], op=ALU.add), reads=[s_b], writes=[s_b])
                rstd_ops(s_[:, 2:3], s_[:, 3:4], D, s_b)
                for hf in range(2):
                    pt, pb_ = pbank[2 + k * 2 + hf]
                    Sd.op("dve", lambda e, pt=pt, o_=o_, s_=s_, hf=hf: e.scalar_tensor_tensor(
                        out=o_[:, hf * 512:(hf + 1) * 512], in0=pt[:, 0:512], scalar=s_[:, 3:4], in1=gpl[:, hf * 512:(hf + 1) * 512],
                        op0=ALU.mult, op1=ALU.mult), reads=[pb_, s_b, gpl_b], writes=[o_b])
                Sd.op("pool", lambda e, o_=o_, x_=x_: e.tensor_tensor(out=o_, in0=o_, in1=x_, op=ALU.add), reads=[o_b, x_b], writes=[o_b])
                Sd.dma(lambda e, o_=o_, ti=ti: e.dma_start(out=x_dst[ti * 128:(ti + 1) * 128, :], in_=o_), o_b.name + "s", reads=[o_b])
                if ti + 4 < NT:
                    p4_load_x(ti + 4)
        Sd.barrier()

    one_t, one_b = A.alloc("one", 1, F32)
    Sd.op("pool", lambda e: e.memset(one_t, 1.0), writes=[one_b])
    base_mark = A.mark()
    base_names = dict(A.namecnt)
    src = x_d
    for l in range(L):
        dst = out_d if l == L - 1 else xb_d
        layer(l, src, dst)
        src = dst
    Sd.barrier()
    Sd.finalize(nc, es)
    es.close()
    stats = dict(n_ops={e: len(Sd.ops[e]) for e in Sd.ENGS}, n_sems=len(Sd.sems), n_waits=Sd.n_waits, arena_peak=A.peak)
    return nc, stats


def host_consts(S):
    bf = ml_dtypes.bfloat16
    c = {}
    c["c_ident"] = np.eye(128, dtype=np.float32).astype(bf)
    pos = np.arange(S, dtype=np.float32)
    inv_freq = (np.float32(500000.0) ** (-np.arange(0, 16, 2, dtype=np.float32) / np.float32(16))).astype(np.float32)
    ang = (pos[:, None] * inv_freq[None, :]).astype(np.float32)
    cs, sn = np.cos(ang).astype(np.float32), np.sin(ang).astype(np.float32)
    C = np.ones((128, S), np.float32)
    Sg = np.zeros((128, S), np.float32)
    for n in range(2):
        for i in range(8):
            C[n * 64 + i] = cs[:, i]
            C[n * 64 + 8 + i] = cs[:, i]
            Sg[n * 64 + i] = -sn[:, i]
            Sg[n * 64 + 8 + i] = sn[:, i]
    c["c_ropeC"] = C
    c["c_ropeS"] = Sg
    p = np.arange(128)[:, None]
    j = np.arange(128)[None, :]
    c["c_mask_cc"] = np.where((p >= 64) & (j < 64), 0.0, 1.0).astype(np.float32).astype(bf)
    c["c_mask_tri"] = (p < j).astype(np.float32).astype(bf)
    c["c_negU"] = np.where(p >= j, -1.0, 0.0).astype(np.float32).astype(bf)
    c["c_negOnes"] = np.full((128, 128), -1.0, np.float32).astype(bf)
    m0 = np.where((p >= 64) & (j < 64), -30000.0, 0.0).astype(np.float32)
    m4 = np.where((p < 64) & (j >= 64), -30000.0, 0.0).astype(np.float32)
    c["c_maskB"] = np.concatenate([m0, m4], axis=1)
    c["c_antiI"] = np.eye(128, dtype=np.float32)[::-1].copy()
    return c


_CACHE = {}

PARAM_NAMES = ["w_in", "w_out", "w_up", "w_down", "norm_pre_mix", "norm_post_mix", "norm_pre_mlp", "norm_post_mlp",
               "lam_q1", "lam_k1", "lam_q2", "lam_k2", "subln_a", "rel_bias", "gn_b", "gn_c"]


def kernel(**inputs):
    x = np.ascontiguousarray(np.asarray(inputs["x"], dtype=np.float32))
    B, S, _ = x.shape
    L = int(np.asarray(inputs["w_in"]).shape[0])
    key = (S, L)
    if key not in _CACHE:
        _CACHE[key] = build(S, L)[0]
    nc = _CACHE[key]
    consts = host_consts(S)
    shared = {n: np.ascontiguousarray(np.asarray(inputs[n], dtype=np.float32)) for n in PARAM_NAMES}
    shared.update(consts)
    n_cores = 8
    active = [0, 1, 4, 5][:B] if B <= 4 else list(range(B))
    zeros = {n: np.zeros_like(v) for n, v in shared.items() if n.startswith("w_")}
    zx = np.zeros_like(x[0])
    in_maps = []
    for c in range(n_cores):
        if c in active:
            m = dict(shared)
            m["x"] = x[active.index(c)]
        else:
            m = dict(shared)
            m.update(zeros)
            m["x"] = zx
        in_maps.append(m)
    res = run_bass_kernel_spmd(nc, in_maps, core_ids=list(range(n_cores)))
    out = np.stack([np.asarray(res.results[active[b]]["out"], dtype=np.float32) for b in range(B)], axis=0)
    return out
```

```python
import math
import types
import numpy as np
import ml_dtypes
import concourse.bass as bass
import concourse.mybir as mybir
from concourse.ap import AP
from concourse.bass_utils import run_bass_kernel_spmd
from contextlib import ExitStack

F32 = mybir.dt.float32
BF16 = mybir.dt.bfloat16
U8 = mybir.dt.uint8
ALU = mybir.AluOpType
AF = mybir.ActivationFunctionType
AX = mybir.AxisListType

D = 1024
DFF = 4096
DIN = 3072
EPS = 1e-6
SAME_ENG_RAW_SYNC = True


class Buf:
    __slots__ = ("name", "w", "r")

    def __init__(self, name):
        self.name = name
        self.w = {}
        self.r = {}


class Op:
    __slots__ = ("eng", "fn", "deps", "signal", "token", "is_dma", "dkey", "extra_waits")

    def __init__(self, eng, fn, is_dma=False, dkey=None):
        self.eng = eng
        self.fn = fn
        self.deps = set()
        self.signal = False
        self.token = None
        self.is_dma = is_dma
        self.dkey = dkey
        self.extra_waits = []


def _freeze(fn):
    if fn.__closure__ is None:
        return fn
    cells = []
    for c in fn.__closure__:
        try:
            cells.append(types.CellType(c.cell_contents))
        except ValueError:
            cells.append(c)
    return types.FunctionType(fn.__code__, fn.__globals__, fn.__name__, fn.__defaults__, tuple(cells))


class Sched:
    ENGS = ("sp", "act", "dve", "pool", "pe")

    def __init__(self):
        self.ops = {e: [] for e in self.ENGS}
        self.all = []
        self.last_op = {e: None for e in self.ENGS}
        self.dma_keys = {}
        self.dma_last = {}
        self.pending_barrier = {e: [] for e in self.ENGS}

    def _deps(self, op, reads, writes):
        for b in reads:
            for e, w in b.w.items():
                if w is op:
                    continue
                if w.is_dma or e != op.eng or op.is_dma or (SAME_ENG_RAW_SYNC and e != "pe"):
                    op.deps.add(w)
        for b in writes:
            for e, w in b.w.items():
                if w is op:
                    continue
                if w.is_dma and op.is_dma and w.dkey == op.dkey and w.eng == op.eng:
                    op.deps |= w.deps
                    continue
                if w.is_dma or e != op.eng or op.is_dma or e != "pe":
                    op.deps.add(w)
            for e, r in b.r.items():
                if r is op:
                    continue
                if r.is_dma or e != op.eng or op.is_dma or e != "pe":
                    op.deps.add(r)
        for b in reads:
            b.r[op.eng if not op.is_dma else ("dma", id(op))] = op
        for b in writes:
            if op.is_dma:
                prev = {k: v for k, v in b.w.items() if v.is_dma and v.dkey == op.dkey}
                b.w = prev
                b.w[("dma", id(op))] = op
            else:
                b.w = {op.eng: op}
            b.r = {}

    def op(self, eng, fn, reads=(), writes=()):
        o = Op(eng, _freeze(fn))
        self._deps(o, reads, writes)
        o.deps |= set(self.pending_barrier[eng])
        self.pending_barrier[eng] = []
        self.ops[eng].append(o)
        self.all.append(o)
        self.last_op[eng] = o
        return o

    def dma(self, fn, key, reads=(), writes=(), eng="sp"):
        o = Op(eng, _freeze(fn), is_dma=True, dkey=key)
        self._deps(o, reads, writes)
        o.deps |= set(self.pending_barrier[eng])
        self.pending_barrier[eng] = []
        self.dma_keys[key] = self.dma_keys.get(key, 0) + 1
        o.token = ("dma:" + key, 16 * self.dma_keys[key])
        o.signal = True
        self.dma_last[key] = o
        self.ops[eng].append(o)
        self.all.append(o)
        return o

    def barrier(self):
        toks = [o for o in self.last_op.values() if o is not None]
        toks += list(self.dma_last.values())
        for e in self.ENGS:
            self.pending_barrier[e] = list(toks)

    def finalize(self, nc, es):
        for o in self.all:
            for d in o.deps:
                d.signal = True
        for e in self.ENGS:
            for d in self.pending_barrier[e]:
                d.signal = True
        cnt = {e: 0 for e in self.ENGS}
        for e in self.ENGS:
            for o in self.ops[e]:
                if o.is_dma:
                    continue
                if o.signal:
                    cnt[e] += 1
                    o.token = ("eng:" + e, cnt[e])
        names = set()
        for o in self.all:
            if o.signal:
                names.add(o.token[0])
        self.sems = {}
        for n in sorted(names):
            self.sems[n] = es.enter_context(nc.semaphore(n.replace(":", "_")))
        final_waits = {}
        self.n_waits = 0
        for e in self.ENGS:
            waited = {}
            for o in self.ops[e]:
                need = {}
                for d in o.deps:
                    k, v = d.token
                    if need.get(k, 0) < v:
                        need[k] = v
                o.extra_waits = []
                for k, v in need.items():
                    if waited.get(k, 0) < v:
                        waited[k] = v
                        o.extra_waits.append((k, v))
                        self.n_waits += 1
            need = {}
            for d in self.pending_barrier[e]:
                k, v = d.token
                if need.get(k, 0) < v:
                    need[k] = v
            final_waits[e] = [(k, v) for k, v in need.items() if waited.get(k, 0) < v]
        block = es.enter_context(nc.Block())
        sems = self.sems

        def make(ename):
            ops = self.ops[ename]
            fw = final_waits[ename]

            def run(eng):
                for o in ops:
                    for k, v in o.extra_waits:
                        eng.wait_ge(sems[k], v)
                    ins = o.fn(eng)
                    if o.signal:
                        k, v = o.token
                        ins.then_inc(sems[k], 16 if o.is_dma else 1)
                for k, v in fw:
                    eng.wait_ge(sems[k], v)
            return run

        block.sync(make("sp"))
        block.scalar(make("act"))
        block.vector(make("dve"))
        block.gpsimd(make("pool"))
        block.tensor(make("pe"))


class Arena:
    def __init__(self, tensor_ap, nbytes):
        self.t = tensor_ap
        self.n = nbytes
        self.off = 0
        self.namecnt = {}
        self.peak = 0

    def mark(self):
        return self.off

    def reset(self, m):
        self.off = m

    def alloc(self, name, free_elems, dtype):
        esz = 4 if dtype == F32 else 2
        nb = free_elems * esz
        nb_al = (nb + 63) // 64 * 64
        assert self.off + nb_al <= self.n, f"arena overflow {name}: {self.off}+{nb_al}>{self.n}"
        v = self.t[:, self.off:self.off + nb].bitcast(dtype)
        self.off += nb_al
        self.peak = max(self.peak, self.off)
        idx = self.namecnt.get(name, 0)
        self.namecnt[name] = idx + 1
        return v, Buf(f"{name}_{idx}")


def bc_mid(ap2, m):
    a = ap2.ap
    return AP(ap2.tensor, ap2.offset, [list(a[0]), [0, m], list(a[1])])


def bc_last(ap2, n):
    a = ap2.ap
    return AP(ap2.tensor, ap2.offset, [list(a[0]), list(a[1]), [0, n]])


def build(S=4096, L=2, dbg=False):
    NT = S // 128
    NQ = S // 256
    NB = S // 512
    NM = S // 256
    nc = bass.Bass("TRN2", target_bir_lowering=False)

    def din(name, shape, dt=F32):
        return nc.dram_tensor(name, shape, dt, kind="ExternalInput").ap()

    def dscr(name, shape, dt):
        return nc.dram_tensor(name, shape, dt, kind="Internal").ap()

    x_d = din("x", [S, D])
    w_in_d = din("w_in", [L, D, DIN])
    w_out_d = din("w_out", [L, D, D])
    w_up_d = din("w_up", [L, D, DFF])
    w_down_d = din("w_down", [L, DFF, D])
    g_pre_mix_d = din("norm_pre_mix", [L, D])
    g_post_mix_d = din("norm_post_mix", [L, D])
    g_pre_mlp_d = din("norm_pre_mlp", [L, D])
    g_post_mlp_d = din("norm_post_mlp", [L, D])
    lam_d = [din(n, [L, 64]) for n in ("lam_q1", "lam_k1", "lam_q2", "lam_k2")]
    subln_d = din("subln_a", [L, 128])
    relb_d = din("rel_bias", [L, 4, 257])
    gnb_d = din("gn_b", [L, 256])
    gnc_d = din("gn_c", [L, 256])
    ident_d = din("c_ident", [128, 128], BF16)
    ropeC_d = din("c_ropeC", [128, S])
    ropeS_d = din("c_ropeS", [128, S])
    mcc_d = din("c_mask_cc", [128, 128], BF16)
    mtri_d = din("c_mask_tri", [128, 128], BF16)
    negU_d = din("c_negU", [128, 128], BF16)
    negOnes_d = din("c_negOnes", [128, 128], BF16)
    mB_d = din("c_maskB", [128, 2 * 128])
    antiI_d = din("c_antiI", [128, 128])
    out_d = nc.dram_tensor("out", [S, D], F32, kind="ExternalOutput").ap()
    ocat_d = dscr("ocat", [S, D], BF16)
    xa_d = dscr("xa", [S, D], F32)
    xb_d = dscr("xb", [S, D], F32)
    h2T_d = dscr("h2T", [NT, 128, 8 * 128], BF16)
    E_d = dscr("Eext", [4, 384], F32)
    dbg_d = {}
    if dbg:
        dbg_d["ocat"] = nc.dram_tensor("dbg_ocat", [S, D], BF16, kind="ExternalOutput").ap()
        dbg_d["xa"] = nc.dram_tensor("dbg_xa", [S, D], F32, kind="ExternalOutput").ap()

    Sd = Sched()
    es = ExitStack()
    ARENA_BYTES = 206 * 1024
    arena_t = es.enter_context(nc.sbuf_tensor("arena", [128, ARENA_BYTES], U8))
    A = Arena(arena_t, ARENA_BYTES)
    pbank = []
    for i in range(8):
        t = es.enter_context(nc.psum_tensor(f"pb{i}", [128, 512], F32))
        pbank.append((t, Buf(f"pb{i}")))

    def pbf(i):
        return pbank[i][0][:, 0:512].bitcast(BF16)

    def dump(name, ap, buf):
        if not dbg:
            return
        t = nc.dram_tensor("dbg_" + name, list(ap.shape), ap.dtype, kind="ExternalOutput").ap()
        Sd.dma(lambda e: e.dma_start(out=t, in_=ap), "dbgdump_" + name, reads=[buf])

    ident, ident_b = A.alloc("ident", 128, BF16)
    mcc, mcc_b = A.alloc("mcc", 128, BF16)
    mtri, mtri_b = A.alloc("mtri", 128, BF16)
    negU, negU_b = A.alloc("negU", 128, BF16)
    negOnes, negOnes_b = A.alloc("negOnes", 128, BF16)
    for (t, b, d) in ((ident, ident_b, ident_d), (mcc, mcc_b, mcc_d), (mtri, mtri_b, mtri_d),
                      (negU, negU_b, negU_d), (negOnes, negOnes_b, negOnes_d)):
        Sd.dma(lambda e, t=t, d=d: e.dma_start(out=t, in_=d), b.name, writes=[b])

    def rstd_ops(ssq, out, n, rb):
        Sd.op("act", lambda e: e.activation(out=out, in_=ssq, func=AF.Ln, scale=1.0 / n, bias=eps_t[:, 0:1]),
              reads=[rb, eps_b], writes=[rb])
        Sd.op("act", lambda e: e.activation(out=out, in_=out, func=AF.Exp, scale=-0.5), reads=[rb], writes=[rb])

    eps_t, eps_b = A.alloc("eps", 1, F32)
    Sd.op("pool", lambda e: e.memset(eps_t, EPS), writes=[eps_b])

    base_mark = A.mark()

    def layer(l, x_src, x_dst):
        lam_init = 0.8 - 0.6 * math.exp(-0.3 * l)
        A.reset(base_mark)
        A.namecnt = dict(base_names)
        hT, hT_b = A.alloc("hT", 8 * S, BF16)
        hT3 = hT.rearrange("p (k s) -> p k s", k=8)
        p12_mark = A.mark()

        xin = [A.alloc("xin", D, F32) for _ in range(4)]
        hb = [A.alloc("hb", D, BF16) for _ in range(3)]
        junks = [A.alloc("junk", D, BF16) for _ in range(4)]
        st = [A.alloc("st", 4, F32) for _ in range(4)]

        def p1_load(i):
            xt, xt_b = xin[i % 4]
            Sd.dma(lambda e: e.dma_start(out=xt, in_=x_src[i * 128:(i + 1) * 128, :]), xt_b.name, writes=[xt_b])

        def p1_sq(i):
            xt, xt_b = xin[i % 4]
            s_, s_b = st[i % 4]
            junk, junk_b = junks[i % 4]
            Sd.op("act", lambda e: e.activation(out=junk, in_=xt, func=AF.Square, accum_out=s_[:, 0:1]),
                  reads=[xt_b], writes=[junk_b, s_b])
            rstd_ops(s_[:, 0:1], s_[:, 1:2], D, s_b)

        def p1_scale(i):
            xt, xt_b = xin[i % 4]
            ht, ht_b = hb[i % 3]
            s_, s_b = st[i % 4]
            Sd.op("dve", lambda e: e.tensor_scalar(out=ht, in0=xt, scalar1=s_[:, 1:2], scalar2=None, op0=ALU.mult),
                  reads=[xt_b, s_b], writes=[ht_b])

        def p1_tr(i):
            ht, ht_b = hb[i % 3]
            pv = pbf(i % 2)

            def tr(e):
                ins = None
                for kc in range(8):
                    ins = e.transpose(out=pv[:, kc * 128:(kc + 1) * 128], in_=ht[:, kc * 128:(kc + 1) * 128], identity=ident)
                return ins
            Sd.op("pe", tr, reads=[ht_b, ident_b], writes=[pbank[i % 2][1]])

        def p1_evac(i):
            pv = pbf(i % 2)
            if i % 2:
                Sd.op("dve", lambda e: e.tensor_copy(out=hT3[:, :, i * 128:(i + 1) * 128], in_=pv.rearrange("p (k t) -> p k t", k=8)),
                      reads=[pbank[i % 2][1]], writes=[hT_b])
            else:
                Sd.op("act", lambda e: e.activation(out=hT3[:, :, i * 128:(i + 1) * 128], in_=pv.rearrange("p (k t) -> p k t", k=8), func=AF.Copy),
                      reads=[pbank[i % 2][1]], writes=[hT_b])

        for i in range(min(2, NT)):
            p1_load(i)
        for j in range(NT + 3):
            if 0 <= j - 3 < NT:
                p1_evac(j - 3)
            if j < NT:
                p1_sq(j)
            if 0 <= j - 2 < NT:
                p1_tr(j - 2)
            if 0 <= j - 1 < NT:
                p1_scale(j - 1)
            if j + 2 < NT:
                p1_load(j + 2)
        Sd.barrier()
        A.reset(p12_mark)

        gin, gin_b = A.alloc("gin", 8, F32)
        Sd.dma(lambda e: e.dma_start(out=gin, in_=g_pre_mix_d[l].rearrange("(k p) -> p k", p=128), allow_slow_non_contiguous=True),
               gin_b.name, writes=[gin_b])
        stage = [A.alloc("stage", 8 * 384, F32) for _ in range(2)]
        wb = [A.alloc("wb", 8 * 384, BF16) for _ in range(2)]
        wslot = [0]

        def load_w(col_groups):
            k = wslot[0] % 2
            wslot[0] += 1
            ntot = sum(n for _, n in col_groups)
            sg, sg_b = stage[k]
            w_, w_b = wb[k]
            sv = sg[:, 0:8 * ntot].rearrange("p (k n) -> p k n", k=8)
            wv = w_[:, 0:8 * ntot].rearrange("p (k n) -> p k n", k=8)
            o = 0
            for (c0, n) in col_groups:
                Sd.dma(lambda e, sv=sv, o=o, n=n, c0=c0: e.dma_start(
                    out=sv[:, :, o:o + n], in_=w_in_d[l][:, c0:c0 + n].rearrange("(k p) n -> p k n", p=128)),
                    sg_b.name, writes=[sg_b])
                o += n
            Sd.op("dve", lambda e, sv=sv, wv=wv, ntot=ntot: e.tensor_tensor(out=wv, in0=sv, in1=bc_last(gin, ntot), op=ALU.mult),
                  reads=[sg_b, gin_b], writes=[w_b])
            return wv, w_b

        ostg = [A.alloc("ostg", 256, BF16) for _ in range(4)]
        ostg_i = [0]
        ftmp = [A.alloc("ftmp", 256, F32) for _ in range(4)]
        fsm = [A.alloc("fsm", 16, F32) for _ in range(4)]
        fin_i = [0]

        def proj_fm(wv, w_b, c0, dstT, dst_b, tb, pi, eng, scale=None, rows=None):
            pt, pb_ = pbank[pi]

            def mm(e):
                ins = None
                for kc in range(8):
                    ins = e.matmul(pt[:, 0:512], lhsT=wv[:, kc, c0:c0 + 128], rhs=hT3[:, kc, tb * 512:(tb + 1) * 512],
                                   start=(kc == 0), stop=(kc == 7))
                return ins
            Sd.op("pe", mm, reads=[w_b, hT_b], writes=[pb_])
            return pt, pb_

        def evac_copy(eng, out, in_, reads, writes, scale=None):
            if eng == "act":
                if scale is None:
                    Sd.op("act", lambda e: e.activation(out=out, in_=in_, func=AF.Copy), reads=reads, writes=writes)
                else:
                    Sd.op("act", lambda e: e.activation(out=out, in_=in_, func=AF.Copy, scale=scale), reads=reads, writes=writes)
            else:
                if scale is None:
                    Sd.op("dve", lambda e: e.tensor_copy(out=out, in_=in_), reads=reads, writes=writes)
                else:
                    Sd.op("dve", lambda e: e.tensor_scalar(out=out, in0=in_, scalar1=scale, scalar2=None, op0=ALU.mult),
                          reads=reads, writes=writes)

        fin_pending = []

        def fin_submit(stages, k=None):
            for stl in list(fin_pending):
                if stl[0] == k:
                    for f in stl[1:]:
                        f()
                    fin_pending.remove(stl)
            fin_pending.append([k] + list(stages))

        def tick():
            for stl in list(fin_pending):
                f = stl.pop(1)
                f()
                if len(stl) == 1:
                    fin_pending.remove(stl)

        def fin_flush():
            while fin_pending:
                tick()

        def fin_slot():
            k = fin_i[0] % 4
            fin_i[0] += 1
            return k

        def norm_stages(k, o32, o32_b, nh, hd, gtile, g_b, tile_i, col0):
            sm, sm_b = fsm[k]
            sq, sq_b = fsqs[k]
            og, og_b = ostg[k]
            w = nh * hd
            o3 = o32[:, 0:w].rearrange("p (h d) -> p h d", h=nh)

            def s_sq():
                if nh == 1:
                    Sd.op("act", lambda e: e.activation(out=sq[:, 0:w], in_=o32[:, 0:w], func=AF.Square, accum_out=sm[:, 0:1]),
                          reads=[o32_b], writes=[sq_b, sm_b])
                else:
                    Sd.op("pool", lambda e: e.tensor_tensor(out=sq[:, 0:w], in0=o32[:, 0:w], in1=o32[:, 0:w], op=ALU.mult),
                          reads=[o32_b], writes=[sq_b])

            def s_red():
                if nh > 1:
                    Sd.op("dve", lambda e: e.tensor_reduce(out=sm[:, 0:nh], in_=sq[:, 0:w].rearrange("p (h d) -> p h d", h=nh), axis=AX.X, op=ALU.add),
                          reads=[sq_b], writes=[sm_b])

            def s_ln():
                Sd.op("act", lambda e: e.activation(out=sm[:, 4:4 + nh], in_=sm[:, 0:nh], func=AF.Ln, scale=1.0 / hd, bias=eps_t[:, 0:1]),
                      reads=[sm_b, eps_b], writes=[sm_b])

            def s_exp():
                Sd.op("act", lambda e: e.activation(out=sm[:, 4:4 + nh], in_=sm[:, 4:4 + nh], func=AF.Exp, scale=-0.5), reads=[sm_b], writes=[sm_b])

            def s_scale():
                if nh == 1:
                    Sd.op("dve", lambda e: e.scalar_tensor_tensor(out=og[:, 0:w], in0=o32[:, 0:w], scalar=sm[:, 4:5], in1=gtile[:, 0:w],
                                                                 op0=ALU.mult, op1=ALU.mult), reads=[o32_b, sm_b, g_b], writes=[og_b])
                else:
                    Sd.op("dve", lambda e: e.tensor_tensor(out=o3, in0=o3, in1=bc_last(sm[:, 4:4 + nh], hd), op=ALU.mult),
                          reads=[o32_b, sm_b], writes=[o32_b])
                    Sd.op("dve", lambda e: e.tensor_tensor(out=og[:, 0:w], in0=o32[:, 0:w], in1=gtile[:, 0:w], op=ALU.mult),
                          reads=[o32_b, g_b], writes=[og_b])

            def s_store():
                Sd.dma(lambda e: e.dma_start(out=ocat_d[tile_i * 128:(tile_i + 1) * 128, col0:col0 + w], in_=og[:, 0:w]),
                       og_b.name, reads=[og_b])
            return [s_sq, s_red, s_ln, s_exp, s_scale, s_store]

        fsqs = [A.alloc("fsq", 256, F32) for _ in range(4)]
        p2_mark = A.mark()

        lamt, lamt_b = A.alloc("lamt", 4 * 64, F32)
        lams, lams_b = A.alloc("lams", 8, F32)
        gA, gA_b = A.alloc("gA", 128, F32)
        for j in range(4):
            Sd.dma(lambda e, j=j: e.dma_start(out=lamt[:, j * 64:(j + 1) * 64], in_=lam_d[j][l].partition_broadcast(128)),
                   lamt_b.name, writes=[lamt_b])
        Sd.dma(lambda e: e.dma_start(out=gA, in_=subln_d[l].partition_broadcast(128)), gA_b.name, writes=[gA_b])
        Sd.op("dve", lambda e: e.tensor_tensor(out=lamt[:, 0:64], in0=lamt[:, 0:64], in1=lamt[:, 64:128], op=ALU.mult),
              reads=[lamt_b], writes=[lamt_b])
        Sd.op("dve", lambda e: e.tensor_tensor(out=lamt[:, 128:192], in0=lamt[:, 128:192], in1=lamt[:, 192:256], op=ALU.mult),
              reads=[lamt_b], writes=[lamt_b])
        Sd.op("dve", lambda e: e.tensor_reduce(out=lams[:, 0:1], in_=lamt[:, 0:64], axis=AX.X, op=ALU.add), reads=[lamt_b], writes=[lams_b])
        Sd.op("dve", lambda e: e.tensor_reduce(out=lams[:, 1:2], in_=lamt[:, 128:192], axis=AX.X, op=ALU.add), reads=[lamt_b], writes=[lams_b])
        Sd.op("act", lambda e: e.activation(out=lams[:, 2:4], in_=lams[:, 0:2], func=AF.Exp), reads=[lams_b], writes=[lams_b])
        Sd.op("dve", lambda e: e.tensor_tensor(out=lams[:, 4:5], in0=lams[:, 3:4], in1=lams[:, 2:3], op=ALU.subtract), reads=[lams_b], writes=[lams_b])
        Sd.op("dve", lambda e: e.tensor_scalar(out=lams[:, 4:5], in0=lams[:, 4:5], scalar1=-lam_init, scalar2=None, op0=ALU.add), reads=[lams_b], writes=[lams_b])
        Sd.op("dve", lambda e: e.tensor_scalar(out=gA, in0=gA, scalar1=(1.0 - lam_init), scalar2=None, op0=ALU.mult), reads=[gA_b], writes=[gA_b])

        qT, qT_b = A.alloc("qT", S, BF16)
        kT0, kT0_b = A.alloc("kT0", S, BF16)
        kT1, kT1_b = A.alloc("kT1", S, BF16)
        vA, vA_b = A.alloc("vA", NT * 129, BF16)
        vA3 = vA.rearrange("p (t e) -> p t e", t=NT)
        wP, wP_b = A.alloc("wP", 8 * 256, BF16)
        wP3 = wP.rearrange("p (k n) -> p k n", k=8)
        ropeC = [A.alloc("ropeC", 512, F32) for _ in range(2)]
        ropeS = [A.alloc("ropeS", 512, F32) for _ in range(2)]
        rt1 = [A.alloc("rt1", 512, F32) for _ in range(2)]
        rt2 = [A.alloc("rt2", 512, F32) for _ in range(2)]
        PT = [A.alloc("PT", 512, BF16) for _ in range(4)]
        Sd.op("pool", lambda e: e.memset(kT0[64:128, :], 0.0), writes=[kT0_b])
        Sd.op("pool", lambda e: e.memset(kT1[0:64, :], 0.0), writes=[kT1_b])
        Sd.op("pool", lambda e: e.memset(wP, 0.0), writes=[wP_b])
        Sd.op("pool", lambda e: e.memset(vA3[:, :, 128:129], 1.0), writes=[vA_b])

        for h in range(4):
            wv, w_b = load_w([(h * 128, 128), (512 + h * 128, 128), (1024 + h * 128, 128)])
            w4 = wv[:, :, 0:256].rearrange("p k (g d) -> p k g d", g=4)
            wP4 = wP3.rearrange("p k (g d) -> p k g d", g=4)
            for kc in range(8):
                Sd.op("pool", lambda e, kc=kc: e.tensor_copy(out=wP4[:, kc, :, 0:8], in_=w4[:, kc, :, 8:16]), reads=[w_b], writes=[wP_b])
                Sd.op("pool", lambda e, kc=kc: e.tensor_copy(out=wP4[:, kc, :, 8:16], in_=w4[:, kc, :, 0:8]), reads=[w_b], writes=[wP_b])
            for tb in range(NB):
                rc, rc_b = ropeC[tb % 2]
                rs, rs_b = ropeS[tb % 2]
                Sd.dma(lambda e, rc=rc, tb=tb: e.dma_start(out=rc, in_=ropeC_d[:, tb * 512:(tb + 1) * 512]), rc_b.name, writes=[rc_b])
                Sd.dma(lambda e, rs=rs, tb=tb: e.dma_start(out=rs, in_=ropeS_d[:, tb * 512:(tb + 1) * 512]), rs_b.name, writes=[rs_b])
                for qk in range(2):
                    p1, p1_b = proj_fm(wv, w_b, qk * 128, None, None, tb, 6, None)
                    p2, p2_b = proj_fm(wP3, wP_b, qk * 128, None, None, tb, 7, None)
                    t1, t1_b = rt1[qk]
                    t2, t2_b = rt2[qk]
                    Sd.op("dve", lambda e, t1=t1, p1=p1, rc=rc: e.tensor_tensor(out=t1, in0=p1[:, 0:512], in1=rc, op=ALU.mult),
                          reads=[p1_b, rc_b], writes=[t1_b])
                    Sd.op("dve", lambda e, t2=t2, p2=p2, rs=rs: e.tensor_tensor(out=t2, in0=p2[:, 0:512], in1=rs, op=ALU.mult),
                          reads=[p2_b, rs_b], writes=[t2_b])
                    sl = slice(tb * 512, (tb + 1) * 512)
                    if qk == 0:
                        Sd.op("dve", lambda e, t1=t1, t2=t2, sl=sl: e.tensor_tensor(out=qT[:, sl], in0=t1, in1=t2, op=ALU.add),
                              reads=[t1_b, t2_b], writes=[qT_b])
                    else:
                        Sd.op("dve", lambda e, t1=t1, t2=t2, sl=sl: e.tensor_tensor(out=kT0[0:64, sl], in0=t1[0:64, :], in1=t2[0:64, :], op=ALU.add),
                              reads=[t1_b, t2_b], writes=[kT0_b])
                        Sd.op("dve", lambda e, t1=t1, t2=t2, sl=sl: e.tensor_tensor(out=kT1[64:128, sl], in0=t1[64:128, :], in1=t2[64:128, :], op=ALU.add),
                              reads=[t1_b, t2_b], writes=[kT1_b])
                pt, pb_ = pbank[6 + (tb % 2)]

                def mmv(e, tb=tb, pt=pt, wv=wv):
                    ins = None
                    for j in range(4):
                        ti = tb * 4 + j
                        for kc in range(8):
                            ins = e.matmul(pt[:, j * 128:(j + 1) * 128], lhsT=hT3[:, kc, ti * 128:(ti + 1) * 128], rhs=wv[:, kc, 256:384],
                                           start=(kc == 0), stop=(kc == 7))
                    return ins
                Sd.op("pe", mmv, reads=[w_b, hT_b], writes=[pb_])
                Sd.op("act", lambda e, tb=tb, pt=pt: e.activation(out=vA3[:, tb * 4:(tb + 1) * 4, 0:128],
                                                                 in_=pt[:, 0:512].rearrange("p (j e) -> p j e", j=4), func=AF.Copy),
                      reads=[pb_], writes=[vA_b])

            if l == 0 and h in (0, 1, 2):
                dump(f"qT{h}", qT, qT_b)
                dump(f"kT0_{h}", kT0, kT0_b)
                dump(f"kT1_{h}", kT1, kT1_b)
                dump(f"vA{h}", vA, vA_b)
                dump(f"wP{h}", wP, wP_b)
            steps = []
            for Q in range(NQ):
                for kt in range(2 * Q + 2):
                    steps.append((Q, kt))

            def emit_S(si):
                Q, kt = steps[si]
                pt, pb_ = pbank[(0, 1, 6)[si % 3]]

                def mm(e, Q=Q, kt=kt, pt=pt):
                    e.matmul(pt[:, 0:256], lhsT=kT0[:, kt * 128:(kt + 1) * 128], rhs=qT[:, Q * 256:(Q + 1) * 256], start=True, stop=True)
                    return e.matmul(pt[:, 256:512], lhsT=kT1[:, kt * 128:(kt + 1) * 128], rhs=qT[:, Q * 256:(Q + 1) * 256], start=True, stop=True)
                Sd.op("pe", mm, reads=[kT0_b, kT1_b, qT_b], writes=[pb_])

            def emit_rest(si):
                Q, kt = steps[si]
                r = kt - 2 * Q
                pt, pb_ = pbank[(0, 1, 6)[si % 3]]
                P, P_b = PT[si % 4]
                Sd.op("act", lambda e: e.activation(out=P, in_=pt[:, 0:512], func=AF.Exp, scale=0.125), reads=[pb_], writes=[P_b])
                P3 = P.rearrange("p (n q) -> p n q", n=2)
                if r >= 0:
                    Sd.op("dve", lambda e: e.tensor_tensor(out=P3[:, :, r * 128:(r + 1) * 128], in0=P3[:, :, r * 128:(r + 1) * 128],
                                                           in1=bc_mid(mcc, 2), op=ALU.mult), reads=[P_b, mcc_b], writes=[P_b])
                first = (kt == 0)
                ab = 2 + 2 * (Q % 2)

                def pv(e):
                    ins = None
                    for n in range(2):
                        acc = pbank[ab + n][0]
                        for qs in range(max(r, 0), 2):
                            last = (kt == 2 * Q + qs)
                            ins = e.matmul(acc[:, qs * 129:(qs + 1) * 129], lhsT=P3[:, n, qs * 128:(qs + 1) * 128], rhs=vA3[:, kt, :],
                                           start=(first and qs == 0), stop=last, skip_group_check=True)
                    return ins
                Sd.op("pe", pv, reads=[P_b, vA_b], writes=[pbank[ab][1], pbank[ab + 1][1]])
                for qs in (range(2) if kt == 2 * Q + 1 else ()):
                    ti = Q * 2 + qs
                    k = fin_slot()
                    sm, sm_b = fsm[k]
                    o32, o32_b = ftmp[k]
                    a0, a0_b = pbank[ab]
                    a1, a1_b = pbank[ab + 1]
                    c = qs * 129

                    def s_comb(sm=sm, sm_b=sm_b, o32=o32, o32_b=o32_b, c=c):
                        Sd.op("dve", lambda e: e.reciprocal(out=sm[:, 8:9], in_=a0[:, c + 128:c + 129]), reads=[a0_b], writes=[sm_b])
                        Sd.op("dve", lambda e: e.reciprocal(out=sm[:, 9:10], in_=a1[:, c + 128:c + 129]), reads=[a1_b], writes=[sm_b])
                        Sd.op("dve", lambda e: e.tensor_tensor(out=sm[:, 9:10], in0=sm[:, 9:10], in1=lams[:, 4:5], op=ALU.mult),
                              reads=[sm_b, lams_b], writes=[sm_b])
                        Sd.op("dve", lambda e: e.tensor_scalar(out=o32[:, 0:128], in0=a0[:, c:c + 128], scalar1=sm[:, 8:9], scalar2=None, op0=ALU.mult),
                              reads=[a0_b, sm_b], writes=[o32_b])
                        Sd.op("dve", lambda e: e.scalar_tensor_tensor(out=o32[:, 0:128], in0=a1[:, c:c + 128], scalar=sm[:, 9:10], in1=o32[:, 0:128],
                                                                     op0=ALU.mult, op1=ALU.add), reads=[a1_b, sm_b, o32_b], writes=[o32_b])
                    fin_submit([s_comb] + norm_stages(k, o32, o32_b, 1, 128, gA, gA_b, ti, h * 128), k)

            n = len(steps)
            for si in range(n + 2):
                if si < n:
                    emit_S(si)
                if si >= 2:
                    emit_rest(si - 2)
                    tick()
        fin_flush()
        Sd.barrier()
        A.reset(p2_mark)

        qB, qB_b = A.alloc("qB", 2 * S, BF16)
        qB3 = qB.rearrange("p (f s) -> p f s", f=2)
        kB = [A.alloc("kB", S, BF16) for _ in range(4)]
        vB, vB_b = A.alloc("vB", NT * 4 * 65, BF16)
        vB4 = vB.rearrange("p (t h e) -> p t h e", t=NT, h=4)
        BT, BT_b = A.alloc("BT", 5 * 512, F32)
        BT3 = BT.rearrange("p (d x) -> p d x", d=5)
        Rt, Rt_b = A.alloc("Rt", 5 * 128, F32)
        antiI, antiI_b = A.alloc("antiI", 128, F32)
        mB, mB_b = A.alloc("mB", 256, F32)
        gnb, gnb_b = A.alloc("gnb", 256, F32)
        BTb, BTb_b = A.alloc("BTb", 5 * 512, BF16)
        BTb3 = BTb.rearrange("p (d x) -> p d x", d=5)
        PB = [A.alloc("PB", 512, BF16) for _ in range(3)]
        Sd.dma(lambda e: e.dma_start(out=antiI, in_=antiI_d), antiI_b.name, writes=[antiI_b])
        Sd.dma(lambda e: e.dma_start(out=mB, in_=mB_d), mB_b.name, writes=[mB_b])
        Sd.dma(lambda e: e.dma_start(out=gnb, in_=gnb_d[l].partition_broadcast(128)), gnb_b.name, writes=[gnb_b])
        E_b = Buf("E_dram")
        c4, c4_b = A.alloc("c4", 1, F32)
        ctile, ctile_b = A.alloc("ctile", 128, F32)
        c256, c256_b = A.alloc("c256", 4, F32)
        Sd.dma(lambda e: e.dma_start(out=c4[0:4, 0:1], in_=relb_d[l, :, 256:257], allow_slow_non_contiguous=True), c4_b.name, writes=[c4_b])
        Sd.dma(lambda e: e.dma_start(out=c256.rearrange("p (h o) -> p h o", o=1),
                                     in_=AP(relb_d.tensor, relb_d[l, 0:1, 256:257].offset, [[0, 128], [257, 4], [1, 1]]),
                                     allow_slow_non_contiguous=True),
               c256_b.name, writes=[c256_b])
        Sd.op("dve", lambda e: e.tensor_copy(out=ctile[0:4, :], in_=AP(c4.tensor, c4.offset, [[c4.ap[0][0], 4], [0, 128]])),
              reads=[c4_b], writes=[ctile_b])
        Sd.dma(lambda e: e.dma_start(out=E_d[:, 0:256], in_=relb_d[l, :, 1:257]), "E_dram", writes=[E_b])
        Sd.dma(lambda e: e.dma_start(out=E_d[:, 256:384], in_=ctile[0:4, :]), "E_dram", reads=[ctile_b], writes=[E_b])
        for hh in range(4):
            k_, k_b = kB[hh]
            if hh % 2 == 0:
                Sd.op("pool", lambda e, k_=k_: e.memset(k_[64:128, :], 0.0), writes=[k_b])
            else:
                Sd.op("pool", lambda e, k_=k_: e.memset(k_[0:64, :], 0.0), writes=[k_b])
        Sd.op("pool", lambda e: e.memset(vB4[:, :, :, 64:65], 1.0), writes=[vB_b])
        for hh in range(4):
            Sd.dma(lambda e, hh=hh: e.dma_start(out=Rt[:, 0:256].rearrange("p (d j) -> p d j", d=2),
                                                in_=AP(E_d.tensor, E_d[hh:hh + 1, 0:1].offset, [[1, 128], [128, 2], [1, 128]])),
                   Rt_b.name, reads=[E_b], writes=[Rt_b])
            pt, pb_ = pbank[4 + hh % 2]
            Sd.op("pe", lambda e, pt=pt: e.matmul(pt[:, 0:256], lhsT=antiI, rhs=Rt[:, 0:256], start=True, stop=True),
                  reads=[antiI_b, Rt_b], writes=[pb_])
            Sd.op("dve", lambda e, pt=pt, hh=hh: e.tensor_copy(
                out=BT3[:, 0:2, hh * 128:(hh + 1) * 128], in_=pt[:, 0:256].rearrange("p (d j) -> p d j", d=2)),
                reads=[pb_], writes=[BT_b])
        for d in range(2, 5):
            Sd.op("dve", lambda e, d=d: e.tensor_copy(out=BT3[:, d, :].rearrange("p (h j) -> p h j", h=4), in_=bc_last(c256, 128)),
                  reads=[c256_b], writes=[BT_b])
        Sd.op("dve", lambda e: e.tensor_tensor(out=BT3[:, 0, :].rearrange("p (h j) -> p h j", h=4), in0=BT3[:, 0, :].rearrange("p (h j) -> p h j", h=4),
                                               in1=bc_mid(mB[:, 0:128], 4), op=ALU.add), reads=[BT_b, mB_b], writes=[BT_b])
        Sd.op("dve", lambda e: e.tensor_tensor(out=BT3[:, 4, :].rearrange("p (h j) -> p h j", h=4), in0=BT3[:, 4, :].rearrange("p (h j) -> p h j", h=4),
                                               in1=bc_mid(mB[:, 128:256], 4), op=ALU.add), reads=[BT_b, mB_b], writes=[BT_b])
        Sd.op("dve", lambda e: e.tensor_copy(out=BTb, in_=BT), reads=[BT_b], writes=[BTb_b])
        wq, wq_b = load_w([(1536, 256)])
        for tb in range(NB):
            for ft in range(2):
                pt, pb_ = proj_fm(wq, wq_b, ft * 128, None, None, tb, 4 + ft, None)
                evac_copy("act" if ft else "dve", qB3[:, ft, tb * 512:(tb + 1) * 512], pt[:, 0:512], [pb_], [qB_b])
        wk, wk_b = load_w([(1792, 256)])
        for tb in range(NB):
            for ft in range(2):
                pt, pb_ = proj_fm(wk, wk_b, ft * 128, None, None, tb, 4 + ft, None)
                k0, k0_b = kB[ft * 2]
                k1, k1_b = kB[ft * 2 + 1]
                evac_copy("act", k0[0:64, tb * 512:(tb + 1) * 512], pt[0:64, 0:512], [pb_], [k0_b], scale=0.125)
                evac_copy("dve", k1[64:128, tb * 512:(tb + 1) * 512], pt[64:128, 0:512], [pb_], [k1_b], scale=0.125)
        wvv, wvv_b = load_w([(2048, 256)])
        for tp in range(NT // 2):
            pt, pb_ = pbank[4 + tp % 2]

            def mmv(e, tp=tp, pt=pt):
                ins = None
                for j in range(2):
                    ti = tp * 2 + j
                    for kc in range(8):
                        ins = e.matmul(pt[:, j * 256:(j + 1) * 256], lhsT=hT3[:, kc, ti * 128:(ti + 1) * 128], rhs=wvv[:, kc, 0:256],
                                       start=(kc == 0), stop=(kc == 7))
                return ins
            Sd.op("pe", mmv, reads=[wvv_b, hT_b], writes=[pb_])
            Sd.op("act" if tp % 2 else "dve",
                  (lambda e, tp=tp, pt=pt: e.activation(out=vB4[:, tp * 2:tp * 2 + 2, :, 0:64],
                                                        in_=pt[:, 0:512].rearrange("p (j h e) -> p j h e", j=2, h=4), func=AF.Copy))
                  if tp % 2 else
                  (lambda e, tp=tp, pt=pt: e.tensor_copy(out=vB4[:, tp * 2:tp * 2 + 2, :, 0:64],
                                                         in_=pt[:, 0:512].rearrange("p (j h e) -> p j h e", j=2, h=4))),
                  reads=[pb_], writes=[vB_b])
        stepsB = []
        for i in range(NT):
            for d in range(4, -1, -1):
                if i - d >= 0:
                    stepsB.append((i, d))

        def emitB_S(si):
            i, d = stepsB[si]
            j = i - d
            pt, pb_ = pbank[(0, 1, 4)[si % 3]]

            def mm(e):
                ins = None
                for hh in range(4):
                    ins = e.matmul(pt[:, hh * 128:(hh + 1) * 128], lhsT=kB[hh][0][:, j * 128:(j + 1) * 128],
                                   rhs=qB3[:, hh // 2, i * 128:(i + 1) * 128], start=(hh == 0), stop=False, skip_group_check=True)
                return e.matmul(pt[:, 0:512], lhsT=ident, rhs=BTb3[:, d, :], start=False, stop=True, skip_group_check=True)
            Sd.op("pe", mm, reads=[kB[0][1], kB[1][1], kB[2][1], kB[3][1], qB_b, BTb_b, ident_b], writes=[pb_])

        def emitB_rest(si):
            i, d = stepsB[si]
            j = i - d
            pt, pb_ = pbank[(0, 1, 4)[si % 3]]
            P, P_b = PB[si % 3]
            Sd.op("act", lambda e: e.activation(out=P, in_=pt[:, 0:512], func=AF.Exp), reads=[pb_], writes=[P_b])
            first = (d == min(4, i))
            acc, acc_b = pbank[2 + (i % 2)]

            def pv(e):
                ins = None
                for hh in range(4):
                    ins = e.matmul(acc[:, hh * 65:(hh + 1) * 65], lhsT=P[:, hh * 128:(hh + 1) * 128], rhs=vB4[:, j, hh, :],
                                   start=(first and hh == 0), stop=(d == 0), skip_group_check=True)
                return ins
            Sd.op("pe", pv, reads=[P_b, vB_b], writes=[acc_b])
            if d == 0:
                k = fin_slot()
                sm, sm_b = fsm[k]
                o32, o32_b = ftmp[k]
                a3 = acc[:, 0:260].rearrange("p (h e) -> p h e", h=4)

                def s_comb():
                    Sd.op("dve", lambda e: e.reciprocal(out=sm[:, 8:12], in_=a3[:, :, 64]), reads=[acc_b], writes=[sm_b])
                    Sd.op("dve", lambda e: e.tensor_tensor(out=o32.rearrange("p (h e) -> p h e", h=4), in0=a3[:, :, 0:64],
                                                           in1=bc_last(sm[:, 8:12], 64), op=ALU.mult), reads=[acc_b, sm_b], writes=[o32_b])
                fin_submit([s_comb] + norm_stages(k, o32, o32_b, 4, 64, gnb, gnb_b, i, 512), k)

        nB = len(stepsB)
        for si in range(nB + 2):
            if si < nB:
                emitB_S(si)
            if si >= 2:
                emitB_rest(si - 2)
                tick()
        fin_flush()
        Sd.barrier()
        A.reset(p2_mark)

        qC, qC_b = A.alloc("qC", S, BF16)
        kC = [A.alloc("kC", S, BF16) for _ in range(2)]
        vC, vC_b = A.alloc("vC", NT * 128, BF16)
        vC4 = vC.rearrange("p (t h e) -> p t h e", t=NT, h=2)
        gnc, gnc_b = A.alloc("gnc", 256, F32)
        Sd.dma(lambda e: e.dma_start(out=gnc, in_=gnc_d[l].partition_broadcast(128)), gnc_b.name, writes=[gnc_b])
        Ebuf = [A.alloc("Ebuf", 512, F32) for _ in range(2)]
        Sp = [[A.alloc("Sp", 512, BF16) for _ in range(2)] for _ in range(2)]
        SpSum = [A.alloc("SpSum", 512, BF16) for _ in range(2)]
        AT = [[A.alloc("AT", 512, BF16) for _ in range(2)] for _ in range(2)]
        Sd.op("pool", lambda e: e.memset(kC[0][0][64:128, :], 0.0), writes=[kC[0][1]])
        Sd.op("pool", lambda e: e.memset(kC[1][0][0:64, :], 0.0), writes=[kC[1][1]])
        zbank = [[0, 1], [6, 7]]
        for hp in range(2):
            wv, w_b = load_w([(2304 + hp * 128, 128), (2560 + hp * 128, 128), (2816 + hp * 128, 128)])
            for tb in range(NB):
                pt, pb_ = proj_fm(wv, w_b, 0, None, None, tb, 4, None)
                evac_copy("dve", qC[:, tb * 512:(tb + 1) * 512], pt[:, 0:512], [pb_], [qC_b])
                pt, pb_ = proj_fm(wv, w_b, 128, None, None, tb, 5, None)
                evac_copy("act", kC[0][0][0:64, tb * 512:(tb + 1) * 512], pt[0:64, 0:512], [pb_], [kC[0][1]], scale=0.125)
                evac_copy("dve", kC[1][0][64:128, tb * 512:(tb + 1) * 512], pt[64:128, 0:512], [pb_], [kC[1][1]], scale=0.125)
                pt, pb_ = pbank[5]

                def mmv(e, tb=tb, pt=pt, wv=wv):
                    ins = None
                    for j in range(4):
                        ti = tb * 4 + j
                        for kc in range(8):
                            ins = e.matmul(pt[:, j * 128:(j + 1) * 128], lhsT=hT3[:, kc, ti * 128:(ti + 1) * 128], rhs=wv[:, kc, 256:384],
                                           start=(kc == 0), stop=(kc == 7))
                    return ins
                Sd.op("pe", mmv, reads=[w_b, hT_b], writes=[pb_])
                Sd.op("act", lambda e, tb=tb, pt=pt: e.activation(out=vC4[:, tb * 4:(tb + 1) * 4, :, :],
                                                                 in_=pt[:, 0:512].rearrange("p (j h e) -> p j h e", j=4, h=2), func=AF.Copy),
                      reads=[pb_], writes=[vC_b])
            stepsC = []
            for Q in range(NB):
                for kt in range(4 * Q + 3, -1, -1):
                    stepsC.append((Q, kt))

            def cols(Q, kt):
                r = kt - 4 * Q
                return r, max(r, 0) * 128

            def emitC_QK(si):
                Q, kt = stepsC[si]
                r, c0 = cols(Q, kt)
                for s in range(2):
                    z, z_b = pbank[zbank[si % 2][s]]
                    Sd.op("pe", lambda e, z=z, s=s, kt=kt, Q=Q, c0=c0: e.matmul(
                        z[:, c0:512], lhsT=kC[s][0][:, kt * 128:(kt + 1) * 128], rhs=qC[:, Q * 512 + c0:(Q + 1) * 512],
                        start=True, stop=False, skip_group_check=True), reads=[kC[s][1], qC_b], writes=[z_b])

            def emitC_rest(si):
                Q, kt = stepsC[si]
                r, c0 = cols(Q, kt)
                firstQ = (kt == 4 * Q + 3)
                for s in range(2):
                    if firstQ:
                        Sd.op("pool", lambda e, s=s: e.memset(SpSum[s][0], 0.0), writes=[SpSum[s][1]])
                for s in range(2):
                    z, z_b = pbank[zbank[si % 2][s]]
                    E_, E_b2 = pbank[4 + s] if si % 2 == 0 else Ebuf[s]
                    sp_, sp_b = Sp[s][si % 2]
                    Sd.op("act", lambda e, z=z, E_=E_, c0=c0: e.activation(out=E_[:, c0:512], in_=z[:, c0:512], func=AF.Exp),
                          reads=[z_b], writes=[E_b2])
                    Sd.op("act", lambda e, E_=E_, sp_=sp_, c0=c0: e.activation(out=sp_[:, c0:512], in_=E_[:, c0:512], func=AF.Ln, bias=one_t[:, 0:1]),
                          reads=[E_b2, one_b], writes=[sp_b])
                    if r >= 0:
                        Sd.op("dve", lambda e, sp_=sp_, c0=c0: e.tensor_tensor(out=sp_[:, c0:c0 + 128], in0=sp_[:, c0:c0 + 128], in1=mtri, op=ALU.mult),
                              reads=[sp_b, mtri_b], writes=[sp_b])

                    def cum(e, z=z, sp_=sp_, s=s, c0=c0):
                        ins = e.matmul(z[:, c0:512], lhsT=negU, rhs=sp_[:, c0:512], start=False, stop=firstQ, skip_group_check=True)
                        if not firstQ:
                            ins = e.matmul(z[:, c0:512], lhsT=negOnes, rhs=SpSum[s][0][:, c0:512], start=False, stop=True, skip_group_check=True)
                        return ins
                    Sd.op("pe", cum, reads=[sp_b, negU_b, negOnes_b, SpSum[s][1]], writes=[z_b])
                for s in range(2):
                    z, z_b = pbank[zbank[si % 2][s]]
                    sp_, sp_b = Sp[s][si % 2]
                    a_, a_b = AT[s][si % 2]
                    Sd.op("act", lambda e, z=z, a_=a_, c0=c0: e.activation(out=a_[:, c0:512], in_=z[:, c0:512], func=AF.Exp),
                          reads=[z_b], writes=[a_b])
                    if r >= 0:
                        Sd.op("dve", lambda e, a_=a_, c0=c0: e.tensor_tensor(out=a_[:, c0:c0 + 128], in0=a_[:, c0:c0 + 128], in1=mtri, op=ALU.mult),
                              reads=[a_b, mtri_b], writes=[a_b])
                    if kt > 0:
                        Sd.op("dve", lambda e, s=s, sp_=sp_, c0=c0: e.tensor_tensor(out=SpSum[s][0][:, c0:512], in0=SpSum[s][0][:, c0:512],
                                                                                      in1=sp_[:, c0:512], op=ALU.add),
                              reads=[SpSum[s][1], sp_b], writes=[SpSum[s][1]])
                    acc, acc_b = pbank[2 + (Q % 2)]

                    def pv(e, a_=a_, s=s, acc=acc):
                        ins = None
                        for qs in range(max(r, 0), 4):
                            firstq = (kt == 4 * Q + 3) and s == 0 and qs == 3
                            ins = e.matmul(acc[:, s * 256 + qs * 64:s * 256 + (qs + 1) * 64], lhsT=a_[:, qs * 128:(qs + 1) * 128],
                                           rhs=vC4[:, kt, s, :], start=firstq, stop=(kt == 0), skip_group_check=True)
                        return ins
                    Sd.op("pe", pv, reads=[a_b, vC_b], writes=[acc_b])
                if kt == 0:
                    acc, acc_b = pbank[2 + (Q % 2)]
                    a4 = acc[:, 0:512].rearrange("p (s q e) -> p s q e", s=2, q=4)
                    for qs in range(4):
                        ti = Q * 4 + qs
                        k = fin_slot()
                        o32, o32_b = ftmp[k]

                        def s_comb(o32=o32, o32_b=o32_b, qs=qs, acc_b=acc_b, a4=a4):
                            Sd.op("dve", lambda e: e.tensor_copy(out=o32[:, 0:128].rearrange("p (s e) -> p s e", s=2), in_=a4[:, :, qs, :]),
                                  reads=[acc_b], writes=[o32_b])
                        fin_submit([s_comb] + norm_stages(k, o32, o32_b, 2, 64, gnc[:, hp * 128:(hp + 1) * 128], gnc_b, ti, 768 + hp * 128), k)

            nC = len(stepsC)
            for si in range(nC + 1):
                if si < nC:
                    emitC_QK(si)
                if si >= 1:
                    emitC_rest(si - 1)
                    tick()
        fin_flush()
        Sd.barrier()
        if dbg and l == 0:
            Sd.dma(lambda e: e.dma_start(out=dbg_d["ocat"], in_=ocat_d), "dbg1")
            Sd.barrier()

        A.reset(base_mark)
        wd, wd_b = A.alloc("wd", 32 * D, BF16)
        wd3 = wd.rearrange("p (k n) -> p k n", k=32)
        gpl, gpl_b = A.alloc("gpl", D, F32)
        gin2, gin2_b = A.alloc("gin2", 8, F32)
        Sd.dma(lambda e: e.dma_start(out=gpl, in_=g_post_mlp_d[l].partition_broadcast(128)), gpl_b.name, writes=[gpl_b])
        Sd.dma(lambda e: e.dma_start(out=gin2, in_=g_pre_mlp_d[l].rearrange("(k p) -> p k", p=128), allow_slow_non_contiguous=True),
               gin2_b.name, writes=[gin2_b])
        p4_mark = A.mark()
        stg4 = [A.alloc("stg4", 2048, F32) for _ in range(3)]
        ci = [0]

        def load_wd_chunk(fc2, engs=("dve", "act")):
            sg, sg_b = stg4[ci[0] % 3]
            Sd.dma(lambda e: e.dma_start(out=sg.rearrange("p (f n) -> p f n", f=2),
                                         in_=w_down_d[l][fc2 * 256:(fc2 + 1) * 256, :].rearrange("(f p) n -> p f n", p=128)),
                   sg_b.name, writes=[sg_b])
            eng = engs[ci[0] % len(engs)]
            dst = wd3[:, fc2 * 2:(fc2 + 1) * 2, :]
            src_ = sg.rearrange("p (f n) -> p f n", f=2)
            if eng == "act":
                Sd.op("act", lambda e: e.activation(out=dst, in_=src_, func=AF.Copy), reads=[sg_b], writes=[wd_b])
            else:
                Sd.op(eng, lambda e: e.tensor_copy(out=dst, in_=src_), reads=[sg_b], writes=[wd_b])
            ci[0] += 1

        wo, wo_b = A.alloc("wo", 8 * D, BF16)
        wo3 = wo.rearrange("p (k n) -> p k n", k=8)
        gpm, gpm_b = A.alloc("gpm", D, F32)
        Sd.dma(lambda e: e.dma_start(out=gpm, in_=g_post_mix_d[l].partition_broadcast(128)), gpm_b.name, writes=[gpm_b])
        for kc in range(8):
            sg, sg_b = stg4[ci[0] % 3]
            ci[0] += 1
            Sd.dma(lambda e, sg=sg, kc=kc: e.dma_start(out=sg[:, 0:1024], in_=w_out_d[l][kc * 128:(kc + 1) * 128, :]), sg_b.name, writes=[sg_b])
            evac_copy("dve" if kc % 2 else "act", wo3[:, kc, :], sg[:, 0:1024], [sg_b], [wo_b])
        NS = 5
        oc = [A.alloc("oc", D, BF16) for _ in range(NS)]
        x3 = [A.alloc("x3", D, F32) for _ in range(NS)]
        oT = [A.alloc("oT", D, BF16) for _ in range(2)]
        xn = [A.alloc("xn", D, F32) for _ in range(3)]
        h2 = [A.alloc("h2", D, BF16) for _ in range(2)]
        h2o = [A.alloc("h2o", D, BF16) for _ in range(2)]
        s3 = [A.alloc("s3", 8, F32) for _ in range(4)]
        junk3s = [A.alloc("junk3", D, BF16) for _ in range(4)]
        jc3 = [0]

        def next_junk3():
            jc3[0] += 1
            return junk3s[jc3[0] % 4]

        def p3_load(i):
            oc_, oc_b = oc[i % NS]
            x_, x_b = x3[i % NS]
            Sd.dma(lambda e: e.dma_start(out=oc_, in_=ocat_d[i * 128:(i + 1) * 128, :]), oc_b.name, writes=[oc_b], eng="act")
            Sd.dma(lambda e: e.dma_start(out=x_, in_=x_src[i * 128:(i + 1) * 128, :]), x_b.name, writes=[x_b], eng="act")

        def st_tr(i):
            k = i % 2
            oc_, oc_b = oc[i % NS]
            pv = pbf(k)

            def tr(e):
                ins = None
                for kc in range(8):
                    ins = e.transpose(out=pv[:, kc * 128:(kc + 1) * 128], in_=oc_[:, kc * 128:(kc + 1) * 128], identity=ident)
                return ins
            Sd.op("pe", tr, reads=[oc_b, ident_b], writes=[pbank[k][1]])

        def st_evac(i):
            k = i % 2
            oT_, oT_b = oT[k]
            pv = pbf(k)
            Sd.op("dve", lambda e: e.tensor_copy(out=oT_, in_=pv), reads=[pbank[k][1]], writes=[oT_b])

        def st_mm(i):
            k = i % 2
            oT_, oT_b = oT[k]
            oT3 = oT_.rearrange("p (k t) -> p k t", k=8)
            for hf in range(2):
                pt, pb_ = pbank[2 + k * 2 + hf]

                def mm(e, pt=pt, hf=hf):
                    ins = None
                    for kc in range(8):
                        ins = e.matmul(pt[:, 0:512], lhsT=oT3[:, kc, :], rhs=wo3[:, kc, hf * 512:(hf + 1) * 512], start=(kc == 0), stop=(kc == 7))
                    return ins
                Sd.op("pe", mm, reads=[oT_b, wo_b], writes=[pb_])

        def st_sqy(i):
            k = i % 2
            s_, s_b = s3[i % 4]
            for hf in range(2):
                pt, pb_ = pbank[2 + k * 2 + hf]
                junk3, junk3_b = next_junk3()
                Sd.op("act", lambda e, pt=pt, hf=hf, junk3=junk3: e.activation(out=junk3[:, 0:512], in_=pt[:, 0:512], func=AF.Square, accum_out=s_[:, hf:hf + 1]),
                      reads=[pb_], writes=[junk3_b, s_b])

        def st_ssqadd(i):
            s_, s_b = s3[i % 4]
            Sd.op("dve", lambda e: e.tensor_tensor(out=s_[:, 2:3], in0=s_[:, 0:1], in1=s_[:, 1:2], op=ALU.add), reads=[s_b], writes=[s_b])

        def st_rstd1(i):
            s_, s_b = s3[i % 4]
            rstd_ops(s_[:, 2:3], s_[:, 3:4], D, s_b)

        def st_xn(i):
            k = i % 2
            x_, x_b = x3[i % NS]
            xn_, xn_b = xn[i % 3]
            s_, s_b = s3[i % 4]
            for hf in range(2):
                pt, pb_ = pbank[2 + k * 2 + hf]
                Sd.op("dve", lambda e, pt=pt, hf=hf: e.scalar_tensor_tensor(
                    out=xn_[:, hf * 512:(hf + 1) * 512], in0=pt[:, 0:512], scalar=s_[:, 3:4], in1=gpm[:, hf * 512:(hf + 1) * 512],
                    op0=ALU.mult, op1=ALU.mult), reads=[pb_, s_b, gpm_b], writes=[xn_b])
            Sd.op("dve", lambda e: e.tensor_tensor(out=xn_, in0=xn_, in1=x_, op=ALU.add), reads=[xn_b, x_b], writes=[xn_b])
            Sd.dma(lambda e: e.dma_start(out=xa_d[i * 128:(i + 1) * 128, :], in_=xn_), xn_b.name + "s", reads=[xn_b])

        def st_sqx(i):
            xn_, xn_b = xn[i % 3]
            s_, s_b = s3[i % 4]
            junk3, junk3_b = next_junk3()
            Sd.op("act", lambda e: e.activation(out=junk3, in_=xn_, func=AF.Square, accum_out=s_[:, 4:5]),
                  reads=[xn_b], writes=[junk3_b, s_b])
            rstd_ops(s_[:, 4:5], s_[:, 5:6], D, s_b)

        def st_h2(i):
            xn_, xn_b = xn[i % 3]
            h2_, h2_b = h2[i % 2]
            s_, s_b = s3[i % 4]
            Sd.op("dve", lambda e: e.tensor_scalar(out=h2_, in0=xn_, scalar1=s_[:, 5:6], scalar2=None, op0=ALU.mult),
                  reads=[xn_b, s_b], writes=[h2_b])

        def st_tr2(i):
            k = i % 2
            h2_, h2_b = h2[k]
            pv2 = pbf(6 + k)

            def tr2(e):
                ins = None
                for kc in range(8):
                    ins = e.transpose(out=pv2[:, kc * 128:(kc + 1) * 128], in_=h2_[:, kc * 128:(kc + 1) * 128], identity=ident)
                return ins
            Sd.op("pe", tr2, reads=[h2_b, ident_b], writes=[pbank[6 + k][1]])

        def st_h2o(i):
            k = i % 2
            h2o_, h2o_b = h2o[k]
            pv2 = pbf(6 + k)
            Sd.op("act", lambda e: e.activation(out=h2o_, in_=pv2, func=AF.Copy), reads=[pbank[6 + k][1]], writes=[h2o_b])
            Sd.dma(lambda e: e.dma_start(out=h2T_d[i], in_=h2o_), h2o_b.name + "s", reads=[h2o_b])

        def ok(t):
            return 0 <= t < NT

        for i in range(min(3, NT)):
            p3_load(i)
        wd_next = 0
        for j in range(NT + 4):
            if ok(j):
                st_tr(j)
            if ok(j - 2):
                st_ssqadd(j - 2)
                st_rstd1(j - 2)
            if ok(j - 4):
                st_tr2(j - 4)
            if ok(j):
                st_evac(j)
            if ok(j - 3):
                st_sqx(j - 3)
            if ok(j - 1):
                st_mm(j - 1)
            if ok(j - 2):
                st_xn(j - 2)
            if ok(j - 4):
                st_h2o(j - 4)
            if ok(j - 3):
                st_h2(j - 3)
            if ok(j - 1):
                st_sqy(j - 1)
            if j + 3 < NT:
                p3_load(j + 3)
            if j % 2 == 1 and wd_next < 16:
                load_wd_chunk(wd_next)
                wd_next += 1
        while wd_next < 16:
            load_wd_chunk(wd_next, engs=("dve", "act"))
            wd_next += 1
        Sd.barrier()
        if dbg and l == 0:
            Sd.dma(lambda e: e.dma_start(out=dbg_d["xa"], in_=xa_d), "dbg2")
            Sd.barrier()

        A.reset(p4_mark)
        wu, wu_b = A.alloc("wu", 8 * DFF, BF16)
        wu3 = wu.rearrange("p (k n) -> p k n", k=8)
        m4 = A.mark()
        stg4 = [A.alloc("stg4b", 2048, F32) for _ in range(3)]
        ci = 0
        for kc in range(8):
            for c in range(2):
                sg, sg_b = stg4[ci % 3]
                Sd.dma(lambda e, sg=sg, kc=kc, c=c: e.dma_start(out=sg, in_=w_up_d[l][kc * 128:(kc + 1) * 128, c * 2048:(c + 1) * 2048]),
                       sg_b.name, writes=[sg_b])
                if ci % 2 == 0:
                    Sd.op("dve", lambda e, sg=sg, kc=kc, c=c: e.tensor_scalar(out=wu3[:, kc, c * 2048:(c + 1) * 2048], in0=sg, scalar1=gin2[:, kc:kc + 1],
                                                                             scalar2=None, op0=ALU.mult), reads=[sg_b, gin2_b], writes=[wu_b])
                else:
                    Sd.op("act", lambda e, sg=sg, kc=kc, c=c: e.activation(out=wu3[:, kc, c * 2048:(c + 1) * 2048], in_=sg, func=AF.Copy,
                                                                          scale=gin2[:, kc:kc + 1]), reads=[sg_b, gin2_b], writes=[wu_b])
                ci += 1
        Sd.barrier()
        A.reset(m4)
        TB = 4
        NMB = NT // TB
        uT, uT_b = A.alloc("uT", 32 * 512, BF16)
        uT3 = uT.rearrange("p (f t) -> p f t", f=32)
        uR = [A.alloc("uR", 512, BF16) for _ in range(3)]
        hblk, hblk_b = A.alloc("hblk", TB * 1024, BF16)
        hb4 = hblk.rearrange("p (t k x) -> p t k x", t=TB, k=8)
        x4 = [A.alloc("x4", D, F32) for _ in range(4)]
        o4 = [A.alloc("o4", D, F32) for _ in range(2)]
        s4 = [A.alloc("s4", 8, F32) for _ in range(2)]
        junk4s = [A.alloc("junk4", 512, BF16) for _ in range(3)]
        jc4 = [0]
        uTb = [Buf(f"uTpart{j}") for j in range(32)]

        def p4_load_h(b):
            Sd.dma(lambda e: e.dma_start(out=hblk.rearrange("p (t x) -> p t x", t=TB),
                                         in_=h2T_d[TB * b:TB * (b + 1)].rearrange("t p x -> p t x")), hblk_b.name, writes=[hblk_b])

        def p4_load_x(ti):
            x_, x_b = x4[ti % 4]
            Sd.dma(lambda e: e.dma_start(out=x_, in_=xa_d[ti * 128:(ti + 1) * 128, :]), x_b.name, writes=[x_b])

        p4_load_h(0)
        for ti in range(min(4, NT)):
            p4_load_x(ti)
        for b in range(NMB):
            for fc in range(32):
                pt, pb_ = pbank[fc % 2]

                def mm(e, pt=pt, fc=fc):
                    ins = None
                    for kc in range(8):
                        ins = e.matmul(pt[:, 0:512].rearrange("p (t x) -> p t x", t=TB), lhsT=wu3[:, kc, fc * 128:(fc + 1) * 128],
                                       rhs=hb4[:, :, kc, :], start=(kc == 0), stop=(kc == 7))
                    return ins
                Sd.op("pe", mm, reads=[wu_b, hblk_b], writes=[pb_])
                ur, ur_b = uR[fc % 3]
                Sd.op("dve", lambda e, pt=pt, ur=ur: e.tensor_scalar(out=ur, in0=pt[:, 0:512], scalar1=0.0, scalar2=None, op0=ALU.max),
                      reads=[pb_], writes=[ur_b])
                Sd.op("pool", lambda e, ur=ur, fc=fc: e.tensor_tensor(out=uT3[:, fc, :], in0=ur, in1=ur, op=ALU.mult),
                      reads=[ur_b], writes=[uTb[fc]])
            if b + 1 < NMB:
                p4_load_h(b + 1)
            for t in range(TB):
                ti = b * TB + t
                k = ti % 2
                x_, x_b = x4[ti % 4]
                o_, o_b = o4[k]
                s_, s_b = s4[k]
                for hf in range(2):
                    pt, pb_ = pbank[2 + k * 2 + hf]

                    def mm(e, pt=pt, hf=hf, t=t):
                        ins = None
                        for fc in range(32):
                            ins = e.matmul(pt[:, 0:512], lhsT=uT3[:, fc, t * 128:(t + 1) * 128], rhs=wd3[:, fc, hf * 512:(hf + 1) * 512],
                                           start=(fc == 0), stop=(fc == 31))
                        return ins
                    Sd.op("pe", mm, reads=uTb + [wd_b], writes=[pb_])
                    jc4[0] += 1
                    junk4, junk4_b = junk4s[jc4[0] % 3]
                    Sd.op("act", lambda e, pt=pt, s_=s_, hf=hf, junk4=junk4: e.activation(out=junk4, in_=pt[:, 0:512], func=AF.Square, accum_out=s_[:, hf:hf + 1]),
                          reads=[pb_], writes=[junk4_b, s_b])
                Sd.op("dve", lambda e, s_=s_: e.tensor_tensor(out=s_[:, 2:3], in0=s_[:, 0:1], in1=s_[:, 1:2], op=ALU.add), reads=[s_b], writes=[s_b])
                rstd_ops(s_[:, 2:3], s_[:, 3:4], D, s_b)
                for hf in range(2):
                    pt, pb_ = pbank[2 + k * 2 + hf]
                    Sd.op("dve", lambda e, pt=pt, o_=o_, s_=s_, hf=hf: e.scalar_tensor_tensor(
                        out=o_[:, hf * 512:(hf + 1) * 512], in0=pt[:, 0:512], scalar=s_[:, 3:4], in1=gpl[:, hf * 512:(hf + 1) * 512],
                        op0=ALU.mult, op1=ALU.mult), reads=[pb_, s_b, gpl_b], writes=[o_b])
                Sd.op("pool", lambda e, o_=o_, x_=x_: e.tensor_tensor(out=o_, in0=o_, in1=x_, op=ALU.add), reads=[o_b, x_b], writes=[o_b])
                Sd.dma(lambda e, o_=o_, ti=ti: e.dma_start(out=x_dst[ti * 128:(ti + 1) * 128, :], in_=o_), o_b.name + "s", reads=[o_b])
                if ti + 4 < NT:
                    p4_load_x(ti + 4)
        Sd.barrier()

    one_t, one_b = A.alloc("one", 1, F32)
    Sd.op("pool", lambda e: e.memset(one_t, 1.0), writes=[one_b])
    base_mark = A.mark()
    base_names = dict(A.namecnt)
    src = x_d
    for l in range(L):
        dst = out_d if l == L - 1 else xb_d
        layer(l, src, dst)
        src = dst
    Sd.barrier()
    Sd.finalize(nc, es)
    es.close()
    stats = dict(n_ops={e: len(Sd.ops[e]) for e in Sd.ENGS}, n_sems=len(Sd.sems), n_waits=Sd.n_waits, arena_peak=A.peak)
    return nc, stats


def host_consts(S):
    bf = ml_dtypes.bfloat16
    c = {}
    c["c_ident"] = np.eye(128, dtype=np.float32).astype(bf)
    pos = np.arange(S, dtype=np.float32)
    inv_freq = (np.float32(500000.0) ** (-np.arange(0, 16, 2, dtype=np.float32) / np.float32(16))).astype(np.float32)
    ang = (pos[:, None] * inv_freq[None, :]).astype(np.float32)
    cs, sn = np.cos(ang).astype(np.float32), np.sin(ang).astype(np.float32)
    C = np.ones((128, S), np.float32)
    Sg = np.zeros((128, S), np.float32)
    for n in range(2):
        for i in range(8):
            C[n * 64 + i] = cs[:, i]
            C[n * 64 + 8 + i] = cs[:, i]
            Sg[n * 64 + i] = -sn[:, i]
            Sg[n * 64 + 8 + i] = sn[:, i]
    c["c_ropeC"] = C
    c["c_ropeS"] = Sg
    p = np.arange(128)[:, None]
    j = np.arange(128)[None, :]
    c["c_mask_cc"] = np.where((p >= 64) & (j < 64), 0.0, 1.0).astype(np.float32).astype(bf)
    c["c_mask_tri"] = (p < j).astype(np.float32).astype(bf)
    c["c_negU"] = np.where(p >= j, -1.0, 0.0).astype(np.float32).astype(bf)
    c["c_negOnes"] = np.full((128, 128), -1.0, np.float32).astype(bf)
    m0 = np.where((p >= 64) & (j < 64), -30000.0, 0.0).astype(np.float32)
    m4 = np.where((p < 64) & (j >= 64), -30000.0, 0.0).astype(np.float32)
    c["c_maskB"] = np.concatenate([m0, m4], axis=1)
    c["c_antiI"] = np.eye(128, dtype=np.float32)[::-1].copy()
    return c


_CACHE = {}

PARAM_NAMES = ["w_in", "w_out", "w_up", "w_down", "norm_pre_mix", "norm_post_mix", "norm_pre_mlp", "norm_post_mlp",
               "lam_q1", "lam_k1", "lam_q2", "lam_k2", "subln_a", "rel_bias", "gn_b", "gn_c"]


def kernel(**inputs):
    x = np.ascontiguousarray(np.asarray(inputs["x"], dtype=np.float32))
    B, S, _ = x.shape
    L = int(np.asarray(inputs["w_in"]).shape[0])
    key = (S, L)
    if key not in _CACHE:
        _CACHE[key] = build(S, L)[0]
    nc = _CACHE[key]
    consts = host_consts(S)
    shared = {n: np.ascontiguousarray(np.asarray(inputs[n], dtype=np.float32)) for n in PARAM_NAMES}
    shared.update(consts)
    n_cores = 8
    in_maps = []
    for c in range(n_cores):
        m = dict(shared)
        m["x"] = x[c % B]
        in_maps.append(m)
    res = run_bass_kernel_spmd(nc, in_maps, core_ids=list(range(n_cores)))
    out = np.stack([np.asarray(res.results[b]["out"], dtype=np.float32) for b in range(B)], axis=0)
    return out
```

```python
import math
import types
import numpy as np
import ml_dtypes
import concourse.bass as bass
import concourse.mybir as mybir
from concourse.ap import AP
from concourse.bass_utils import run_bass_kernel_spmd
from contextlib import ExitStack

F32 = mybir.dt.float32
BF16 = mybir.dt.bfloat16
U8 = mybir.dt.uint8
ALU = mybir.AluOpType
AF = mybir.ActivationFunctionType
AX = mybir.AxisListType

D = 1024
DFF = 4096
DIN = 3072
EPS = 1e-6
SAME_ENG_RAW_SYNC = True


class Buf:
    __slots__ = ("name", "w", "r")

    def __init__(self, name):
        self.name = name
        self.w = {}
        self.r = {}


class Op:
    __slots__ = ("eng", "fn", "deps", "signal", "token", "is_dma", "dkey", "extra_waits")

    def __init__(self, eng, fn, is_dma=False, dkey=None):
        self.eng = eng
        self.fn = fn
        self.deps = set()
        self.signal = False
        self.token = None
        self.is_dma = is_dma
        self.dkey = dkey
        self.extra_waits = []


def _freeze(fn):
    if fn.__closure__ is None:
        return fn
    cells = []
    for c in fn.__closure__:
        try:
            cells.append(types.CellType(c.cell_contents))
        except ValueError:
            cells.append(c)
    return types.FunctionType(fn.__code__, fn.__globals__, fn.__name__, fn.__defaults__, tuple(cells))


class Sched:
    ENGS = ("sp", "act", "dve", "pool", "pe")

    def __init__(self):
        self.ops = {e: [] for e in self.ENGS}
        self.all = []
        self.last_op = {e: None for e in self.ENGS}
        self.dma_keys = {}
        self.dma_last = {}
        self.pending_barrier = {e: [] for e in self.ENGS}

    def _deps(self, op, reads, writes):
        for b in reads:
            for e, w in b.w.items():
                if w is op:
                    continue
                if w.is_dma or e != op.eng or op.is_dma or (SAME_ENG_RAW_SYNC and e != "pe"):
                    op.deps.add(w)
        for b in writes:
            for e, w in b.w.items():
                if w is op:
                    continue
                if w.is_dma and op.is_dma and w.dkey == op.dkey and w.eng == op.eng:
                    op.deps |= w.deps
                    continue
                if w.is_dma or e != op.eng or op.is_dma or e != "pe":
                    op.deps.add(w)
            for e, r in b.r.items():
                if r is op:
                    continue
                if r.is_dma or e != op.eng or op.is_dma or e != "pe":
                    op.deps.add(r)
        for b in reads:
            b.r[op.eng if not op.is_dma else ("dma", id(op))] = op
        for b in writes:
            if op.is_dma:
                prev = {k: v for k, v in b.w.items() if v.is_dma and v.dkey == op.dkey}
                b.w = prev
                b.w[("dma", id(op))] = op
            else:
                b.w = {op.eng: op}
            b.r = {}

    def op(self, eng, fn, reads=(), writes=()):
        o = Op(eng, _freeze(fn))
        self._deps(o, reads, writes)
        o.deps |= set(self.pending_barrier[eng])
        self.pending_barrier[eng] = []
        self.ops[eng].append(o)
        self.all.append(o)
        self.last_op[eng] = o
        return o

    def dma(self, fn, key, reads=(), writes=(), eng="sp"):
        o = Op(eng, _freeze(fn), is_dma=True, dkey=key)
        self._deps(o, reads, writes)
        o.deps |= set(self.pending_barrier[eng])
        self.pending_barrier[eng] = []
        self.dma_keys[key] = self.dma_keys.get(key, 0) + 1
        o.token = ("dma:" + key, 16 * self.dma_keys[key])
        o.signal = True
        self.dma_last[key] = o
        self.ops[eng].append(o)
        self.all.append(o)
        return o

    def barrier(self):
        toks = [o for o in self.last_op.values() if o is not None]
        toks += list(self.dma_last.values())
        for e in self.ENGS:
            self.pending_barrier[e] = list(toks)

    def finalize(self, nc, es):
        for o in self.all:
            for d in o.deps:
                d.signal = True
        for e in self.ENGS:
            for d in self.pending_barrier[e]:
                d.signal = True
        cnt = {e: 0 for e in self.ENGS}
        for e in self.ENGS:
            for o in self.ops[e]:
                if o.is_dma:
                    continue
                if o.signal:
                    cnt[e] += 1
                    o.token = ("eng:" + e, cnt[e])
        names = set()
        for o in self.all:
            if o.signal:
                names.add(o.token[0])
        self.sems = {}
        for n in sorted(names):
            self.sems[n] = es.enter_context(nc.semaphore(n.replace(":", "_")))
        final_waits = {}
        self.n_waits = 0
        for e in self.ENGS:
            waited = {}
            for o in self.ops[e]:
                need = {}
                for d in o.deps:
                    k, v = d.token
                    if need.get(k, 0) < v:
                        need[k] = v
                o.extra_waits = []
                for k, v in need.items():
                    if waited.get(k, 0) < v:
                        waited[k] = v
                        o.extra_waits.append((k, v))
                        self.n_waits += 1
            need = {}
            for d in self.pending_barrier[e]:
                k, v = d.token
                if need.get(k, 0) < v:
                    need[k] = v
            final_waits[e] = [(k, v) for k, v in need.items() if waited.get(k, 0) < v]
        block = es.enter_context(nc.Block())
        sems = self.sems

        def make(ename):
            ops = self.ops[ename]
            fw = final_waits[ename]

            def run(eng):
                for o in ops:
                    for k, v in o.extra_waits:
                        eng.wait_ge(sems[k], v)
                    ins = o.fn(eng)
                    if o.signal:
                        k, v = o.token
                        ins.then_inc(sems[k], 16 if o.is_dma else 1)
                for k, v in fw:
                    eng.wait_ge(sems[k], v)
            return run

        block.sync(make("sp"))
        block.scalar(make("act"))
        block.vector(make("dve"))
        block.gpsimd(make("pool"))
        block.tensor(make("pe"))


class Arena:
    def __init__(self, tensor_ap, nbytes):
        self.t = tensor_ap
        self.n = nbytes
        self.off = 0
        self.namecnt = {}
        self.peak = 0

    def mark(self):
        return self.off

    def reset(self, m):
        self.off = m

    def alloc(self, name, free_elems, dtype):
        esz = 4 if dtype == F32 else 2
        nb = free_elems * esz
        nb_al = (nb + 63) // 64 * 64
        assert self.off + nb_al <= self.n, f"arena overflow {name}: {self.off}+{nb_al}>{self.n}"
        v = self.t[:, self.off:self.off + nb].bitcast(dtype)
        self.off += nb_al
        self.peak = max(self.peak, self.off)
        idx = self.namecnt.get(name, 0)
        self.namecnt[name] = idx + 1
        return v, Buf(f"{name}_{idx}")


def bc_mid(ap2, m):
    a = ap2.ap
    return AP(ap2.tensor, ap2.offset, [list(a[0]), [0, m], list(a[1])])


def bc_last(ap2, n):
    a = ap2.ap
    return AP(ap2.tensor, ap2.offset, [list(a[0]), list(a[1]), [0, n]])


def build(S=4096, L=2, dbg=False):
    NT = S // 128
    NQ = S // 256
    NB = S // 512
    NM = S // 256
    nc = bass.Bass("TRN2", target_bir_lowering=False)

    def din(name, shape, dt=F32):
        return nc.dram_tensor(name, shape, dt, kind="ExternalInput").ap()

    def dscr(name, shape, dt):
        return nc.dram_tensor(name, shape, dt, kind="Internal").ap()

    x_d = din("x", [S, D])
    w_in_d = din("w_in", [L, D, DIN])
    w_out_d = din("w_out", [L, D, D])
    w_up_d = din("w_up", [L, D, DFF])
    w_down_d = din("w_down", [L, DFF, D])
    g_pre_mix_d = din("norm_pre_mix", [L, D])
    g_post_mix_d = din("norm_post_mix", [L, D])
    g_pre_mlp_d = din("norm_pre_mlp", [L, D])
    g_post_mlp_d = din("norm_post_mlp", [L, D])
    lam_d = [din(n, [L, 64]) for n in ("lam_q1", "lam_k1", "lam_q2", "lam_k2")]
    subln_d = din("subln_a", [L, 128])
    relb_d = din("rel_bias", [L, 4, 257])
    gnb_d = din("gn_b", [L, 256])
    gnc_d = din("gn_c", [L, 256])
    ident_d = din("c_ident", [128, 128], BF16)
    ropeC_d = din("c_ropeC", [128, S])
    ropeS_d = din("c_ropeS", [128, S])
    mcc_d = din("c_mask_cc", [128, 128], BF16)
    mtri_d = din("c_mask_tri", [128, 128], BF16)
    negU_d = din("c_negU", [128, 128], BF16)
    negOnes_d = din("c_negOnes", [128, 128], BF16)
    mB_d = din("c_maskB", [128, 2 * 128])
    antiI_d = din("c_antiI", [128, 128])
    out_d = nc.dram_tensor("out", [S, D], F32, kind="ExternalOutput").ap()
    ocat_d = dscr("ocat", [S, D], BF16)
    xa_d = dscr("xa", [S, D], F32)
    xb_d = dscr("xb", [S, D], F32)
    h2T_d = dscr("h2T", [NT, 128, 8 * 128], BF16)
    E_d = dscr("Eext", [4, 384], F32)
    dbg_d = {}
    if dbg:
        dbg_d["ocat"] = nc.dram_tensor("dbg_ocat", [S, D], BF16, kind="ExternalOutput").ap()
        dbg_d["xa"] = nc.dram_tensor("dbg_xa", [S, D], F32, kind="ExternalOutput").ap()

    Sd = Sched()
    es = ExitStack()
    ARENA_BYTES = 206 * 1024
    arena_t = es.enter_context(nc.sbuf_tensor("arena", [128, ARENA_BYTES], U8))
    A = Arena(arena_t, ARENA_BYTES)
    pbank = []
    for i in range(8):
        t = es.enter_context(nc.psum_tensor(f"pb{i}", [128, 512], F32))
        pbank.append((t, Buf(f"pb{i}")))

    def pbf(i):
        return pbank[i][0][:, 0:512].bitcast(BF16)

    def dump(name, ap, buf):
        if not dbg:
            return
        t = nc.dram_tensor("dbg_" + name, list(ap.shape), ap.dtype, kind="ExternalOutput").ap()
        Sd.dma(lambda e: e.dma_start(out=t, in_=ap), "dbgdump_" + name, reads=[buf])

    ident, ident_b = A.alloc("ident", 128, BF16)
    mcc, mcc_b = A.alloc("mcc", 128, BF16)
    mtri, mtri_b = A.alloc("mtri", 128, BF16)
    negU, negU_b = A.alloc("negU", 128, BF16)
    negOnes, negOnes_b = A.alloc("negOnes", 128, BF16)
    for (t, b, d) in ((ident, ident_b, ident_d), (mcc, mcc_b, mcc_d), (mtri, mtri_b, mtri_d),
                      (negU, negU_b, negU_d), (negOnes, negOnes_b, negOnes_d)):
        Sd.dma(lambda e, t=t, d=d: e.dma_start(out=t, in_=d), b.name, writes=[b])

    def rstd_ops(ssq, out, n, rb):
        Sd.op("act", lambda e: e.activation(out=out, in_=ssq, func=AF.Ln, scale=1.0 / n, bias=eps_t[:, 0:1]),
              reads=[rb, eps_b], writes=[rb])
        Sd.op("act", lambda e: e.activation(out=out, in_=out, func=AF.Exp, scale=-0.5), reads=[rb], writes=[rb])

    eps_t, eps_b = A.alloc("eps", 1, F32)
    Sd.op("pool", lambda e: e.memset(eps_t, EPS), writes=[eps_b])

    base_mark = A.mark()

    def layer(l, x_src, x_dst):
        lam_init = 0.8 - 0.6 * math.exp(-0.3 * l)
        A.reset(base_mark)
        A.namecnt = dict(base_names)
        hT, hT_b = A.alloc("hT", 8 * S, BF16)
        hT3 = hT.rearrange("p (k s) -> p k s", k=8)
        p12_mark = A.mark()

        xin = [A.alloc("xin", D, F32) for _ in range(4)]
        hb = [A.alloc("hb", D, BF16) for _ in range(3)]
        junks = [A.alloc("junk", D, BF16) for _ in range(4)]
        st = [A.alloc("st", 4, F32) for _ in range(4)]

        def p1_load(i):
            xt, xt_b = xin[i % 4]
            Sd.dma(lambda e: e.dma_start(out=xt, in_=x_src[i * 128:(i + 1) * 128, :]), xt_b.name, writes=[xt_b])

        def p1_sq(i):
            xt, xt_b = xin[i % 4]
            s_, s_b = st[i % 4]
            junk, junk_b = junks[i % 4]
            Sd.op("act", lambda e: e.activation(out=junk, in_=xt, func=AF.Square, accum_out=s_[:, 0:1]),
                  reads=[xt_b], writes=[junk_b, s_b])
            rstd_ops(s_[:, 0:1], s_[:, 1:2], D, s_b)

        def p1_scale(i):
            xt, xt_b = xin[i % 4]
            ht, ht_b = hb[i % 3]
            s_, s_b = st[i % 4]
            Sd.op("dve", lambda e: e.tensor_scalar(out=ht, in0=xt, scalar1=s_[:, 1:2], scalar2=None, op0=ALU.mult),
                  reads=[xt_b, s_b], writes=[ht_b])

        def p1_tr(i):
            ht, ht_b = hb[i % 3]
            pv = pbf(i % 2)

            def tr(e):
                ins = None
                for kc in range(8):
                    ins = e.transpose(out=pv[:, kc * 128:(kc + 1) * 128], in_=ht[:, kc * 128:(kc + 1) * 128], identity=ident)
                return ins
            Sd.op("pe", tr, reads=[ht_b, ident_b], writes=[pbank[i % 2][1]])

        def p1_evac(i):
            pv = pbf(i % 2)
            if i % 2:
                Sd.op("dve", lambda e: e.tensor_copy(out=hT3[:, :, i * 128:(i + 1) * 128], in_=pv.rearrange("p (k t) -> p k t", k=8)),
                      reads=[pbank[i % 2][1]], writes=[hT_b])
            else:
                Sd.op("act", lambda e: e.activation(out=hT3[:, :, i * 128:(i + 1) * 128], in_=pv.rearrange("p (k t) -> p k t", k=8), func=AF.Copy),
                      reads=[pbank[i % 2][1]], writes=[hT_b])

        for i in range(min(2, NT)):
            p1_load(i)
        for j in range(NT + 3):
            if 0 <= j - 3 < NT:
                p1_evac(j - 3)
            if j < NT:
                p1_sq(j)
            if 0 <= j - 2 < NT:
                p1_tr(j - 2)
            if 0 <= j - 1 < NT:
                p1_scale(j - 1)
            if j + 2 < NT:
                p1_load(j + 2)
        Sd.barrier()
        A.reset(p12_mark)

        gin, gin_b = A.alloc("gin", 8, F32)
        Sd.dma(lambda e: e.dma_start(out=gin, in_=g_pre_mix_d[l].rearrange("(k p) -> p k", p=128), allow_slow_non_contiguous=True),
               gin_b.name, writes=[gin_b])
        stage = [A.alloc("stage", 8 * 384, F32) for _ in range(2)]
        wb = [A.alloc("wb", 8 * 384, BF16) for _ in range(2)]
        wslot = [0]

        def load_w(col_groups):
            k = wslot[0] % 2
            wslot[0] += 1
            ntot = sum(n for _, n in col_groups)
            sg, sg_b = stage[k]
            w_, w_b = wb[k]
            sv = sg[:, 0:8 * ntot].rearrange("p (k n) -> p k n", k=8)
            wv = w_[:, 0:8 * ntot].rearrange("p (k n) -> p k n", k=8)
            o = 0
            for (c0, n) in col_groups:
                Sd.dma(lambda e, sv=sv, o=o, n=n, c0=c0: e.dma_start(
                    out=sv[:, :, o:o + n], in_=w_in_d[l][:, c0:c0 + n].rearrange("(k p) n -> p k n", p=128)),
                    sg_b.name, writes=[sg_b])
                o += n
            Sd.op("dve", lambda e, sv=sv, wv=wv, ntot=ntot: e.tensor_tensor(out=wv, in0=sv, in1=bc_last(gin, ntot), op=ALU.mult),
                  reads=[sg_b, gin_b], writes=[w_b])
            return wv, w_b

        ostg = [A.alloc("ostg", 256, BF16) for _ in range(4)]
        ostg_i = [0]
        ftmp = [A.alloc("ftmp", 256, F32) for _ in range(4)]
        fsm = [A.alloc("fsm", 16, F32) for _ in range(4)]
        fin_i = [0]

        def proj_fm(wv, w_b, c0, dstT, dst_b, tb, pi, eng, scale=None, rows=None):
            pt, pb_ = pbank[pi]

            def mm(e):
                ins = None
                for kc in range(8):
                    ins = e.matmul(pt[:, 0:512], lhsT=wv[:, kc, c0:c0 + 128], rhs=hT3[:, kc, tb * 512:(tb + 1) * 512],
                                   start=(kc == 0), stop=(kc == 7))
                return ins
            Sd.op("pe", mm, reads=[w_b, hT_b], writes=[pb_])
            return pt, pb_

        def evac_copy(eng, out, in_, reads, writes, scale=None):
            if eng == "act":
                if scale is None:
                    Sd.op("act", lambda e: e.activation(out=out, in_=in_, func=AF.Copy), reads=reads, writes=writes)
                else:
                    Sd.op("act", lambda e: e.activation(out=out, in_=in_, func=AF.Copy, scale=scale), reads=reads, writes=writes)
            else:
                if scale is None:
                    Sd.op("dve", lambda e: e.tensor_copy(out=out, in_=in_), reads=reads, writes=writes)
                else:
                    Sd.op("dve", lambda e: e.tensor_scalar(out=out, in0=in_, scalar1=scale, scalar2=None, op0=ALU.mult),
                          reads=reads, writes=writes)

        fin_pending = []

        def fin_submit(stages, k=None):
            for stl in list(fin_pending):
                if stl[0] == k:
                    for f in stl[1:]:
                        f()
                    fin_pending.remove(stl)
            fin_pending.append([k] + list(stages))

        def tick():
            for stl in list(fin_pending):
                f = stl.pop(1)
                f()
                if len(stl) == 1:
                    fin_pending.remove(stl)

        def fin_flush():
            while fin_pending:
                tick()

        def fin_slot():
            k = fin_i[0] % 4
            fin_i[0] += 1
            return k

        def norm_stages(k, o32, o32_b, nh, hd, gtile, g_b, tile_i, col0):
            sm, sm_b = fsm[k]
            sq, sq_b = fsqs[k]
            og, og_b = ostg[k]
            w = nh * hd
            o3 = o32[:, 0:w].rearrange("p (h d) -> p h d", h=nh)

            def s_sq():
                if nh == 1:
                    Sd.op("act", lambda e: e.activation(out=sq[:, 0:w], in_=o32[:, 0:w], func=AF.Square, accum_out=sm[:, 0:1]),
                          reads=[o32_b], writes=[sq_b, sm_b])
                else:
                    Sd.op("pool", lambda e: e.tensor_tensor(out=sq[:, 0:w], in0=o32[:, 0:w], in1=o32[:, 0:w], op=ALU.mult),
                          reads=[o32_b], writes=[sq_b])

            def s_red():
                if nh > 1:
                    Sd.op("dve", lambda e: e.tensor_reduce(out=sm[:, 0:nh], in_=sq[:, 0:w].rearrange("p (h d) -> p h d", h=nh), axis=AX.X, op=ALU.add),
                          reads=[sq_b], writes=[sm_b])

            def s_ln():
                Sd.op("act", lambda e: e.activation(out=sm[:, 4:4 + nh], in_=sm[:, 0:nh], func=AF.Ln, scale=1.0 / hd, bias=eps_t[:, 0:1]),
                      reads=[sm_b, eps_b], writes=[sm_b])

            def s_exp():
                Sd.op("act", lambda e: e.activation(out=sm[:, 4:4 + nh], in_=sm[:, 4:4 + nh], func=AF.Exp, scale=-0.5), reads=[sm_b], writes=[sm_b])

            def s_scale():
                if nh == 1:
                    Sd.op("dve", lambda e: e.scalar_tensor_tensor(out=og[:, 0:w], in0=o32[:, 0:w], scalar=sm[:, 4:5], in1=gtile[:, 0:w],
                                                                 op0=ALU.mult, op1=ALU.mult), reads=[o32_b, sm_b, g_b], writes=[og_b])
                else:
                    Sd.op("dve", lambda e: e.tensor_tensor(out=o3, in0=o3, in1=bc_last(sm[:, 4:4 + nh], hd), op=ALU.mult),
                          reads=[o32_b, sm_b], writes=[o32_b])
                    Sd.op("dve", lambda e: e.tensor_tensor(out=og[:, 0:w], in0=o32[:, 0:w], in1=gtile[:, 0:w], op=ALU.mult),
                          reads=[o32_b, g_b], writes=[og_b])

            def s_store():
                Sd.dma(lambda e: e.dma_start(out=ocat_d[tile_i * 128:(tile_i + 1) * 128, col0:col0 + w], in_=og[:, 0:w]),
                       og_b.name, reads=[og_b])
            return [s_sq, s_red, s_ln, s_exp, s_scale, s_store]

        fsqs = [A.alloc("fsq", 256, F32) for _ in range(4)]
        p2_mark = A.mark()

        lamt, lamt_b = A.alloc("lamt", 4 * 64, F32)
        lams, lams_b = A.alloc("lams", 8, F32)
        gA, gA_b = A.alloc("gA", 128, F32)
        for j in range(4):
            Sd.dma(lambda e, j=j: e.dma_start(out=lamt[:, j * 64:(j + 1) * 64], in_=lam_d[j][l].partition_broadcast(128)),
                   lamt_b.name, writes=[lamt_b])
        Sd.dma(lambda e: e.dma_start(out=gA, in_=subln_d[l].partition_broadcast(128)), gA_b.name, writes=[gA_b])
        Sd.op("dve", lambda e: e.tensor_tensor(out=lamt[:, 0:64], in0=lamt[:, 0:64], in1=lamt[:, 64:128], op=ALU.mult),
              reads=[lamt_b], writes=[lamt_b])
        Sd.op("dve", lambda e: e.tensor_tensor(out=lamt[:, 128:192], in0=lamt[:, 128:192], in1=lamt[:, 192:256], op=ALU.mult),
              reads=[lamt_b], writes=[lamt_b])
        Sd.op("dve", lambda e: e.tensor_reduce(out=lams[:, 0:1], in_=lamt[:, 0:64], axis=AX.X, op=ALU.add), reads=[lamt_b], writes=[lams_b])
        Sd.op("dve", lambda e: e.tensor_reduce(out=lams[:, 1:2], in_=lamt[:, 128:192], axis=AX.X, op=ALU.add), reads=[lamt_b], writes=[lams_b])
        Sd.op("act", lambda e: e.activation(out=lams[:, 2:4], in_=lams[:, 0:2], func=AF.Exp), reads=[lams_b], writes=[lams_b])
        Sd.op("dve", lambda e: e.tensor_tensor(out=lams[:, 4:5], in0=lams[:, 3:4], in1=lams[:, 2:3], op=ALU.subtract), reads=[lams_b], writes=[lams_b])
        Sd.op("dve", lambda e: e.tensor_scalar(out=lams[:, 4:5], in0=lams[:, 4:5], scalar1=-lam_init, scalar2=None, op0=ALU.add), reads=[lams_b], writes=[lams_b])
        Sd.op("dve", lambda e: e.tensor_scalar(out=gA, in0=gA, scalar1=(1.0 - lam_init), scalar2=None, op0=ALU.mult), reads=[gA_b], writes=[gA_b])

        qT, qT_b = A.alloc("qT", S, BF16)
        kT0, kT0_b = A.alloc("kT0", S, BF16)
        kT1, kT1_b = A.alloc("kT1", S, BF16)
        vA, vA_b = A.alloc("vA", NT * 129, BF16)
        vA3 = vA.rearrange("p (t e) -> p t e", t=NT)
        wP, wP_b = A.alloc("wP", 8 * 256, BF16)
        wP3 = wP.rearrange("p (k n) -> p k n", k=8)
        ropeC = [A.alloc("ropeC", 512, F32) for _ in range(2)]
        ropeS = [A.alloc("ropeS", 512, F32) for _ in range(2)]
        rt1 = [A.alloc("rt1", 512, F32) for _ in range(2)]
        rt2 = [A.alloc("rt2", 512, F32) for _ in range(2)]
        PT = [A.alloc("PT", 512, BF16) for _ in range(4)]
        Sd.op("pool", lambda e: e.memset(kT0[64:128, :], 0.0), writes=[kT0_b])
        Sd.op("pool", lambda e: e.memset(kT1[0:64, :], 0.0), writes=[kT1_b])
        Sd.op("dve", lambda e: e.memset(wP, 0.0), writes=[wP_b])
        Sd.op("pool", lambda e: e.memset(vA3[:, :, 128:129], 1.0), writes=[vA_b])

        for h in range(4):
            wv, w_b = load_w([(h * 128, 128), (512 + h * 128, 128), (1024 + h * 128, 128)])
            w4 = wv[:, :, 0:256].rearrange("p k (g d) -> p k g d", g=4)
            wP4 = wP3.rearrange("p k (g d) -> p k g d", g=4)
            Sd.op("dve", lambda e: e.tensor_copy(out=wP4[:, :, :, 0:8], in_=w4[:, :, :, 8:16]), reads=[w_b], writes=[wP_b])
            Sd.op("dve", lambda e: e.tensor_copy(out=wP4[:, :, :, 8:16], in_=w4[:, :, :, 0:8]), reads=[w_b], writes=[wP_b])
            for tb in range(NB):
                rc, rc_b = ropeC[tb % 2]
                rs, rs_b = ropeS[tb % 2]
                Sd.dma(lambda e, rc=rc, tb=tb: e.dma_start(out=rc, in_=ropeC_d[:, tb * 512:(tb + 1) * 512]), rc_b.name, writes=[rc_b])
                Sd.dma(lambda e, rs=rs, tb=tb: e.dma_start(out=rs, in_=ropeS_d[:, tb * 512:(tb + 1) * 512]), rs_b.name, writes=[rs_b])
                for qk in range(2):
                    p1, p1_b = proj_fm(wv, w_b, qk * 128, None, None, tb, 6, None)
                    p2, p2_b = proj_fm(wP3, wP_b, qk * 128, None, None, tb, 7, None)
                    t1, t1_b = rt1[qk]
                    t2, t2_b = rt2[qk]
                    Sd.op("dve", lambda e, t1=t1, p1=p1, rc=rc: e.tensor_tensor(out=t1, in0=p1[:, 0:512], in1=rc, op=ALU.mult),
                          reads=[p1_b, rc_b], writes=[t1_b])
                    Sd.op("dve", lambda e, t2=t2, p2=p2, rs=rs: e.tensor_tensor(out=t2, in0=p2[:, 0:512], in1=rs, op=ALU.mult),
                          reads=[p2_b, rs_b], writes=[t2_b])
                    sl = slice(tb * 512, (tb + 1) * 512)
                    if qk == 0:
                        Sd.op("dve", lambda e, t1=t1, t2=t2, sl=sl: e.tensor_tensor(out=qT[:, sl], in0=t1, in1=t2, op=ALU.add),
                              reads=[t1_b, t2_b], writes=[qT_b])
                    else:
                        Sd.op("dve", lambda e, t1=t1, t2=t2, sl=sl: e.tensor_tensor(out=kT0[0:64, sl], in0=t1[0:64, :], in1=t2[0:64, :], op=ALU.add),
                              reads=[t1_b, t2_b], writes=[kT0_b])
                        Sd.op("dve", lambda e, t1=t1, t2=t2, sl=sl: e.tensor_tensor(out=kT1[64:128, sl], in0=t1[64:128, :], in1=t2[64:128, :], op=ALU.add),
                              reads=[t1_b, t2_b], writes=[kT1_b])
                pt, pb_ = pbank[6 + (tb % 2)]

                def mmv(e, tb=tb, pt=pt, wv=wv):
                    ins = None
                    for j in range(4):
                        ti = tb * 4 + j
                        for kc in range(8):
                            ins = e.matmul(pt[:, j * 128:(j + 1) * 128], lhsT=hT3[:, kc, ti * 128:(ti + 1) * 128], rhs=wv[:, kc, 256:384],
                                           start=(kc == 0), stop=(kc == 7))
                    return ins
                Sd.op("pe", mmv, reads=[w_b, hT_b], writes=[pb_])
                Sd.op("act", lambda e, tb=tb, pt=pt: e.activation(out=vA3[:, tb * 4:(tb + 1) * 4, 0:128],
                                                                 in_=pt[:, 0:512].rearrange("p (j e) -> p j e", j=4), func=AF.Copy),
                      reads=[pb_], writes=[vA_b])

            if l == 0 and h in (0, 1, 2):
                dump(f"qT{h}", qT, qT_b)
                dump(f"kT0_{h}", kT0, kT0_b)
                dump(f"kT1_{h}", kT1, kT1_b)
                dump(f"vA{h}", vA, vA_b)
                dump(f"wP{h}", wP, wP_b)
            steps = []
            for Q in range(NQ):
                for kt in range(2 * Q + 2):
                    steps.append((Q, kt))

            def emit_S(si):
                Q, kt = steps[si]
                pt, pb_ = pbank[(0, 1, 6)[si % 3]]

                def mm(e, Q=Q, kt=kt, pt=pt):
                    e.matmul(pt[:, 0:256], lhsT=kT0[:, kt * 128:(kt + 1) * 128], rhs=qT[:, Q * 256:(Q + 1) * 256], start=True, stop=True)
                    return e.matmul(pt[:, 256:512], lhsT=kT1[:, kt * 128:(kt + 1) * 128], rhs=qT[:, Q * 256:(Q + 1) * 256], start=True, stop=True)
                Sd.op("pe", mm, reads=[kT0_b, kT1_b, qT_b], writes=[pb_])

            def emit_rest(si):
                Q, kt = steps[si]
                r = kt - 2 * Q
                pt, pb_ = pbank[(0, 1, 6)[si % 3]]
                P, P_b = PT[si % 4]
                Sd.op("act", lambda e: e.activation(out=P, in_=pt[:, 0:512], func=AF.Exp, scale=0.125), reads=[pb_], writes=[P_b])
                P3 = P.rearrange("p (n q) -> p n q", n=2)
                if r >= 0:
                    Sd.op("dve", lambda e: e.tensor_tensor(out=P3[:, :, r * 128:(r + 1) * 128], in0=P3[:, :, r * 128:(r + 1) * 128],
                                                           in1=bc_mid(mcc, 2), op=ALU.mult), reads=[P_b, mcc_b], writes=[P_b])
                first = (kt == 0)
                ab = 2 + 2 * (Q % 2)

                def pv(e):
                    ins = None
                    for n in range(2):
                        acc = pbank[ab + n][0]
                        for qs in range(max(r, 0), 2):
                            last = (kt == 2 * Q + qs)
                            ins = e.matmul(acc[:, qs * 129:(qs + 1) * 129], lhsT=P3[:, n, qs * 128:(qs + 1) * 128], rhs=vA3[:, kt, :],
                                           start=(first and qs == 0), stop=last, skip_group_check=True)
                    return ins
                Sd.op("pe", pv, reads=[P_b, vA_b], writes=[pbank[ab][1], pbank[ab + 1][1]])
                for qs in (range(2) if kt == 2 * Q + 1 else ()):
                    ti = Q * 2 + qs
                    k = fin_slot()
                    sm, sm_b = fsm[k]
                    o32, o32_b = ftmp[k]
                    a0, a0_b = pbank[ab]
                    a1, a1_b = pbank[ab + 1]
                    c = qs * 129

                    def s_comb(sm=sm, sm_b=sm_b, o32=o32, o32_b=o32_b, c=c):
                        Sd.op("dve", lambda e: e.reciprocal(out=sm[:, 8:9], in_=a0[:, c + 128:c + 129]), reads=[a0_b], writes=[sm_b])
                        Sd.op("dve", lambda e: e.reciprocal(out=sm[:, 9:10], in_=a1[:, c + 128:c + 129]), reads=[a1_b], writes=[sm_b])
                        Sd.op("dve", lambda e: e.tensor_tensor(out=sm[:, 9:10], in0=sm[:, 9:10], in1=lams[:, 4:5], op=ALU.mult),
                              reads=[sm_b, lams_b], writes=[sm_b])
                        Sd.op("dve", lambda e: e.tensor_scalar(out=o32[:, 0:128], in0=a0[:, c:c + 128], scalar1=sm[:, 8:9], scalar2=None, op0=ALU.mult),
                              reads=[a0_b, sm_b], writes=[o32_b])
                        Sd.op("dve", lambda e: e.scalar_tensor_tensor(out=o32[:, 0:128], in0=a1[:, c:c + 128], scalar=sm[:, 9:10], in1=o32[:, 0:128],
                                                                     op0=ALU.mult, op1=ALU.add), reads=[a1_b, sm_b, o32_b], writes=[o32_b])
                    fin_submit([s_comb] + norm_stages(k, o32, o32_b, 1, 128, gA, gA_b, ti, h * 128), k)

            n = len(steps)
            for si in range(n + 2):
                if si < n:
                    emit_S(si)
                if si >= 2:
                    emit_rest(si - 2)
                    tick()
        fin_flush()
        Sd.barrier()
        A.reset(p2_mark)

        qB, qB_b = A.alloc("qB", 2 * S, BF16)
        qB3 = qB.rearrange("p (f s) -> p f s", f=2)
        kB = [A.alloc("kB", S, BF16) for _ in range(4)]
        vB, vB_b = A.alloc("vB", NT * 4 * 65, BF16)
        vB4 = vB.rearrange("p (t h e) -> p t h e", t=NT, h=4)
        BT, BT_b = A.alloc("BT", 5 * 512, F32)
        BT3 = BT.rearrange("p (d x) -> p d x", d=5)
        Rt, Rt_b = A.alloc("Rt", 5 * 128, F32)
        antiI, antiI_b = A.alloc("antiI", 128, F32)
        mB, mB_b = A.alloc("mB", 256, F32)
        gnb, gnb_b = A.alloc("gnb", 256, F32)
        BTb, BTb_b = A.alloc("BTb", 5 * 512, BF16)
        BTb3 = BTb.rearrange("p (d x) -> p d x", d=5)
        PB = [A.alloc("PB", 512, BF16) for _ in range(3)]
        Sd.dma(lambda e: e.dma_start(out=antiI, in_=antiI_d), antiI_b.name, writes=[antiI_b])
        Sd.dma(lambda e: e.dma_start(out=mB, in_=mB_d), mB_b.name, writes=[mB_b])
        Sd.dma(lambda e: e.dma_start(out=gnb, in_=gnb_d[l].partition_broadcast(128)), gnb_b.name, writes=[gnb_b])
        E_b = Buf("E_dram")
        c4, c4_b = A.alloc("c4", 1, F32)
        ctile, ctile_b = A.alloc("ctile", 128, F32)
        c256, c256_b = A.alloc("c256", 4, F32)
        Sd.dma(lambda e: e.dma_start(out=c4[0:4, 0:1], in_=relb_d[l, :, 256:257], allow_slow_non_contiguous=True), c4_b.name, writes=[c4_b])
        Sd.dma(lambda e: e.dma_start(out=c256.rearrange("p (h o) -> p h o", o=1),
                                     in_=AP(relb_d.tensor, relb_d[l, 0:1, 256:257].offset, [[0, 128], [257, 4], [1, 1]]),
                                     allow_slow_non_contiguous=True),
               c256_b.name, writes=[c256_b])
        Sd.op("dve", lambda e: e.tensor_copy(out=ctile[0:4, :], in_=AP(c4.tensor, c4.offset, [[c4.ap[0][0], 4], [0, 128]])),
              reads=[c4_b], writes=[ctile_b])
        Sd.dma(lambda e: e.dma_start(out=E_d[:, 0:256], in_=relb_d[l, :, 1:257]), "E_dram", writes=[E_b])
        Sd.dma(lambda e: e.dma_start(out=E_d[:, 256:384], in_=ctile[0:4, :]), "E_dram", reads=[ctile_b], writes=[E_b])
        for hh in range(4):
            k_, k_b = kB[hh]
            if hh % 2 == 0:
                Sd.op("pool", lambda e, k_=k_: e.memset(k_[64:128, :], 0.0), writes=[k_b])
            else:
                Sd.op("pool", lambda e, k_=k_: e.memset(k_[0:64, :], 0.0), writes=[k_b])
        Sd.op("pool", lambda e: e.memset(vB4[:, :, :, 64:65], 1.0), writes=[vB_b])
        for hh in range(4):
            Sd.dma(lambda e, hh=hh: e.dma_start(out=Rt[:, 0:256].rearrange("p (d j) -> p d j", d=2),
                                                in_=AP(E_d.tensor, E_d[hh:hh + 1, 0:1].offset, [[1, 128], [128, 2], [1, 128]])),
                   Rt_b.name, reads=[E_b], writes=[Rt_b])
            pt, pb_ = pbank[4 + hh % 2]
            Sd.op("pe", lambda e, pt=pt: e.matmul(pt[:, 0:256], lhsT=antiI, rhs=Rt[:, 0:256], start=True, stop=True),
                  reads=[antiI_b, Rt_b], writes=[pb_])
            Sd.op("dve", lambda e, pt=pt, hh=hh: e.tensor_copy(
                out=BT3[:, 0:2, hh * 128:(hh + 1) * 128], in_=pt[:, 0:256].rearrange("p (d j) -> p d j", d=2)),
                reads=[pb_], writes=[BT_b])
        for d in range(2, 5):
            Sd.op("dve", lambda e, d=d: e.tensor_copy(out=BT3[:, d, :].rearrange("p (h j) -> p h j", h=4), in_=bc_last(c256, 128)),
                  reads=[c256_b], writes=[BT_b])
        Sd.op("dve", lambda e: e.tensor_tensor(out=BT3[:, 0, :].rearrange("p (h j) -> p h j", h=4), in0=BT3[:, 0, :].rearrange("p (h j) -> p h j", h=4),
                                               in1=bc_mid(mB[:, 0:128], 4), op=ALU.add), reads=[BT_b, mB_b], writes=[BT_b])
        Sd.op("dve", lambda e: e.tensor_tensor(out=BT3[:, 4, :].rearrange("p (h j) -> p h j", h=4), in0=BT3[:, 4, :].rearrange("p (h j) -> p h j", h=4),
                                               in1=bc_mid(mB[:, 128:256], 4), op=ALU.add), reads=[BT_b, mB_b], writes=[BT_b])
        Sd.op("dve", lambda e: e.tensor_copy(out=BTb, in_=BT), reads=[BT_b], writes=[BTb_b])
        wq, wq_b = load_w([(1536, 256)])
        for tb in range(NB):
            for ft in range(2):
                pt, pb_ = proj_fm(wq, wq_b, ft * 128, None, None, tb, 4 + ft, None)
                evac_copy("act" if ft else "dve", qB3[:, ft, tb * 512:(tb + 1) * 512], pt[:, 0:512], [pb_], [qB_b])
        wk, wk_b = load_w([(1792, 256)])
        for tb in range(NB):
            for ft in range(2):
                pt, pb_ = proj_fm(wk, wk_b, ft * 128, None, None, tb, 4 + ft, None)
                k0, k0_b = kB[ft * 2]
                k1, k1_b = kB[ft * 2 + 1]
                evac_copy("act", k0[0:64, tb * 512:(tb + 1) * 512], pt[0:64, 0:512], [pb_], [k0_b], scale=0.125)
                evac_copy("dve", k1[64:128, tb * 512:(tb + 1) * 512], pt[64:128, 0:512], [pb_], [k1_b], scale=0.125)
        wvv, wvv_b = load_w([(2048, 256)])
        for tp in range(NT // 2):
            pt, pb_ = pbank[4 + tp % 2]

            def mmv(e, tp=tp, pt=pt):
                ins = None
                for j in range(2):
                    ti = tp * 2 + j
                    for kc in range(8):
                        ins = e.matmul(pt[:, j * 256:(j + 1) * 256], lhsT=hT3[:, kc, ti * 128:(ti + 1) * 128], rhs=wvv[:, kc, 0:256],
                                       start=(kc == 0), stop=(kc == 7))
                return ins
            Sd.op("pe", mmv, reads=[wvv_b, hT_b], writes=[pb_])
            Sd.op("act" if tp % 2 else "dve",
                  (lambda e, tp=tp, pt=pt: e.activation(out=vB4[:, tp * 2:tp * 2 + 2, :, 0:64],
                                                        in_=pt[:, 0:512].rearrange("p (j h e) -> p j h e", j=2, h=4), func=AF.Copy))
                  if tp % 2 else
                  (lambda e, tp=tp, pt=pt: e.tensor_copy(out=vB4[:, tp * 2:tp * 2 + 2, :, 0:64],
                                                         in_=pt[:, 0:512].rearrange("p (j h e) -> p j h e", j=2, h=4))),
                  reads=[pb_], writes=[vB_b])
        stepsB = []
        for i in range(NT):
            for d in range(4, -1, -1):
                if i - d >= 0:
                    stepsB.append((i, d))

        def emitB_S(si):
            i, d = stepsB[si]
            j = i - d
            pt, pb_ = pbank[(0, 1, 4)[si % 3]]

            def mm(e):
                ins = None
                for hh in range(4):
                    ins = e.matmul(pt[:, hh * 128:(hh + 1) * 128], lhsT=kB[hh][0][:, j * 128:(j + 1) * 128],
                                   rhs=qB3[:, hh // 2, i * 128:(i + 1) * 128], start=(hh == 0), stop=False, skip_group_check=True)
                return e.matmul(pt[:, 0:512], lhsT=ident, rhs=BTb3[:, d, :], start=False, stop=True, skip_group_check=True)
            Sd.op("pe", mm, reads=[kB[0][1], kB[1][1], kB[2][1], kB[3][1], qB_b, BTb_b, ident_b], writes=[pb_])

        def emitB_rest(si):
            i, d = stepsB[si]
            j = i - d
            pt, pb_ = pbank[(0, 1, 4)[si % 3]]
            P, P_b = PB[si % 3]
            Sd.op("act", lambda e: e.activation(out=P, in_=pt[:, 0:512], func=AF.Exp), reads=[pb_], writes=[P_b])
            first = (d == min(4, i))
            acc, acc_b = pbank[2 + (i % 2)]

            def pv(e):
                ins = None
                for hh in range(4):
                    ins = e.matmul(acc[:, hh * 65:(hh + 1) * 65], lhsT=P[:, hh * 128:(hh + 1) * 128], rhs=vB4[:, j, hh, :],
                                   start=(first and hh == 0), stop=(d == 0), skip_group_check=True)
                return ins
            Sd.op("pe", pv, reads=[P_b, vB_b], writes=[acc_b])
            if d == 0:
                k = fin_slot()
                sm, sm_b = fsm[k]
                o32, o32_b = ftmp[k]
                a3 = acc[:, 0:260].rearrange("p (h e) -> p h e", h=4)

                def s_comb():
                    Sd.op("dve", lambda e: e.reciprocal(out=sm[:, 8:12], in_=a3[:, :, 64]), reads=[acc_b], writes=[sm_b])
                    Sd.op("dve", lambda e: e.tensor_tensor(out=o32.rearrange("p (h e) -> p h e", h=4), in0=a3[:, :, 0:64],
                                                           in1=bc_last(sm[:, 8:12], 64), op=ALU.mult), reads=[acc_b, sm_b], writes=[o32_b])
                fin_submit([s_comb] + norm_stages(k, o32, o32_b, 4, 64, gnb, gnb_b, i, 512), k)

        nB = len(stepsB)
        for si in range(nB + 2):
            if si < nB:
                emitB_S(si)
            if si >= 2:
                emitB_rest(si - 2)
                tick()
        fin_flush()
        Sd.barrier()
        A.reset(p2_mark)

        qC, qC_b = A.alloc("qC", S, BF16)
        kC = [A.alloc("kC", S, BF16) for _ in range(2)]
        vC, vC_b = A.alloc("vC", NT * 128, BF16)
        vC4 = vC.rearrange("p (t h e) -> p t h e", t=NT, h=2)
        gnc, gnc_b = A.alloc("gnc", 256, F32)
        Sd.dma(lambda e: e.dma_start(out=gnc, in_=gnc_d[l].partition_broadcast(128)), gnc_b.name, writes=[gnc_b])
        Ebuf = [A.alloc("Ebuf", 512, F32) for _ in range(2)]
        Sp = [[A.alloc("Sp", 512, BF16) for _ in range(2)] for _ in range(2)]
        SpSum = [A.alloc("SpSum", 512, BF16) for _ in range(2)]
        AT = [[A.alloc("AT", 512, BF16) for _ in range(2)] for _ in range(2)]
        Sd.op("pool", lambda e: e.memset(kC[0][0][64:128, :], 0.0), writes=[kC[0][1]])
        Sd.op("pool", lambda e: e.memset(kC[1][0][0:64, :], 0.0), writes=[kC[1][1]])
        zbank = [[0, 1], [6, 7]]
        for hp in range(2):
            wv, w_b = load_w([(2304 + hp * 128, 128), (2560 + hp * 128, 128), (2816 + hp * 128, 128)])
            for tb in range(NB):
                pt, pb_ = proj_fm(wv, w_b, 0, None, None, tb, 4, None)
                evac_copy("dve", qC[:, tb * 512:(tb + 1) * 512], pt[:, 0:512], [pb_], [qC_b])
                pt, pb_ = proj_fm(wv, w_b, 128, None, None, tb, 5, None)
                evac_copy("act", kC[0][0][0:64, tb * 512:(tb + 1) * 512], pt[0:64, 0:512], [pb_], [kC[0][1]], scale=0.125)
                evac_copy("dve", kC[1][0][64:128, tb * 512:(tb + 1) * 512], pt[64:128, 0:512], [pb_], [kC[1][1]], scale=0.125)
                pt, pb_ = pbank[5]

                def mmv(e, tb=tb, pt=pt, wv=wv):
                    ins = None
                    for j in range(4):
                        ti = tb * 4 + j
                        for kc in range(8):
                            ins = e.matmul(pt[:, j * 128:(j + 1) * 128], lhsT=hT3[:, kc, ti * 128:(ti + 1) * 128], rhs=wv[:, kc, 256:384],
                                           start=(kc == 0), stop=(kc == 7))
                    return ins
                Sd.op("pe", mmv, reads=[w_b, hT_b], writes=[pb_])
                Sd.op("act", lambda e, tb=tb, pt=pt: e.activation(out=vC4[:, tb * 4:(tb + 1) * 4, :, :],
                                                                 in_=pt[:, 0:512].rearrange("p (j h e) -> p j h e", j=4, h=2), func=AF.Copy),
                      reads=[pb_], writes=[vC_b])
            stepsC = []
            for Q in range(NB):
                for kt in range(4 * Q + 3, -1, -1):
                    stepsC.append((Q, kt))

            def cols(Q, kt):
                r = kt - 4 * Q
                return r, max(r, 0) * 128

            def emitC_QK(si):
                Q, kt = stepsC[si]
                r, c0 = cols(Q, kt)
                for s in range(2):
                    z, z_b = pbank[zbank[si % 2][s]]
                    Sd.op("pe", lambda e, z=z, s=s, kt=kt, Q=Q, c0=c0: e.matmul(
                        z[:, c0:512], lhsT=kC[s][0][:, kt * 128:(kt + 1) * 128], rhs=qC[:, Q * 512 + c0:(Q + 1) * 512],
                        start=True, stop=False, skip_group_check=True), reads=[kC[s][1], qC_b], writes=[z_b])

            def emitC_rest(si):
                Q, kt = stepsC[si]
                r, c0 = cols(Q, kt)
                firstQ = (kt == 4 * Q + 3)
                for s in range(2):
                    if firstQ:
                        Sd.op("pool", lambda e, s=s: e.memset(SpSum[s][0], 0.0), writes=[SpSum[s][1]])
                for s in range(2):
                    z, z_b = pbank[zbank[si % 2][s]]
                    E_, E_b2 = pbank[4 + s] if si % 2 == 0 else Ebuf[s]
                    sp_, sp_b = Sp[s][si % 2]
                    Sd.op("act", lambda e, z=z, E_=E_, c0=c0: e.activation(out=E_[:, c0:512], in_=z[:, c0:512], func=AF.Exp),
                          reads=[z_b], writes=[E_b2])
                    Sd.op("act", lambda e, E_=E_, sp_=sp_, c0=c0: e.activation(out=sp_[:, c0:512], in_=E_[:, c0:512], func=AF.Ln, bias=one_t[:, 0:1]),
                          reads=[E_b2, one_b], writes=[sp_b])
                    if r >= 0:
                        Sd.op("dve", lambda e, sp_=sp_, c0=c0: e.tensor_tensor(out=sp_[:, c0:c0 + 128], in0=sp_[:, c0:c0 + 128], in1=mtri, op=ALU.mult),
                              reads=[sp_b, mtri_b], writes=[sp_b])

                    def cum(e, z=z, sp_=sp_, s=s, c0=c0):
                        ins = e.matmul(z[:, c0:512], lhsT=negU, rhs=sp_[:, c0:512], start=False, stop=firstQ, skip_group_check=True)
                        if not firstQ:
                            ins = e.matmul(z[:, c0:512], lhsT=negOnes, rhs=SpSum[s][0][:, c0:512], start=False, stop=True, skip_group_check=True)
                        return ins
                    Sd.op("pe", cum, reads=[sp_b, negU_b, negOnes_b, SpSum[s][1]], writes=[z_b])
                for s in range(2):
                    z, z_b = pbank[zbank[si % 2][s]]
                    sp_, sp_b = Sp[s][si % 2]
                    a_, a_b = AT[s][si % 2]
                    Sd.op("act", lambda e, z=z, a_=a_, c0=c0: e.activation(out=a_[:, c0:512], in_=z[:, c0:512], func=AF.Exp),
                          reads=[z_b], writes=[a_b])
                    if r >= 0:
                        Sd.op("dve", lambda e, a_=a_, c0=c0: e.tensor_tensor(out=a_[:, c0:c0 + 128], in0=a_[:, c0:c0 + 128], in1=mtri, op=ALU.mult),
                              reads=[a_b, mtri_b], writes=[a_b])
                    if kt > 0:
                        Sd.op("dve", lambda e, s=s, sp_=sp_, c0=c0: e.tensor_tensor(out=SpSum[s][0][:, c0:512], in0=SpSum[s][0][:, c0:512],
                                                                                      in1=sp_[:, c0:512], op=ALU.add),
                              reads=[SpSum[s][1], sp_b], writes=[SpSum[s][1]])
                    acc, acc_b = pbank[2 + (Q % 2)]

                    def pv(e, a_=a_, s=s, acc=acc):
                        ins = None
                        for qs in range(max(r, 0), 4):
                            firstq = (kt == 4 * Q + 3) and s == 0 and qs == 3
                            ins = e.matmul(acc[:, s * 256 + qs * 64:s * 256 + (qs + 1) * 64], lhsT=a_[:, qs * 128:(qs + 1) * 128],
                                           rhs=vC4[:, kt, s, :], start=firstq, stop=(kt == 0), skip_group_check=True)
                        return ins
                    Sd.op("pe", pv, reads=[a_b, vC_b], writes=[acc_b])
                if kt == 0:
                    acc, acc_b = pbank[2 + (Q % 2)]
                    a4 = acc[:, 0:512].rearrange("p (s q e) -> p s q e", s=2, q=4)
                    for qs in range(4):
                        ti = Q * 4 + qs
                        k = fin_slot()
                        o32, o32_b = ftmp[k]

                        def s_comb(o32=o32, o32_b=o32_b, qs=qs, acc_b=acc_b, a4=a4):
                            Sd.op("dve", lambda e: e.tensor_copy(out=o32[:, 0:128].rearrange("p (s e) -> p s e", s=2), in_=a4[:, :, qs, :]),
                                  reads=[acc_b], writes=[o32_b])
                        fin_submit([s_comb] + norm_stages(k, o32, o32_b, 2, 64, gnc[:, hp * 128:(hp + 1) * 128], gnc_b, ti, 768 + hp * 128), k)

            nC = len(stepsC)
            for si in range(nC + 1):
                if si < nC:
                    emitC_QK(si)
                if si >= 1:
                    emitC_rest(si - 1)
                    tick()
        fin_flush()
        Sd.barrier()
        if dbg and l == 0:
            Sd.dma(lambda e: e.dma_start(out=dbg_d["ocat"], in_=ocat_d), "dbg1")
            Sd.barrier()

        A.reset(base_mark)
        wd, wd_b = A.alloc("wd", 32 * D, BF16)
        wd3 = wd.rearrange("p (k n) -> p k n", k=32)
        gpl, gpl_b = A.alloc("gpl", D, F32)
        gin2, gin2_b = A.alloc("gin2", 8, F32)
        Sd.dma(lambda e: e.dma_start(out=gpl, in_=g_post_mlp_d[l].partition_broadcast(128)), gpl_b.name, writes=[gpl_b])
        Sd.dma(lambda e: e.dma_start(out=gin2, in_=g_pre_mlp_d[l].rearrange("(k p) -> p k", p=128), allow_slow_non_contiguous=True),
               gin2_b.name, writes=[gin2_b])
        p4_mark = A.mark()
        stg4 = [A.alloc("stg4", 2048, F32) for _ in range(3)]
        ci = [0]

        def load_wd_chunk(fc2, engs=("dve", "act")):
            sg, sg_b = stg4[ci[0] % 3]
            Sd.dma(lambda e: e.dma_start(out=sg.rearrange("p (f n) -> p f n", f=2),
                                         in_=w_down_d[l][fc2 * 256:(fc2 + 1) * 256, :].rearrange("(f p) n -> p f n", p=128)),
                   sg_b.name, writes=[sg_b])
            eng = engs[ci[0] % len(engs)]
            dst = wd3[:, fc2 * 2:(fc2 + 1) * 2, :]
            src_ = sg.rearrange("p (f n) -> p f n", f=2)
            if eng == "act":
                Sd.op("act", lambda e: e.activation(out=dst, in_=src_, func=AF.Copy), reads=[sg_b], writes=[wd_b])
            else:
                Sd.op(eng, lambda e: e.tensor_copy(out=dst, in_=src_), reads=[sg_b], writes=[wd_b])
            ci[0] += 1

        wo, wo_b = A.alloc("wo", 8 * D, BF16)
        wo3 = wo.rearrange("p (k n) -> p k n", k=8)
        gpm, gpm_b = A.alloc("gpm", D, F32)
        Sd.dma(lambda e: e.dma_start(out=gpm, in_=g_post_mix_d[l].partition_broadcast(128)), gpm_b.name, writes=[gpm_b])
        for kc in range(8):
            sg, sg_b = stg4[ci[0] % 3]
            ci[0] += 1
            Sd.dma(lambda e, sg=sg, kc=kc: e.dma_start(out=sg[:, 0:1024], in_=w_out_d[l][kc * 128:(kc + 1) * 128, :]), sg_b.name, writes=[sg_b])
            evac_copy("dve" if kc % 2 else "act", wo3[:, kc, :], sg[:, 0:1024], [sg_b], [wo_b])
        NS = 5
        oc = [A.alloc("oc", D, BF16) for _ in range(NS)]
        x3 = [A.alloc("x3", D, F32) for _ in range(NS)]
        oT = [A.alloc("oT", D, BF16) for _ in range(2)]
        xn = [A.alloc("xn", D, F32) for _ in range(3)]
        h2 = [A.alloc("h2", D, BF16) for _ in range(2)]
        h2o = [A.alloc("h2o", D, BF16) for _ in range(2)]
        s3 = [A.alloc("s3", 8, F32) for _ in range(4)]
        junk3s = [A.alloc("junk3", D, BF16) for _ in range(4)]
        jc3 = [0]

        def next_junk3():
            jc3[0] += 1
            return junk3s[jc3[0] % 4]

        def p3_load(i):
            oc_, oc_b = oc[i % NS]
            x_, x_b = x3[i % NS]
            Sd.dma(lambda e: e.dma_start(out=oc_, in_=ocat_d[i * 128:(i + 1) * 128, :]), oc_b.name, writes=[oc_b])
            Sd.dma(lambda e: e.dma_start(out=x_, in_=x_src[i * 128:(i + 1) * 128, :]), x_b.name, writes=[x_b])

        def st_tr(i):
            k = i % 2
            oc_, oc_b = oc[i % NS]
            pv = pbf(k)

            def tr(e):
                ins = None
                for kc in range(8):
                    ins = e.transpose(out=pv[:, kc * 128:(kc + 1) * 128], in_=oc_[:, kc * 128:(kc + 1) * 128], identity=ident)
                return ins
            Sd.op("pe", tr, reads=[oc_b, ident_b], writes=[pbank[k][1]])

        def st_evac(i):
            k = i % 2
            oT_, oT_b = oT[k]
            pv = pbf(k)
            Sd.op("dve", lambda e: e.tensor_copy(out=oT_, in_=pv), reads=[pbank[k][1]], writes=[oT_b])

        def st_mm(i):
            k = i % 2
            oT_, oT_b = oT[k]
            oT3 = oT_.rearrange("p (k t) -> p k t", k=8)
            for hf in range(2):
                pt, pb_ = pbank[2 + k * 2 + hf]

                def mm(e, pt=pt, hf=hf):
                    ins = None
                    for kc in range(8):
                        ins = e.matmul(pt[:, 0:512], lhsT=oT3[:, kc, :], rhs=wo3[:, kc, hf * 512:(hf + 1) * 512], start=(kc == 0), stop=(kc == 7))
                    return ins
                Sd.op("pe", mm, reads=[oT_b, wo_b], writes=[pb_])

        def st_sqy(i):
            k = i % 2
            s_, s_b = s3[i % 4]
            for hf in range(2):
                pt, pb_ = pbank[2 + k * 2 + hf]
                junk3, junk3_b = next_junk3()
                Sd.op("act", lambda e, pt=pt, hf=hf, junk3=junk3: e.activation(out=junk3[:, 0:512], in_=pt[:, 0:512], func=AF.Square, accum_out=s_[:, hf:hf + 1]),
                      reads=[pb_], writes=[junk3_b, s_b])

        def st_ssqadd(i):
            s_, s_b = s3[i % 4]
            Sd.op("dve", lambda e: e.tensor_tensor(out=s_[:, 2:3], in0=s_[:, 0:1], in1=s_[:, 1:2], op=ALU.add), reads=[s_b], writes=[s_b])

        def st_rstd1(i):
            s_, s_b = s3[i % 4]
            rstd_ops(s_[:, 2:3], s_[:, 3:4], D, s_b)

        def st_xn(i):
            k = i % 2
            x_, x_b = x3[i % NS]
            xn_, xn_b = xn[i % 3]
            s_, s_b = s3[i % 4]
            for hf in range(2):
                pt, pb_ = pbank[2 + k * 2 + hf]
                Sd.op("dve", lambda e, pt=pt, hf=hf: e.scalar_tensor_tensor(
                    out=xn_[:, hf * 512:(hf + 1) * 512], in0=pt[:, 0:512], scalar=s_[:, 3:4], in1=gpm[:, hf * 512:(hf + 1) * 512],
                    op0=ALU.mult, op1=ALU.mult), reads=[pb_, s_b, gpm_b], writes=[xn_b])
            Sd.op("dve", lambda e: e.tensor_tensor(out=xn_, in0=xn_, in1=x_, op=ALU.add), reads=[xn_b, x_b], writes=[xn_b])
            Sd.dma(lambda e: e.dma_start(out=xa_d[i * 128:(i + 1) * 128, :], in_=xn_), xn_b.name + "s", reads=[xn_b])

        def st_sqx(i):
            xn_, xn_b = xn[i % 3]
            s_, s_b = s3[i % 4]
            junk3, junk3_b = next_junk3()
            Sd.op("act", lambda e: e.activation(out=junk3, in_=xn_, func=AF.Square, accum_out=s_[:, 4:5]),
                  reads=[xn_b], writes=[junk3_b, s_b])
            rstd_ops(s_[:, 4:5], s_[:, 5:6], D, s_b)

        def st_h2(i):
            xn_, xn_b = xn[i % 3]
            h2_, h2_b = h2[i % 2]
            s_, s_b = s3[i % 4]
            Sd.op("dve", lambda e: e.tensor_scalar(out=h2_, in0=xn_, scalar1=s_[:, 5:6], scalar2=None, op0=ALU.mult),
                  reads=[xn_b, s_b], writes=[h2_b])

        def st_tr2(i):
            k = i % 2
            h2_, h2_b = h2[k]
            pv2 = pbf(6 + k)

            def tr2(e):
                ins = None
                for kc in range(8):
                    ins = e.transpose(out=pv2[:, kc * 128:(kc + 1) * 128], in_=h2_[:, kc * 128:(kc + 1) * 128], identity=ident)
                return ins
            Sd.op("pe", tr2, reads=[h2_b, ident_b], writes=[pbank[6 + k][1]])

        def st_h2o(i):
            k = i % 2
            h2o_, h2o_b = h2o[k]
            pv2 = pbf(6 + k)
            Sd.op("act", lambda e: e.activation(out=h2o_, in_=pv2, func=AF.Copy), reads=[pbank[6 + k][1]], writes=[h2o_b])
            Sd.dma(lambda e: e.dma_start(out=h2T_d[i], in_=h2o_), h2o_b.name + "s", reads=[h2o_b])

        def ok(t):
            return 0 <= t < NT

        for i in range(min(3, NT)):
            p3_load(i)
        wd_next = 0
        for j in range(NT + 4):
            if ok(j):
                st_tr(j)
            if ok(j - 2):
                st_ssqadd(j - 2)
                st_rstd1(j - 2)
            if ok(j - 4):
                st_tr2(j - 4)
            if ok(j):
                st_evac(j)
            if ok(j - 3):
                st_sqx(j - 3)
            if ok(j - 1):
                st_mm(j - 1)
            if ok(j - 2):
                st_xn(j - 2)
            if ok(j - 4):
                st_h2o(j - 4)
            if ok(j - 3):
                st_h2(j - 3)
            if ok(j - 1):
                st_sqy(j - 1)
            if j + 3 < NT:
                p3_load(j + 3)
            if j % 2 == 1 and wd_next < 16:
                load_wd_chunk(wd_next)
                wd_next += 1
        while wd_next < 16:
            load_wd_chunk(wd_next, engs=("dve", "act"))
            wd_next += 1
        Sd.barrier()
        if dbg and l == 0:
            Sd.dma(lambda e: e.dma_start(out=dbg_d["xa"], in_=xa_d), "dbg2")
            Sd.barrier()

        A.reset(p4_mark)
        wu, wu_b = A.alloc("wu", 8 * DFF, BF16)
        wu3 = wu.rearrange("p (k n) -> p k n", k=8)
        m4 = A.mark()
        stg4 = [A.alloc("stg4b", 2048, F32) for _ in range(3)]
        ci = 0
        for kc in range(8):
            for c in range(2):
                sg, sg_b = stg4[ci % 3]
                Sd.dma(lambda e, sg=sg, kc=kc, c=c: e.dma_start(out=sg, in_=w_up_d[l][kc * 128:(kc + 1) * 128, c * 2048:(c + 1) * 2048]),
                       sg_b.name, writes=[sg_b])
                if ci % 2 == 0:
                    Sd.op("dve", lambda e, sg=sg, kc=kc, c=c: e.tensor_scalar(out=wu3[:, kc, c * 2048:(c + 1) * 2048], in0=sg, scalar1=gin2[:, kc:kc + 1],
                                                                             scalar2=None, op0=ALU.mult), reads=[sg_b, gin2_b], writes=[wu_b])
                else:
                    Sd.op("act", lambda e, sg=sg, kc=kc, c=c: e.activation(out=wu3[:, kc, c * 2048:(c + 1) * 2048], in_=sg, func=AF.Copy,
                                                                          scale=gin2[:, kc:kc + 1]), reads=[sg_b, gin2_b], writes=[wu_b])
                ci += 1
        Sd.barrier()
        A.reset(m4)
        TB = 4
        NMB = NT // TB
        uT, uT_b = A.alloc("uT", 32 * 512, BF16)
        uT3 = uT.rearrange("p (f t) -> p f t", f=32)
        uR = [A.alloc("uR", 512, BF16) for _ in range(3)]
        hblk, hblk_b = A.alloc("hblk", TB * 1024, BF16)
        hb4 = hblk.rearrange("p (t k x) -> p t k x", t=TB, k=8)
        x4 = [A.alloc("x4", D, F32) for _ in range(4)]
        o4 = [A.alloc("o4", D, F32) for _ in range(2)]
        s4 = [A.alloc("s4", 8, F32) for _ in range(2)]
        junk4s = [A.alloc("junk4", 512, BF16) for _ in range(3)]
        jc4 = [0]
        uTb = [Buf(f"uTpart{j}") for j in range(32)]

        def p4_load_h(b):
            Sd.dma(lambda e: e.dma_start(out=hblk.rearrange("p (t x) -> p t x", t=TB),
                                         in_=h2T_d[TB * b:TB * (b + 1)].rearrange("t p x -> p t x")), hblk_b.name, writes=[hblk_b])

        def p4_load_x(ti):
            x_, x_b = x4[ti % 4]
            Sd.dma(lambda e: e.dma_start(out=x_, in_=xa_d[ti * 128:(ti + 1) * 128, :]), x_b.name, writes=[x_b])

        p4_load_h(0)
        for ti in range(min(4, NT)):
            p4_load_x(ti)
        for b in range(NMB):
            for fc in range(32):
                pt, pb_ = pbank[fc % 2]

                def mm(e, pt=pt, fc=fc):
                    ins = None
                    for kc in range(8):
                        ins = e.matmul(pt[:, 0:512].rearrange("p (t x) -> p t x", t=TB), lhsT=wu3[:, kc, fc * 128:(fc + 1) * 128],
                                       rhs=hb4[:, :, kc, :], start=(kc == 0), stop=(kc == 7))
                    return ins
                Sd.op("pe", mm, reads=[wu_b, hblk_b], writes=[pb_])
                ur, ur_b = uR[fc % 3]
                Sd.op("dve", lambda e, pt=pt, ur=ur: e.tensor_scalar(out=ur, in0=pt[:, 0:512], scalar1=0.0, scalar2=None, op0=ALU.max),
                      reads=[pb_], writes=[ur_b])
                Sd.op("pool", lambda e, ur=ur, fc=fc: e.tensor_tensor(out=uT3[:, fc, :], in0=ur, in1=ur, op=ALU.mult),
                      reads=[ur_b], writes=[uTb[fc]])
            if b + 1 < NMB:
                p4_load_h(b + 1)
            for t in range(TB):
                ti = b * TB + t
                k = ti % 2
                x_, x_b = x4[ti % 4]
                o_, o_b = o4[k]
                s_, s_b = s4[k]
                for hf in range(2):
                    pt, pb_ = pbank[2 + k * 2 + hf]

                    def mm(e, pt=pt, hf=hf, t=t):
                        ins = None
                        for fc in range(32):
                            ins = e.matmul(pt[:, 0:512], lhsT=uT3[:, fc, t * 128:(t + 1) * 128], rhs=wd3[:, fc, hf * 512:(hf + 1) * 512],
                                           start=(fc == 0), stop=(fc == 31))
                        return ins
                    Sd.op("pe", mm, reads=uTb + [wd_b], writes=[pb_])
                    jc4[0] += 1
                    junk4, junk4_b = junk4s[jc4[0] % 3]
                    Sd.op("act", lambda e, pt=pt, s_=s_, hf=hf, junk4=junk4: e.activation(out=junk4, in_=pt[:, 0:512], func=AF.Square, accum_out=s_[:, hf:hf + 1]),
                          reads=[pb_], writes=[junk4_b, s_b])
                Sd.op("dve", lambda e, s_=s_: e.tensor_tensor(out=s_[:, 2:3], in0=s_[:, 0:1], in1=s_[:, 1:2], op=ALU.add), reads=[s_b], writes=[s_b])
                rstd_ops(s_[:, 2:3], s_[:, 3:4], D, s_b)
                for hf in range(2):
                    pt, pb_ = pbank[2 + k * 2 + hf]
                    Sd.op("dve", lambda e, pt=pt, o_=o_, s_=s_, hf=hf: e.scalar_tensor_tensor(
                        out=o_[:, hf * 512:(hf + 1) * 512], in0=pt[:, 0:512], scalar=s_[:, 3:4], in1=gpl[:, hf * 512:(hf + 1) * 512],
                        op0=ALU.mult, op1=ALU.mult), reads=[pb_, s_b, gpl_b], writes=[o_b])
                Sd.op("pool", lambda e, o_=o_, x_=x_: e.tensor_tensor(out=o_, in0=o_, in1=x_, op=ALU.add), reads=[o_b, x_b], writes=[o_b])
                Sd.dma(lambda e, o_=o_, ti=ti: e.dma_start(out=x_dst[ti * 128:(ti + 1) * 128, :], in_=o_), o_b.name + "s", reads=[o_b])
                if ti + 4 < NT:
                    p4_load_x(ti + 4)
        Sd.barrier()

    one_t, one_b = A.alloc("one", 1, F32)
    Sd.op("pool", lambda e: e.memset(one_t, 1.0), writes=[one_b])
    base_mark = A.mark()
    base_names = dict(A.namecnt)
    src = x_d
    for l in range(L):
        dst = out_d if l == L - 1 else xb_d
        layer(l, src, dst)
        src = dst
    Sd.barrier()
    Sd.finalize(nc, es)
    es.close()
    stats = dict(n_ops={e: len(Sd.ops[e]) for e in Sd.ENGS}, n_sems=len(Sd.sems), n_waits=Sd.n_waits, arena_peak=A.peak)
    return nc, stats


def host_consts(S):
    bf = ml_dtypes.bfloat16
    c = {}
    c["c_ident"] = np.eye(128, dtype=np.float32).astype(bf)
    pos = np.arange(S, dtype=np.float32)
    inv_freq = (np.float32(500000.0) ** (-np.arange(0, 16, 2, dtype=np.float32) / np.float32(16))).astype(np.float32)
    ang = (pos[:, None] * inv_freq[None, :]).astype(np.float32)
    cs, sn = np.cos(ang).astype(np.float32), np.sin(ang).astype(np.float32)
    C = np.ones((128, S), np.float32)
    Sg = np.zeros((128, S), np.float32)
    for n in range(2):
        for i in range(8):
            C[n * 64 + i] = cs[:, i]
            C[n * 64 + 8 + i] = cs[:, i]
            Sg[n * 64 + i] = -sn[:, i]
            Sg[n * 64 + 8 + i] = sn[:, i]
    c["c_ropeC"] = C
    c["c_ropeS"] = Sg
    p = np.arange(128)[:, None]
    j = np.arange(128)[None, :]
    c["c_mask_cc"] = np.where((p >= 64) & (j < 64), 0.0, 1.0).astype(np.float32).astype(bf)
    c["c_mask_tri"] = (p < j).astype(np.float32).astype(bf)
    c["c_negU"] = np.where(p >= j, -1.0, 0.0).astype(np.float32).astype(bf)
    c["c_negOnes"] = np.full((128, 128), -1.0, np.float32).astype(bf)
    m0 = np.where((p >= 64) & (j < 64), -30000.0, 0.0).astype(np.float32)
    m4 = np.where((p < 64) & (j >= 64), -30000.0, 0.0).astype(np.float32)
    c["c_maskB"] = np.concatenate([m0, m4], axis=1)
    c["c_antiI"] = np.eye(128, dtype=np.float32)[::-1].copy()
    return c


_CACHE = {}

PARAM_NAMES = ["w_in", "w_out", "w_up", "w_down", "norm_pre_mix", "norm_post_mix", "norm_pre_mlp", "norm_post_mlp",
               "lam_q1", "lam_k1", "lam_q2", "lam_k2", "subln_a", "rel_bias", "gn_b", "gn_c"]


def kernel(**inputs):
    x = np.ascontiguousarray(np.asarray(inputs["x"], dtype=np.float32))
    B, S, _ = x.shape
    L = int(np.asarray(inputs["w_in"]).shape[0])
    key = (S, L)
    if key not in _CACHE:
        _CACHE[key] = build(S, L)[0]
    nc = _CACHE[key]
    consts = host_consts(S)
    shared = {n: np.ascontiguousarray(np.asarray(inputs[n], dtype=np.float32)) for n in PARAM_NAMES}
    shared.update(consts)
    n_cores = 8
    in_maps = []
    for c in range(n_cores):
        m = dict(shared)
        m["x"] = x[c % B]
        in_maps.append(m)
    res = run_bass_kernel_spmd(nc, in_maps, core_ids=list(range(n_cores)))
    out = np.stack([np.asarray(res.results[b]["out"], dtype=np.float32) for b in range(B)], axis=0)
    return out
```

```python
import math
import types
import numpy as np
import ml_dtypes
import concourse.bass as bass
import concourse.mybir as mybir
from concourse.ap import AP
from concourse.bass_utils import run_bass_kernel_spmd
from contextlib import ExitStack

F32 = mybir.dt.float32
BF16 = mybir.dt.bfloat16
U8 = mybir.dt.uint8
ALU = mybir.AluOpType
AF = mybir.ActivationFunctionType
AX = mybir.AxisListType

D = 1024
DFF = 4096
DIN = 3072
EPS = 1e-6
SAME_ENG_RAW_SYNC = True


class Buf:
    __slots__ = ("name", "w", "r")

    def __init__(self, name):
        self.name = name
        self.w = {}
        self.r = {}


class Op:
    __slots__ = ("eng", "fn", "deps", "signal", "token", "is_dma", "dkey", "extra_waits")

    def __init__(self, eng, fn, is_dma=False, dkey=None):
        self.eng = eng
        self.fn = fn
        self.deps = set()
        self.signal = False
        self.token = None
        self.is_dma = is_dma
        self.dkey = dkey
        self.extra_waits = []


def _freeze(fn):
    if fn.__closure__ is None:
        return fn
    cells = []
    for c in fn.__closure__:
        try:
            cells.append(types.CellType(c.cell_contents))
        except ValueError:
            cells.append(c)
    return types.FunctionType(fn.__code__, fn.__globals__, fn.__name__, fn.__defaults__, tuple(cells))


class Sched:
    ENGS = ("sp", "act", "dve", "pool", "pe")

    def __init__(self):
        self.ops = {e: [] for e in self.ENGS}
        self.all = []
        self.last_op = {e: None for e in self.ENGS}
        self.dma_keys = {}
        self.dma_last = {}
        self.pending_barrier = {e: [] for e in self.ENGS}

    def _deps(self, op, reads, writes):
        for b in reads:
            for e, w in b.w.items():
                if w is op:
                    continue
                if w.is_dma or e != op.eng or op.is_dma or (SAME_ENG_RAW_SYNC and e != "pe"):
                    op.deps.add(w)
        for b in writes:
            for e, w in b.w.items():
                if w is op:
                    continue
                if w.is_dma and op.is_dma and w.dkey == op.dkey and w.eng == op.eng:
                    op.deps |= w.deps
                    continue
                if w.is_dma or e != op.eng or op.is_dma or e != "pe":
                    op.deps.add(w)
            for e, r in b.r.items():
                if r is op:
                    continue
                if r.is_dma or e != op.eng or op.is_dma or e != "pe":
                    op.deps.add(r)
        for b in reads:
            b.r[op.eng if not op.is_dma else ("dma", id(op))] = op
        for b in writes:
            if op.is_dma:
                prev = {k: v for k, v in b.w.items() if v.is_dma and v.dkey == op.dkey}
                b.w = prev
                b.w[("dma", id(op))] = op
            else:
                b.w = {op.eng: op}
            b.r = {}

    def op(self, eng, fn, reads=(), writes=()):
        o = Op(eng, _freeze(fn))
        self._deps(o, reads, writes)
        o.deps |= set(self.pending_barrier[eng])
        self.pending_barrier[eng] = []
        self.ops[eng].append(o)
        self.all.append(o)
        self.last_op[eng] = o
        return o

    def dma(self, fn, key, reads=(), writes=(), eng="sp"):
        o = Op(eng, _freeze(fn), is_dma=True, dkey=key)
        self._deps(o, reads, writes)
        o.deps |= set(self.pending_barrier[eng])
        self.pending_barrier[eng] = []
        self.dma_keys[key] = self.dma_keys.get(key, 0) + 1
        o.token = ("dma:" + key, 16 * self.dma_keys[key])
        o.signal = True
        self.dma_last[key] = o
        self.ops[eng].append(o)
        self.all.append(o)
        return o

    def barrier(self):
        toks = [o for o in self.last_op.values() if o is not None]
        toks += list(self.dma_last.values())
        for e in self.ENGS:
            self.pending_barrier[e] = list(toks)

    def finalize(self, nc, es):
        for o in self.all:
            for d in o.deps:
                d.signal = True
        for e in self.ENGS:
            for d in self.pending_barrier[e]:
                d.signal = True
        cnt = {e: 0 for e in self.ENGS}
        for e in self.ENGS:
            for o in self.ops[e]:
                if o.is_dma:
                    continue
                if o.signal:
                    cnt[e] += 1
                    o.token = ("eng:" + e, cnt[e])
        names = set()
        for o in self.all:
            if o.signal:
                names.add(o.token[0])
        self.sems = {}
        for n in sorted(names):
            self.sems[n] = es.enter_context(nc.semaphore(n.replace(":", "_")))
        final_waits = {}
        self.n_waits = 0
        for e in self.ENGS:
            waited = {}
            for o in self.ops[e]:
                need = {}
                for d in o.deps:
                    k, v = d.token
                    if need.get(k, 0) < v:
                        need[k] = v
                o.extra_waits = []
                for k, v in need.items():
                    if waited.get(k, 0) < v:
                        waited[k] = v
                        o.extra_waits.append((k, v))
                        self.n_waits += 1
            need = {}
            for d in self.pending_barrier[e]:
                k, v = d.token
                if need.get(k, 0) < v:
                    need[k] = v
            final_waits[e] = [(k, v) for k, v in need.items() if waited.get(k, 0) < v]
        block = es.enter_context(nc.Block())
        sems = self.sems

        def make(ename):
            ops = self.ops[ename]
            fw = final_waits[ename]

            def run(eng):
                for o in ops:
                    for k, v in o.extra_waits:
                        eng.wait_ge(sems[k], v)
                    ins = o.fn(eng)
                    if o.signal:
                        k, v = o.token
                        ins.then_inc(sems[k], 16 if o.is_dma else 1)
                for k, v in fw:
                    eng.wait_ge(sems[k], v)
            return run

        block.sync(make("sp"))
        block.scalar(make("act"))
        block.vector(make("dve"))
        block.gpsimd(make("pool"))
        block.tensor(make("pe"))


class Arena:
    def __init__(self, tensor_ap, nbytes):
        self.t = tensor_ap
        self.n = nbytes
        self.off = 0
        self.namecnt = {}
        self.peak = 0

    def mark(self):
        return self.off

    def reset(self, m):
        self.off = m

    def alloc(self, name, free_elems, dtype):
        esz = 4 if dtype == F32 else 2
        nb = free_elems * esz
        nb_al = (nb + 63) // 64 * 64
        assert self.off + nb_al <= self.n, f"arena overflow {name}: {self.off}+{nb_al}>{self.n}"
        v = self.t[:, self.off:self.off + nb].bitcast(dtype)
        self.off += nb_al
        self.peak = max(self.peak, self.off)
        idx = self.namecnt.get(name, 0)
        self.namecnt[name] = idx + 1
        return v, Buf(f"{name}_{idx}")


def bc_mid(ap2, m):
    a = ap2.ap
    return AP(ap2.tensor, ap2.offset, [list(a[0]), [0, m], list(a[1])])


def bc_last(ap2, n):
    a = ap2.ap
    return AP(ap2.tensor, ap2.offset, [list(a[0]), list(a[1]), [0, n]])


def build(S=4096, L=2, dbg=False):
    NT = S // 128
    NQ = S // 256
    NB = S // 512
    NM = S // 256
    nc = bass.Bass("TRN2", target_bir_lowering=False)

    def din(name, shape, dt=F32):
        return nc.dram_tensor(name, shape, dt, kind="ExternalInput").ap()

    def dscr(name, shape, dt):
        return nc.dram_tensor(name, shape, dt, kind="Internal").ap()

    x_d = din("x", [S, D])
    w_in_d = din("w_in", [L, D, DIN])
    w_out_d = din("w_out", [L, D, D])
    w_up_d = din("w_up", [L, D, DFF])
    w_down_d = din("w_down", [L, DFF, D])
    g_pre_mix_d = din("norm_pre_mix", [L, D])
    g_post_mix_d = din("norm_post_mix", [L, D])
    g_pre_mlp_d = din("norm_pre_mlp", [L, D])
    g_post_mlp_d = din("norm_post_mlp", [L, D])
    lam_d = [din(n, [L, 64]) for n in ("lam_q1", "lam_k1", "lam_q2", "lam_k2")]
    subln_d = din("subln_a", [L, 128])
    relb_d = din("rel_bias", [L, 4, 257])
    gnb_d = din("gn_b", [L, 256])
    gnc_d = din("gn_c", [L, 256])
    ident_d = din("c_ident", [128, 128], BF16)
    ropeC_d = din("c_ropeC", [128, S])
    ropeS_d = din("c_ropeS", [128, S])
    mcc_d = din("c_mask_cc", [128, 128], BF16)
    mtri_d = din("c_mask_tri", [128, 128], BF16)
    negU_d = din("c_negU", [128, 128], BF16)
    negOnes_d = din("c_negOnes", [128, 128], BF16)
    mB_d = din("c_maskB", [128, 2 * 128])
    antiI_d = din("c_antiI", [128, 128])
    out_d = nc.dram_tensor("out", [S, D], F32, kind="ExternalOutput").ap()
    ocat_d = dscr("ocat", [S, D], BF16)
    xa_d = dscr("xa", [S, D], F32)
    xb_d = dscr("xb", [S, D], F32)
    h2T_d = dscr("h2T", [NT, 128, 8 * 128], BF16)
    E_d = dscr("Eext", [4, 384], F32)
    dbg_d = {}
    if dbg:
        dbg_d["ocat"] = nc.dram_tensor("dbg_ocat", [S, D], BF16, kind="ExternalOutput").ap()
        dbg_d["xa"] = nc.dram_tensor("dbg_xa", [S, D], F32, kind="ExternalOutput").ap()

    Sd = Sched()
    es = ExitStack()
    ARENA_BYTES = 206 * 1024
    arena_t = es.enter_context(nc.sbuf_tensor("arena", [128, ARENA_BYTES], U8))
    A = Arena(arena_t, ARENA_BYTES)
    pbank = []
    for i in range(8):
        t = es.enter_context(nc.psum_tensor(f"pb{i}", [128, 512], F32))
        pbank.append((t, Buf(f"pb{i}")))

    def pbf(i):
        return pbank[i][0][:, 0:512].bitcast(BF16)

    def dump(name, ap, buf):
        if not dbg:
            return
        t = nc.dram_tensor("dbg_" + name, list(ap.shape), ap.dtype, kind="ExternalOutput").ap()
        Sd.dma(lambda e: e.dma_start(out=t, in_=ap), "dbgdump_" + name, reads=[buf])

    ident, ident_b = A.alloc("ident", 128, BF16)
    mcc, mcc_b = A.alloc("mcc", 128, BF16)
    mtri, mtri_b = A.alloc("mtri", 128, BF16)
    negU, negU_b = A.alloc("negU", 128, BF16)
    negOnes, negOnes_b = A.alloc("negOnes", 128, BF16)
    for (t, b, d) in ((ident, ident_b, ident_d), (mcc, mcc_b, mcc_d), (mtri, mtri_b, mtri_d),
                      (negU, negU_b, negU_d), (negOnes, negOnes_b, negOnes_d)):
        Sd.dma(lambda e, t=t, d=d: e.dma_start(out=t, in_=d), b.name, writes=[b])

    def rstd_ops(ssq, out, n, rb):
        Sd.op("act", lambda e: e.activation(out=out, in_=ssq, func=AF.Ln, scale=1.0 / n, bias=eps_t[:, 0:1]),
              reads=[rb, eps_b], writes=[rb])
        Sd.op("act", lambda e: e.activation(out=out, in_=out, func=AF.Exp, scale=-0.5), reads=[rb], writes=[rb])

    eps_t, eps_b = A.alloc("eps", 1, F32)
    Sd.op("pool", lambda e: e.memset(eps_t, EPS), writes=[eps_b])

    base_mark = A.mark()

    def layer(l, x_src, x_dst):
        lam_init = 0.8 - 0.6 * math.exp(-0.3 * l)
        A.reset(base_mark)
        A.namecnt = dict(base_names)
        hT, hT_b = A.alloc("hT", 8 * S, BF16)
        hT3 = hT.rearrange("p (k s) -> p k s", k=8)
        p12_mark = A.mark()

        xin = [A.alloc("xin", D, F32) for _ in range(4)]
        hb = [A.alloc("hb", D, BF16) for _ in range(3)]
        junks = [A.alloc("junk", D, BF16) for _ in range(4)]
        st = [A.alloc("st", 4, F32) for _ in range(4)]

        def p1_load(i):
            xt, xt_b = xin[i % 4]
            Sd.dma(lambda e: e.dma_start(out=xt, in_=x_src[i * 128:(i + 1) * 128, :]), xt_b.name, writes=[xt_b])

        def p1_sq(i):
            xt, xt_b = xin[i % 4]
            s_, s_b = st[i % 4]
            junk, junk_b = junks[i % 4]
            Sd.op("act", lambda e: e.activation(out=junk, in_=xt, func=AF.Square, accum_out=s_[:, 0:1]),
                  reads=[xt_b], writes=[junk_b, s_b])
            rstd_ops(s_[:, 0:1], s_[:, 1:2], D, s_b)

        def p1_scale(i):
            xt, xt_b = xin[i % 4]
            ht, ht_b = hb[i % 3]
            s_, s_b = st[i % 4]
            Sd.op("dve", lambda e: e.tensor_scalar(out=ht, in0=xt, scalar1=s_[:, 1:2], scalar2=None, op0=ALU.mult),
                  reads=[xt_b, s_b], writes=[ht_b])

        def p1_tr(i):
            ht, ht_b = hb[i % 3]
            pv = pbf(i % 2)

            def tr(e):
                ins = None
                for kc in range(8):
                    ins = e.transpose(out=pv[:, kc * 128:(kc + 1) * 128], in_=ht[:, kc * 128:(kc + 1) * 128], identity=ident)
                return ins
            Sd.op("pe", tr, reads=[ht_b, ident_b], writes=[pbank[i % 2][1]])

        def p1_evac(i):
            pv = pbf(i % 2)
            if i % 2:
                Sd.op("dve", lambda e: e.tensor_copy(out=hT3[:, :, i * 128:(i + 1) * 128], in_=pv.rearrange("p (k t) -> p k t", k=8)),
                      reads=[pbank[i % 2][1]], writes=[hT_b])
            else:
                Sd.op("act", lambda e: e.activation(out=hT3[:, :, i * 128:(i + 1) * 128], in_=pv.rearrange("p (k t) -> p k t", k=8), func=AF.Copy),
                      reads=[pbank[i % 2][1]], writes=[hT_b])

        for i in range(min(2, NT)):
            p1_load(i)
        for j in range(NT + 3):
            if 0 <= j - 3 < NT:
                p1_evac(j - 3)
            if j < NT:
                p1_sq(j)
            if 0 <= j - 2 < NT:
                p1_tr(j - 2)
            if 0 <= j - 1 < NT:
                p1_scale(j - 1)
            if j + 2 < NT:
                p1_load(j + 2)
        Sd.barrier()
        A.reset(p12_mark)

        gin, gin_b = A.alloc("gin", 8, F32)
        Sd.dma(lambda e: e.dma_start(out=gin, in_=g_pre_mix_d[l].rearrange("(k p) -> p k", p=128), allow_slow_non_contiguous=True),
               gin_b.name, writes=[gin_b])
        stage = [A.alloc("stage", 8 * 384, F32) for _ in range(2)]
        wb = [A.alloc("wb", 8 * 384, BF16) for _ in range(2)]
        wslot = [0]

        def load_w(col_groups):
            k = wslot[0] % 2
            wslot[0] += 1
            ntot = sum(n for _, n in col_groups)
            sg, sg_b = stage[k]
            w_, w_b = wb[k]
            sv = sg[:, 0:8 * ntot].rearrange("p (k n) -> p k n", k=8)
            wv = w_[:, 0:8 * ntot].rearrange("p (k n) -> p k n", k=8)
            o = 0
            for (c0, n) in col_groups:
                Sd.dma(lambda e, sv=sv, o=o, n=n, c0=c0: e.dma_start(
                    out=sv[:, :, o:o + n], in_=w_in_d[l][:, c0:c0 + n].rearrange("(k p) n -> p k n", p=128)),
                    sg_b.name, writes=[sg_b])
                o += n
            Sd.op("dve", lambda e, sv=sv, wv=wv, ntot=ntot: e.tensor_tensor(out=wv, in0=sv, in1=bc_last(gin, ntot), op=ALU.mult),
                  reads=[sg_b, gin_b], writes=[w_b])
            return wv, w_b

        ostg = [A.alloc("ostg", 256, BF16) for _ in range(4)]
        ostg_i = [0]
        ftmp = [A.alloc("ftmp", 256, F32) for _ in range(4)]
        fsm = [A.alloc("fsm", 16, F32) for _ in range(4)]
        fin_i = [0]

        def proj_fm(wv, w_b, c0, dstT, dst_b, tb, pi, eng, scale=None, rows=None):
            pt, pb_ = pbank[pi]

            def mm(e):
                ins = None
                for kc in range(8):
                    ins = e.matmul(pt[:, 0:512], lhsT=wv[:, kc, c0:c0 + 128], rhs=hT3[:, kc, tb * 512:(tb + 1) * 512],
                                   start=(kc == 0), stop=(kc == 7))
                return ins
            Sd.op("pe", mm, reads=[w_b, hT_b], writes=[pb_])
            return pt, pb_

        def evac_copy(eng, out, in_, reads, writes, scale=None):
            if eng == "act":
                if scale is None:
                    Sd.op("act", lambda e: e.activation(out=out, in_=in_, func=AF.Copy), reads=reads, writes=writes)
                else:
                    Sd.op("act", lambda e: e.activation(out=out, in_=in_, func=AF.Copy, scale=scale), reads=reads, writes=writes)
            else:
                if scale is None:
                    Sd.op("dve", lambda e: e.tensor_copy(out=out, in_=in_), reads=reads, writes=writes)
                else:
                    Sd.op("dve", lambda e: e.tensor_scalar(out=out, in0=in_, scalar1=scale, scalar2=None, op0=ALU.mult),
                          reads=reads, writes=writes)

        fin_pending = []

        def fin_submit(stages, k=None):
            for stl in list(fin_pending):
                if stl[0] == k:
                    for f in stl[1:]:
                        f()
                    fin_pending.remove(stl)
            fin_pending.append([k] + list(stages))

        def tick():
            for stl in list(fin_pending):
                f = stl.pop(1)
                f()
                if len(stl) == 1:
                    fin_pending.remove(stl)

        def fin_flush():
            while fin_pending:
                tick()

        def fin_slot():
            k = fin_i[0] % 4
            fin_i[0] += 1
            return k

        def norm_stages(k, o32, o32_b, nh, hd, gtile, g_b, tile_i, col0):
            sm, sm_b = fsm[k]
            sq, sq_b = fsqs[k]
            og, og_b = ostg[k]
            w = nh * hd
            o3 = o32[:, 0:w].rearrange("p (h d) -> p h d", h=nh)

            def s_sq():
                if nh == 1:
                    Sd.op("act", lambda e: e.activation(out=sq[:, 0:w], in_=o32[:, 0:w], func=AF.Square, accum_out=sm[:, 0:1]),
                          reads=[o32_b], writes=[sq_b, sm_b])
                else:
                    Sd.op("pool", lambda e: e.tensor_tensor(out=sq[:, 0:w], in0=o32[:, 0:w], in1=o32[:, 0:w], op=ALU.mult),
                          reads=[o32_b], writes=[sq_b])

            def s_red():
                if nh > 1:
                    Sd.op("dve", lambda e: e.tensor_reduce(out=sm[:, 0:nh], in_=sq[:, 0:w].rearrange("p (h d) -> p h d", h=nh), axis=AX.X, op=ALU.add),
                          reads=[sq_b], writes=[sm_b])

            def s_ln():
                Sd.op("act", lambda e: e.activation(out=sm[:, 4:4 + nh], in_=sm[:, 0:nh], func=AF.Ln, scale=1.0 / hd, bias=eps_t[:, 0:1]),
                      reads=[sm_b, eps_b], writes=[sm_b])

            def s_exp():
                Sd.op("act", lambda e: e.activation(out=sm[:, 4:4 + nh], in_=sm[:, 4:4 + nh], func=AF.Exp, scale=-0.5), reads=[sm_b], writes=[sm_b])

            def s_scale():
                if nh == 1:
                    Sd.op("dve", lambda e: e.scalar_tensor_tensor(out=og[:, 0:w], in0=o32[:, 0:w], scalar=sm[:, 4:5], in1=gtile[:, 0:w],
                                                                 op0=ALU.mult, op1=ALU.mult), reads=[o32_b, sm_b, g_b], writes=[og_b])
                else:
                    Sd.op("dve", lambda e: e.tensor_tensor(out=o3, in0=o3, in1=bc_last(sm[:, 4:4 + nh], hd), op=ALU.mult),
                          reads=[o32_b, sm_b], writes=[o32_b])
                    Sd.op("dve", lambda e: e.tensor_tensor(out=og[:, 0:w], in0=o32[:, 0:w], in1=gtile[:, 0:w], op=ALU.mult),
                          reads=[o32_b, g_b], writes=[og_b])

            def s_store():
                Sd.dma(lambda e: e.dma_start(out=ocat_d[tile_i * 128:(tile_i + 1) * 128, col0:col0 + w], in_=og[:, 0:w]),
                       og_b.name, reads=[og_b])
            return [s_sq, s_red, s_ln, s_exp, s_scale, s_store]

        fsqs = [A.alloc("fsq", 256, F32) for _ in range(4)]
        p2_mark = A.mark()

        lamt, lamt_b = A.alloc("lamt", 4 * 64, F32)
        lams, lams_b = A.alloc("lams", 8, F32)
        gA, gA_b = A.alloc("gA", 128, F32)
        for j in range(4):
            Sd.dma(lambda e, j=j: e.dma_start(out=lamt[:, j * 64:(j + 1) * 64], in_=lam_d[j][l].partition_broadcast(128)),
                   lamt_b.name, writes=[lamt_b])
        Sd.dma(lambda e: e.dma_start(out=gA, in_=subln_d[l].partition_broadcast(128)), gA_b.name, writes=[gA_b])
        Sd.op("dve", lambda e: e.tensor_tensor(out=lamt[:, 0:64], in0=lamt[:, 0:64], in1=lamt[:, 64:128], op=ALU.mult),
              reads=[lamt_b], writes=[lamt_b])
        Sd.op("dve", lambda e: e.tensor_tensor(out=lamt[:, 128:192], in0=lamt[:, 128:192], in1=lamt[:, 192:256], op=ALU.mult),
              reads=[lamt_b], writes=[lamt_b])
        Sd.op("dve", lambda e: e.tensor_reduce(out=lams[:, 0:1], in_=lamt[:, 0:64], axis=AX.X, op=ALU.add), reads=[lamt_b], writes=[lams_b])
        Sd.op("dve", lambda e: e.tensor_reduce(out=lams[:, 1:2], in_=lamt[:, 128:192], axis=AX.X, op=ALU.add), reads=[lamt_b], writes=[lams_b])
        Sd.op("act", lambda e: e.activation(out=lams[:, 2:4], in_=lams[:, 0:2], func=AF.Exp), reads=[lams_b], writes=[lams_b])
        Sd.op("dve", lambda e: e.tensor_tensor(out=lams[:, 4:5], in0=lams[:, 3:4], in1=lams[:, 2:3], op=ALU.subtract), reads=[lams_b], writes=[lams_b])
        Sd.op("dve", lambda e: e.tensor_scalar(out=lams[:, 4:5], in0=lams[:, 4:5], scalar1=-lam_init, scalar2=None, op0=ALU.add), reads=[lams_b], writes=[lams_b])
        Sd.op("dve", lambda e: e.tensor_scalar(out=gA, in0=gA, scalar1=(1.0 - lam_init), scalar2=None, op0=ALU.mult), reads=[gA_b], writes=[gA_b])

        qT, qT_b = A.alloc("qT", S, BF16)
        kT0, kT0_b = A.alloc("kT0", S, BF16)
        kT1, kT1_b = A.alloc("kT1", S, BF16)
        vA, vA_b = A.alloc("vA", NT * 129, BF16)
        vA3 = vA.rearrange("p (t e) -> p t e", t=NT)
        wP, wP_b = A.alloc("wP", 8 * 256, BF16)
        wP3 = wP.rearrange("p (k n) -> p k n", k=8)
        ropeC = [A.alloc("ropeC", 512, F32) for _ in range(2)]
        ropeS = [A.alloc("ropeS", 512, F32) for _ in range(2)]
        rt1 = [A.alloc("rt1", 512, F32) for _ in range(2)]
        rt2 = [A.alloc("rt2", 512, F32) for _ in range(2)]
        PT = [A.alloc("PT", 512, BF16) for _ in range(5)]
        Sd.op("pool", lambda e: e.memset(kT0[64:128, :], 0.0), writes=[kT0_b])
        Sd.op("pool", lambda e: e.memset(kT1[0:64, :], 0.0), writes=[kT1_b])
        Sd.op("dve", lambda e: e.memset(wP, 0.0), writes=[wP_b])
        Sd.op("pool", lambda e: e.memset(vA3[:, :, 128:129], 1.0), writes=[vA_b])

        for h in range(4):
            wv, w_b = load_w([(h * 128, 128), (512 + h * 128, 128), (1024 + h * 128, 128)])
            w4 = wv[:, :, 0:256].rearrange("p k (g d) -> p k g d", g=4)
            wP4 = wP3.rearrange("p k (g d) -> p k g d", g=4)
            Sd.op("dve", lambda e: e.tensor_copy(out=wP4[:, :, :, 0:8], in_=w4[:, :, :, 8:16]), reads=[w_b], writes=[wP_b])
            Sd.op("dve", lambda e: e.tensor_copy(out=wP4[:, :, :, 8:16], in_=w4[:, :, :, 0:8]), reads=[w_b], writes=[wP_b])
            for tb in range(NB):
                rc, rc_b = ropeC[tb % 2]
                rs, rs_b = ropeS[tb % 2]
                Sd.dma(lambda e, rc=rc, tb=tb: e.dma_start(out=rc, in_=ropeC_d[:, tb * 512:(tb + 1) * 512]), rc_b.name, writes=[rc_b])
                Sd.dma(lambda e, rs=rs, tb=tb: e.dma_start(out=rs, in_=ropeS_d[:, tb * 512:(tb + 1) * 512]), rs_b.name, writes=[rs_b])
                for qk in range(2):
                    p1, p1_b = proj_fm(wv, w_b, qk * 128, None, None, tb, 6, None)
                    p2, p2_b = proj_fm(wP3, wP_b, qk * 128, None, None, tb, 7, None)
                    t1, t1_b = rt1[qk]
                    t2, t2_b = rt2[qk]
                    Sd.op("dve", lambda e, t1=t1, p1=p1, rc=rc: e.tensor_tensor(out=t1, in0=p1[:, 0:512], in1=rc, op=ALU.mult),
                          reads=[p1_b, rc_b], writes=[t1_b])
                    Sd.op("dve", lambda e, t2=t2, p2=p2, rs=rs: e.tensor_tensor(out=t2, in0=p2[:, 0:512], in1=rs, op=ALU.mult),
                          reads=[p2_b, rs_b], writes=[t2_b])
                    sl = slice(tb * 512, (tb + 1) * 512)
                    if qk == 0:
                        Sd.op("dve", lambda e, t1=t1, t2=t2, sl=sl: e.tensor_tensor(out=qT[:, sl], in0=t1, in1=t2, op=ALU.add),
                              reads=[t1_b, t2_b], writes=[qT_b])
                    else:
                        Sd.op("dve", lambda e, t1=t1, t2=t2, sl=sl: e.tensor_tensor(out=kT0[0:64, sl], in0=t1[0:64, :], in1=t2[0:64, :], op=ALU.add),
                              reads=[t1_b, t2_b], writes=[kT0_b])
                        Sd.op("dve", lambda e, t1=t1, t2=t2, sl=sl: e.tensor_tensor(out=kT1[64:128, sl], in0=t1[64:128, :], in1=t2[64:128, :], op=ALU.add),
                              reads=[t1_b, t2_b], writes=[kT1_b])
                pt, pb_ = pbank[6 + (tb % 2)]

                def mmv(e, tb=tb, pt=pt, wv=wv):
                    ins = None
                    for j in range(4):
                        ti = tb * 4 + j
                        for kc in range(8):
                            ins = e.matmul(pt[:, j * 128:(j + 1) * 128], lhsT=hT3[:, kc, ti * 128:(ti + 1) * 128], rhs=wv[:, kc, 256:384],
                                           start=(kc == 0), stop=(kc == 7))
                    return ins
                Sd.op("pe", mmv, reads=[w_b, hT_b], writes=[pb_])
                Sd.op("act", lambda e, tb=tb, pt=pt: e.activation(out=vA3[:, tb * 4:(tb + 1) * 4, 0:128],
                                                                 in_=pt[:, 0:512].rearrange("p (j e) -> p j e", j=4), func=AF.Copy),
                      reads=[pb_], writes=[vA_b])

            if l == 0 and h in (0, 1, 2):
                dump(f"qT{h}", qT, qT_b)
                dump(f"kT0_{h}", kT0, kT0_b)
                dump(f"kT1_{h}", kT1, kT1_b)
                dump(f"vA{h}", vA, vA_b)
                dump(f"wP{h}", wP, wP_b)
            steps = []
            for Q in range(NQ):
                for kt in range(2 * Q + 2):
                    steps.append((Q, kt))

            def emit_S(si):
                Q, kt = steps[si]
                pt, pb_ = pbank[(0, 1, 6, 7)[si % 4]]

                def mm(e, Q=Q, kt=kt, pt=pt):
                    e.matmul(pt[:, 0:256], lhsT=kT0[:, kt * 128:(kt + 1) * 128], rhs=qT[:, Q * 256:(Q + 1) * 256], start=True, stop=True)
                    return e.matmul(pt[:, 256:512], lhsT=kT1[:, kt * 128:(kt + 1) * 128], rhs=qT[:, Q * 256:(Q + 1) * 256], start=True, stop=True)
                Sd.op("pe", mm, reads=[kT0_b, kT1_b, qT_b], writes=[pb_])

            def emit_rest(si):
                Q, kt = steps[si]
                r = kt - 2 * Q
                pt, pb_ = pbank[(0, 1, 6, 7)[si % 4]]
                P, P_b = PT[si % 5]
                Sd.op("act", lambda e: e.activation(out=P, in_=pt[:, 0:512], func=AF.Exp, scale=0.125), reads=[pb_], writes=[P_b])
                P3 = P.rearrange("p (n q) -> p n q", n=2)
                if r >= 0:
                    Sd.op("dve", lambda e: e.tensor_tensor(out=P3[:, :, r * 128:(r + 1) * 128], in0=P3[:, :, r * 128:(r + 1) * 128],
                                                           in1=bc_mid(mcc, 2), op=ALU.mult), reads=[P_b, mcc_b], writes=[P_b])
                first = (kt == 0)
                ab = 2 + 2 * (Q % 2)

                def pv(e):
                    ins = None
                    for n in range(2):
                        acc = pbank[ab + n][0]
                        for qs in range(max(r, 0), 2):
                            last = (kt == 2 * Q + qs)
                            ins = e.matmul(acc[:, qs * 129:(qs + 1) * 129], lhsT=P3[:, n, qs * 128:(qs + 1) * 128], rhs=vA3[:, kt, :],
                                           start=(first and qs == 0), stop=last, skip_group_check=True)
                    return ins
                Sd.op("pe", pv, reads=[P_b, vA_b], writes=[pbank[ab][1], pbank[ab + 1][1]])
                for qs in (range(2) if kt == 2 * Q + 1 else ()):
                    ti = Q * 2 + qs
                    k = fin_slot()
                    sm, sm_b = fsm[k]
                    o32, o32_b = ftmp[k]
                    a0, a0_b = pbank[ab]
                    a1, a1_b = pbank[ab + 1]
                    c = qs * 129

                    def s_comb(sm=sm, sm_b=sm_b, o32=o32, o32_b=o32_b, c=c):
                        Sd.op("dve", lambda e: e.reciprocal(out=sm[:, 8:9], in_=a0[:, c + 128:c + 129]), reads=[a0_b], writes=[sm_b])
                        Sd.op("dve", lambda e: e.reciprocal(out=sm[:, 9:10], in_=a1[:, c + 128:c + 129]), reads=[a1_b], writes=[sm_b])
                        Sd.op("dve", lambda e: e.tensor_tensor(out=sm[:, 9:10], in0=sm[:, 9:10], in1=lams[:, 4:5], op=ALU.mult),
                              reads=[sm_b, lams_b], writes=[sm_b])
                        Sd.op("dve", lambda e: e.tensor_scalar(out=o32[:, 0:128], in0=a0[:, c:c + 128], scalar1=sm[:, 8:9], scalar2=None, op0=ALU.mult),
                              reads=[a0_b, sm_b], writes=[o32_b])
                        Sd.op("dve", lambda e: e.scalar_tensor_tensor(out=o32[:, 0:128], in0=a1[:, c:c + 128], scalar=sm[:, 9:10], in1=o32[:, 0:128],
                                                                     op0=ALU.mult, op1=ALU.add), reads=[a1_b, sm_b, o32_b], writes=[o32_b])
                    fin_submit([s_comb] + norm_stages(k, o32, o32_b, 1, 128, gA, gA_b, ti, h * 128), k)

            n = len(steps)
            for si in range(n + 3):
                if si < n:
                    emit_S(si)
                if si >= 3:
                    emit_rest(si - 3)
                    tick()
        fin_flush()
        Sd.barrier()
        A.reset(p2_mark)

        qB, qB_b = A.alloc("qB", 2 * S, BF16)
        qB3 = qB.rearrange("p (f s) -> p f s", f=2)
        kB = [A.alloc("kB", S, BF16) for _ in range(4)]
        vB, vB_b = A.alloc("vB", NT * 4 * 65, BF16)
        vB4 = vB.rearrange("p (t h e) -> p t h e", t=NT, h=4)
        BT, BT_b = A.alloc("BT", 5 * 512, F32)
        BT3 = BT.rearrange("p (d x) -> p d x", d=5)
        Rt, Rt_b = A.alloc("Rt", 5 * 128, F32)
        antiI, antiI_b = A.alloc("antiI", 128, F32)
        mB, mB_b = A.alloc("mB", 256, F32)
        gnb, gnb_b = A.alloc("gnb", 256, F32)
        BTb, BTb_b = A.alloc("BTb", 5 * 512, BF16)
        BTb3 = BTb.rearrange("p (d x) -> p d x", d=5)
        PB = [A.alloc("PB", 512, BF16) for _ in range(3)]
        Sd.dma(lambda e: e.dma_start(out=antiI, in_=antiI_d), antiI_b.name, writes=[antiI_b])
        Sd.dma(lambda e: e.dma_start(out=mB, in_=mB_d), mB_b.name, writes=[mB_b])
        Sd.dma(lambda e: e.dma_start(out=gnb, in_=gnb_d[l].partition_broadcast(128)), gnb_b.name, writes=[gnb_b])
        E_b = Buf("E_dram")
        c4, c4_b = A.alloc("c4", 1, F32)
        ctile, ctile_b = A.alloc("ctile", 128, F32)
        c256, c256_b = A.alloc("c256", 4, F32)
        Sd.dma(lambda e: e.dma_start(out=c4[0:4, 0:1], in_=relb_d[l, :, 256:257], allow_slow_non_contiguous=True), c4_b.name, writes=[c4_b])
        Sd.dma(lambda e: e.dma_start(out=c256.rearrange("p (h o) -> p h o", o=1),
                                     in_=AP(relb_d.tensor, relb_d[l, 0:1, 256:257].offset, [[0, 128], [257, 4], [1, 1]]),
                                     allow_slow_non_contiguous=True),
               c256_b.name, writes=[c256_b])
        Sd.op("dve", lambda e: e.tensor_copy(out=ctile[0:4, :], in_=AP(c4.tensor, c4.offset, [[c4.ap[0][0], 4], [0, 128]])),
              reads=[c4_b], writes=[ctile_b])
        Sd.dma(lambda e: e.dma_start(out=E_d[:, 0:256], in_=relb_d[l, :, 1:257]), "E_dram", writes=[E_b])
        Sd.dma(lambda e: e.dma_start(out=E_d[:, 256:384], in_=ctile[0:4, :]), "E_dram", reads=[ctile_b], writes=[E_b])
        for hh in range(4):
            k_, k_b = kB[hh]
            if hh % 2 == 0:
                Sd.op("pool", lambda e, k_=k_: e.memset(k_[64:128, :], 0.0), writes=[k_b])
            else:
                Sd.op("pool", lambda e, k_=k_: e.memset(k_[0:64, :], 0.0), writes=[k_b])
        Sd.op("pool", lambda e: e.memset(vB4[:, :, :, 64:65], 1.0), writes=[vB_b])
        for hh in range(4):
            Sd.dma(lambda e, hh=hh: e.dma_start(out=Rt[:, 0:256].rearrange("p (d j) -> p d j", d=2),
                                                in_=AP(E_d.tensor, E_d[hh:hh + 1, 0:1].offset, [[1, 128], [128, 2], [1, 128]])),
                   Rt_b.name, reads=[E_b], writes=[Rt_b])
            pt, pb_ = pbank[4 + hh % 2]
            Sd.op("pe", lambda e, pt=pt: e.matmul(pt[:, 0:256], lhsT=antiI, rhs=Rt[:, 0:256], start=True, stop=True),
                  reads=[antiI_b, Rt_b], writes=[pb_])
            Sd.op("dve", lambda e, pt=pt, hh=hh: e.tensor_copy(
                out=BT3[:, 0:2, hh * 128:(hh + 1) * 128], in_=pt[:, 0:256].rearrange("p (d j) -> p d j", d=2)),
                reads=[pb_], writes=[BT_b])
        for d in range(2, 5):
            Sd.op("dve", lambda e, d=d: e.tensor_copy(out=BT3[:, d, :].rearrange("p (h j) -> p h j", h=4), in_=bc_last(c256, 128)),
                  reads=[c256_b], writes=[BT_b])
        Sd.op("dve", lambda e: e.tensor_tensor(out=BT3[:, 0, :].rearrange("p (h j) -> p h j", h=4), in0=BT3[:, 0, :].rearrange("p (h j) -> p h j", h=4),
                                               in1=bc_mid(mB[:, 0:128], 4), op=ALU.add), reads=[BT_b, mB_b], writes=[BT_b])
        Sd.op("dve", lambda e: e.tensor_tensor(out=BT3[:, 4, :].rearrange("p (h j) -> p h j", h=4), in0=BT3[:, 4, :].rearrange("p (h j) -> p h j", h=4),
                                               in1=bc_mid(mB[:, 128:256], 4), op=ALU.add), reads=[BT_b, mB_b], writes=[BT_b])
        Sd.op("dve", lambda e: e.tensor_copy(out=BTb, in_=BT), reads=[BT_b], writes=[BTb_b])
        wq, wq_b = load_w([(1536, 256)])
        for tb in range(NB):
            for ft in range(2):
                pt, pb_ = proj_fm(wq, wq_b, ft * 128, None, None, tb, 4 + ft, None)
                evac_copy("act" if ft else "dve", qB3[:, ft, tb * 512:(tb + 1) * 512], pt[:, 0:512], [pb_], [qB_b])
        wk, wk_b = load_w([(1792, 256)])
        for tb in range(NB):
            for ft in range(2):
                pt, pb_ = proj_fm(wk, wk_b, ft * 128, None, None, tb, 4 + ft, None)
                k0, k0_b = kB[ft * 2]
                k1, k1_b = kB[ft * 2 + 1]
                evac_copy("act", k0[0:64, tb * 512:(tb + 1) * 512], pt[0:64, 0:512], [pb_], [k0_b], scale=0.125)
                evac_copy("dve", k1[64:128, tb * 512:(tb + 1) * 512], pt[64:128, 0:512], [pb_], [k1_b], scale=0.125)
        wvv, wvv_b = load_w([(2048, 256)])
        for tp in range(NT // 2):
            pt, pb_ = pbank[4 + tp % 2]

            def mmv(e, tp=tp, pt=pt):
                ins = None
                for j in range(2):
                    ti = tp * 2 + j
                    for kc in range(8):
                        ins = e.matmul(pt[:, j * 256:(j + 1) * 256], lhsT=hT3[:, kc, ti * 128:(ti + 1) * 128], rhs=wvv[:, kc, 0:256],
                                       start=(kc == 0), stop=(kc == 7))
                return ins
            Sd.op("pe", mmv, reads=[wvv_b, hT_b], writes=[pb_])
            Sd.op("act" if tp % 2 else "dve",
                  (lambda e, tp=tp, pt=pt: e.activation(out=vB4[:, tp * 2:tp * 2 + 2, :, 0:64],
                                                        in_=pt[:, 0:512].rearrange("p (j h e) -> p j h e", j=2, h=4), func=AF.Copy))
                  if tp % 2 else
                  (lambda e, tp=tp, pt=pt: e.tensor_copy(out=vB4[:, tp * 2:tp * 2 + 2, :, 0:64],
                                                         in_=pt[:, 0:512].rearrange("p (j h e) -> p j h e", j=2, h=4))),
                  reads=[pb_], writes=[vB_b])
        stepsB = []
        for i in range(NT):
            for d in range(4, -1, -1):
                if i - d >= 0:
                    stepsB.append((i, d))

        def emitB_S(si):
            i, d = stepsB[si]
            j = i - d
            pt, pb_ = pbank[(0, 1, 4)[si % 3]]

            def mm(e):
                ins = None
                for hh in range(4):
                    ins = e.matmul(pt[:, hh * 128:(hh + 1) * 128], lhsT=kB[hh][0][:, j * 128:(j + 1) * 128],
                                   rhs=qB3[:, hh // 2, i * 128:(i + 1) * 128], start=(hh == 0), stop=False, skip_group_check=True)
                return e.matmul(pt[:, 0:512], lhsT=ident, rhs=BTb3[:, d, :], start=False, stop=True, skip_group_check=True)
            Sd.op("pe", mm, reads=[kB[0][1], kB[1][1], kB[2][1], kB[3][1], qB_b, BTb_b, ident_b], writes=[pb_])

        def emitB_rest(si):
            i, d = stepsB[si]
            j = i - d
            pt, pb_ = pbank[(0, 1, 4)[si % 3]]
            P, P_b = PB[si % 3]
            Sd.op("act", lambda e: e.activation(out=P, in_=pt[:, 0:512], func=AF.Exp), reads=[pb_], writes=[P_b])
            first = (d == min(4, i))
            acc, acc_b = pbank[2 + (i % 2)]

            def pv(e):
                ins = None
                for hh in range(4):
                    ins = e.matmul(acc[:, hh * 65:(hh + 1) * 65], lhsT=P[:, hh * 128:(hh + 1) * 128], rhs=vB4[:, j, hh, :],
                                   start=(first and hh == 0), stop=(d == 0), skip_group_check=True)
                return ins
            Sd.op("pe", pv, reads=[P_b, vB_b], writes=[acc_b])
            if d == 0:
                k = fin_slot()
                sm, sm_b = fsm[k]
                o32, o32_b = ftmp[k]
                a3 = acc[:, 0:260].rearrange("p (h e) -> p h e", h=4)

                def s_comb():
                    Sd.op("dve", lambda e: e.reciprocal(out=sm[:, 8:12], in_=a3[:, :, 64]), reads=[acc_b], writes=[sm_b])
                    Sd.op("dve", lambda e: e.tensor_tensor(out=o32.rearrange("p (h e) -> p h e", h=4), in0=a3[:, :, 0:64],
                                                           in1=bc_last(sm[:, 8:12], 64), op=ALU.mult), reads=[acc_b, sm_b], writes=[o32_b])
                fin_submit([s_comb] + norm_stages(k, o32, o32_b, 4, 64, gnb, gnb_b, i, 512), k)

        nB = len(stepsB)
        for si in range(nB + 2):
            if si < nB:
                emitB_S(si)
            if si >= 2:
                emitB_rest(si - 2)
                tick()
        fin_flush()
        Sd.barrier()
        A.reset(p2_mark)

        qC, qC_b = A.alloc("qC", S, BF16)
        kC = [A.alloc("kC", S, BF16) for _ in range(2)]
        vC, vC_b = A.alloc("vC", NT * 128, BF16)
        vC4 = vC.rearrange("p (t h e) -> p t h e", t=NT, h=2)
        gnc, gnc_b = A.alloc("gnc", 256, F32)
        Sd.dma(lambda e: e.dma_start(out=gnc, in_=gnc_d[l].partition_broadcast(128)), gnc_b.name, writes=[gnc_b])
        Ebuf = [A.alloc("Ebuf", 512, F32) for _ in range(2)]
        Sp = [[A.alloc("Sp", 512, BF16) for _ in range(2)] for _ in range(2)]
        SpSum = [A.alloc("SpSum", 512, BF16) for _ in range(2)]
        AT = [[A.alloc("AT", 512, BF16) for _ in range(2)] for _ in range(2)]
        Sd.op("pool", lambda e: e.memset(kC[0][0][64:128, :], 0.0), writes=[kC[0][1]])
        Sd.op("pool", lambda e: e.memset(kC[1][0][0:64, :], 0.0), writes=[kC[1][1]])
        zbank = [[0, 1], [6, 7]]
        for hp in range(2):
            wv, w_b = load_w([(2304 + hp * 128, 128), (2560 + hp * 128, 128), (2816 + hp * 128, 128)])
            for tb in range(NB):
                pt, pb_ = proj_fm(wv, w_b, 0, None, None, tb, 4, None)
                evac_copy("dve", qC[:, tb * 512:(tb + 1) * 512], pt[:, 0:512], [pb_], [qC_b])
                pt, pb_ = proj_fm(wv, w_b, 128, None, None, tb, 5, None)
                evac_copy("act", kC[0][0][0:64, tb * 512:(tb + 1) * 512], pt[0:64, 0:512], [pb_], [kC[0][1]], scale=0.125)
                evac_copy("dve", kC[1][0][64:128, tb * 512:(tb + 1) * 512], pt[64:128, 0:512], [pb_], [kC[1][1]], scale=0.125)
                pt, pb_ = pbank[5]

                def mmv(e, tb=tb, pt=pt, wv=wv):
                    ins = None
                    for j in range(4):
                        ti = tb * 4 + j
                        for kc in range(8):
                            ins = e.matmul(pt[:, j * 128:(j + 1) * 128], lhsT=hT3[:, kc, ti * 128:(ti + 1) * 128], rhs=wv[:, kc, 256:384],
                                           start=(kc == 0), stop=(kc == 7))
                    return ins
                Sd.op("pe", mmv, reads=[w_b, hT_b], writes=[pb_])
                Sd.op("act", lambda e, tb=tb, pt=pt: e.activation(out=vC4[:, tb * 4:(tb + 1) * 4, :, :],
                                                                 in_=pt[:, 0:512].rearrange("p (j h e) -> p j h e", j=4, h=2), func=AF.Copy),
                      reads=[pb_], writes=[vC_b])
            stepsC = []
            for Q in range(NB):
                for kt in range(4 * Q + 3, -1, -1):
                    stepsC.append((Q, kt))

            def cols(Q, kt):
                r = kt - 4 * Q
                return r, max(r, 0) * 128

            def emitC_QK(si):
                Q, kt = stepsC[si]
                r, c0 = cols(Q, kt)
                for s in range(2):
                    z, z_b = pbank[zbank[si % 2][s]]
                    Sd.op("pe", lambda e, z=z, s=s, kt=kt, Q=Q, c0=c0: e.matmul(
                        z[:, c0:512], lhsT=kC[s][0][:, kt * 128:(kt + 1) * 128], rhs=qC[:, Q * 512 + c0:(Q + 1) * 512],
                        start=True, stop=False, skip_group_check=True), reads=[kC[s][1], qC_b], writes=[z_b])

            def emitC_rest(si):
                Q, kt = stepsC[si]
                r, c0 = cols(Q, kt)
                firstQ = (kt == 4 * Q + 3)
                for s in range(2):
                    if firstQ:
                        Sd.op("pool", lambda e, s=s: e.memset(SpSum[s][0], 0.0), writes=[SpSum[s][1]])
                for s in range(2):
                    z, z_b = pbank[zbank[si % 2][s]]
                    E_, E_b2 = pbank[4 + s] if si % 2 == 0 else Ebuf[s]
                    sp_, sp_b = Sp[s][si % 2]
                    Sd.op("act", lambda e, z=z, E_=E_, c0=c0: e.activation(out=E_[:, c0:512], in_=z[:, c0:512], func=AF.Exp),
                          reads=[z_b], writes=[E_b2])
                    Sd.op("act", lambda e, E_=E_, sp_=sp_, c0=c0: e.activation(out=sp_[:, c0:512], in_=E_[:, c0:512], func=AF.Ln, bias=one_t[:, 0:1]),
                          reads=[E_b2, one_b], writes=[sp_b])
                    if r >= 0:
                        Sd.op("dve", lambda e, sp_=sp_, c0=c0: e.tensor_tensor(out=sp_[:, c0:c0 + 128], in0=sp_[:, c0:c0 + 128], in1=mtri, op=ALU.mult),
                              reads=[sp_b, mtri_b], writes=[sp_b])

                    def cum(e, z=z, sp_=sp_, s=s, c0=c0):
                        ins = e.matmul(z[:, c0:512], lhsT=negU, rhs=sp_[:, c0:512], start=False, stop=firstQ, skip_group_check=True)
                        if not firstQ:
                            ins = e.matmul(z[:, c0:512], lhsT=negOnes, rhs=SpSum[s][0][:, c0:512], start=False, stop=True, skip_group_check=True)
                        return ins
                    Sd.op("pe", cum, reads=[sp_b, negU_b, negOnes_b, SpSum[s][1]], writes=[z_b])
                for s in range(2):
                    z, z_b = pbank[zbank[si % 2][s]]
                    sp_, sp_b = Sp[s][si % 2]
                    a_, a_b = AT[s][si % 2]
                    Sd.op("act", lambda e, z=z, a_=a_, c0=c0: e.activation(out=a_[:, c0:512], in_=z[:, c0:512], func=AF.Exp),
                          reads=[z_b], writes=[a_b])
                    if r >= 0:
                        Sd.op("dve", lambda e, a_=a_, c0=c0: e.tensor_tensor(out=a_[:, c0:c0 + 128], in0=a_[:, c0:c0 + 128], in1=mtri, op=ALU.mult),
                              reads=[a_b, mtri_b], writes=[a_b])
                    if kt > 0:
                        Sd.op("dve", lambda e, s=s, sp_=sp_, c0=c0: e.tensor_tensor(out=SpSum[s][0][:, c0:512], in0=SpSum[s][0][:, c0:512],
                                                                                      in1=sp_[:, c0:512], op=ALU.add),
                              reads=[SpSum[s][1], sp_b], writes=[SpSum[s][1]])
                    acc, acc_b = pbank[2 + (Q % 2)]

                    def pv(e, a_=a_, s=s, acc=acc):
                        ins = None
                        for qs in range(max(r, 0), 4):
                            firstq = (kt == 4 * Q + 3) and s == 0 and qs == 3
                            ins = e.matmul(acc[:, s * 256 + qs * 64:s * 256 + (qs + 1) * 64], lhsT=a_[:, qs * 128:(qs + 1) * 128],
                                           rhs=vC4[:, kt, s, :], start=firstq, stop=(kt == 0), skip_group_check=True)
                        return ins
                    Sd.op("pe", pv, reads=[a_b, vC_b], writes=[acc_b])
                if kt == 0:
                    acc, acc_b = pbank[2 + (Q % 2)]
                    a4 = acc[:, 0:512].rearrange("p (s q e) -> p s q e", s=2, q=4)
                    for qs in range(4):
                        ti = Q * 4 + qs
                        k = fin_slot()
                        o32, o32_b = ftmp[k]

                        def s_comb(o32=o32, o32_b=o32_b, qs=qs, acc_b=acc_b, a4=a4):
                            Sd.op("dve", lambda e: e.tensor_copy(out=o32[:, 0:128].rearrange("p (s e) -> p s e", s=2), in_=a4[:, :, qs, :]),
                                  reads=[acc_b], writes=[o32_b])
                        fin_submit([s_comb] + norm_stages(k, o32, o32_b, 2, 64, gnc[:, hp * 128:(hp + 1) * 128], gnc_b, ti, 768 + hp * 128), k)

            nC = len(stepsC)
            for si in range(nC + 1):
                if si < nC:
                    emitC_QK(si)
                if si >= 1:
                    emitC_rest(si - 1)
                    tick()
        fin_flush()
        Sd.barrier()
        if dbg and l == 0:
            Sd.dma(lambda e: e.dma_start(out=dbg_d["ocat"], in_=ocat_d), "dbg1")
            Sd.barrier()

        A.reset(base_mark)
        wd, wd_b = A.alloc("wd", 32 * D, BF16)
        wd3 = wd.rearrange("p (k n) -> p k n", k=32)
        gpl, gpl_b = A.alloc("gpl", D, F32)
        gin2, gin2_b = A.alloc("gin2", 8, F32)
        Sd.dma(lambda e: e.dma_start(out=gpl, in_=g_post_mlp_d[l].partition_broadcast(128)), gpl_b.name, writes=[gpl_b])
        Sd.dma(lambda e: e.dma_start(out=gin2, in_=g_pre_mlp_d[l].rearrange("(k p) -> p k", p=128), allow_slow_non_contiguous=True),
               gin2_b.name, writes=[gin2_b])
        p4_mark = A.mark()
        stg4 = [A.alloc("stg4", 2048, F32) for _ in range(3)]
        ci = [0]

        def load_wd_chunk(fc2, engs=("dve", "act")):
            sg, sg_b = stg4[ci[0] % 3]
            Sd.dma(lambda e: e.dma_start(out=sg.rearrange("p (f n) -> p f n", f=2),
                                         in_=w_down_d[l][fc2 * 256:(fc2 + 1) * 256, :].rearrange("(f p) n -> p f n", p=128)),
                   sg_b.name, writes=[sg_b])
            eng = engs[ci[0] % len(engs)]
            dst = wd3[:, fc2 * 2:(fc2 + 1) * 2, :]
            src_ = sg.rearrange("p (f n) -> p f n", f=2)
            if eng == "act":
                Sd.op("act", lambda e: e.activation(out=dst, in_=src_, func=AF.Copy), reads=[sg_b], writes=[wd_b])
            else:
                Sd.op(eng, lambda e: e.tensor_copy(out=dst, in_=src_), reads=[sg_b], writes=[wd_b])
            ci[0] += 1

        wo, wo_b = A.alloc("wo", 8 * D, BF16)
        wo3 = wo.rearrange("p (k n) -> p k n", k=8)
        gpm, gpm_b = A.alloc("gpm", D, F32)
        Sd.dma(lambda e: e.dma_start(out=gpm, in_=g_post_mix_d[l].partition_broadcast(128)), gpm_b.name, writes=[gpm_b])
        for kc in range(8):
            sg, sg_b = stg4[ci[0] % 3]
            ci[0] += 1
            Sd.dma(lambda e, sg=sg, kc=kc: e.dma_start(out=sg[:, 0:1024], in_=w_out_d[l][kc * 128:(kc + 1) * 128, :]), sg_b.name, writes=[sg_b])
            evac_copy("dve" if kc % 2 else "act", wo3[:, kc, :], sg[:, 0:1024], [sg_b], [wo_b])
        NS = 5
        oc = [A.alloc("oc", D, BF16) for _ in range(NS)]
        x3 = [A.alloc("x3", D, F32) for _ in range(NS)]
        oT = [A.alloc("oT", D, BF16) for _ in range(2)]
        xn = [A.alloc("xn", D, F32) for _ in range(3)]
        h2 = [A.alloc("h2", D, BF16) for _ in range(2)]
        h2o = [A.alloc("h2o", D, BF16) for _ in range(2)]
        s3 = [A.alloc("s3", 8, F32) for _ in range(4)]
        junk3s = [A.alloc("junk3", D, BF16) for _ in range(4)]
        jc3 = [0]

        def next_junk3():
            jc3[0] += 1
            return junk3s[jc3[0] % 4]

        def p3_load(i):
            oc_, oc_b = oc[i % NS]
            x_, x_b = x3[i % NS]
            Sd.dma(lambda e: e.dma_start(out=oc_, in_=ocat_d[i * 128:(i + 1) * 128, :]), oc_b.name, writes=[oc_b])
            Sd.dma(lambda e: e.dma_start(out=x_, in_=x_src[i * 128:(i + 1) * 128, :]), x_b.name, writes=[x_b])

        def st_tr(i):
            k = i % 2
            oc_, oc_b = oc[i % NS]
            pv = pbf(k)

            def tr(e):
                ins = None
                for kc in range(8):
                    ins = e.transpose(out=pv[:, kc * 128:(kc + 1) * 128], in_=oc_[:, kc * 128:(kc + 1) * 128], identity=ident)
                return ins
            Sd.op("pe", tr, reads=[oc_b, ident_b], writes=[pbank[k][1]])

        def st_evac(i):
            k = i % 2
            oT_, oT_b = oT[k]
            pv = pbf(k)
            Sd.op("dve", lambda e: e.tensor_copy(out=oT_, in_=pv), reads=[pbank[k][1]], writes=[oT_b])

        def st_mm(i):
            k = i % 2
            oT_, oT_b = oT[k]
            oT3 = oT_.rearrange("p (k t) -> p k t", k=8)
            for hf in range(2):
                pt, pb_ = pbank[2 + k * 2 + hf]

                def mm(e, pt=pt, hf=hf):
                    ins = None
                    for kc in range(8):
                        ins = e.matmul(pt[:, 0:512], lhsT=oT3[:, kc, :], rhs=wo3[:, kc, hf * 512:(hf + 1) * 512], start=(kc == 0), stop=(kc == 7))
                    return ins
                Sd.op("pe", mm, reads=[oT_b, wo_b], writes=[pb_])

        def st_sqy(i):
            k = i % 2
            s_, s_b = s3[i % 4]
            for hf in range(2):
                pt, pb_ = pbank[2 + k * 2 + hf]
                junk3, junk3_b = next_junk3()
                Sd.op("act", lambda e, pt=pt, hf=hf, junk3=junk3: e.activation(out=junk3[:, 0:512], in_=pt[:, 0:512], func=AF.Square, accum_out=s_[:, hf:hf + 1]),
                      reads=[pb_], writes=[junk3_b, s_b])

        def st_ssqadd(i):
            s_, s_b = s3[i % 4]
            Sd.op("dve", lambda e: e.tensor_tensor(out=s_[:, 2:3], in0=s_[:, 0:1], in1=s_[:, 1:2], op=ALU.add), reads=[s_b], writes=[s_b])

        def st_rstd1(i):
            s_, s_b = s3[i % 4]
            rstd_ops(s_[:, 2:3], s_[:, 3:4], D, s_b)

        def st_xn(i):
            k = i % 2
            x_, x_b = x3[i % NS]
            xn_, xn_b = xn[i % 3]
            s_, s_b = s3[i % 4]
            for hf in range(2):
                pt, pb_ = pbank[2 + k * 2 + hf]
                Sd.op("dve", lambda e, pt=pt, hf=hf: e.scalar_tensor_tensor(
                    out=xn_[:, hf * 512:(hf + 1) * 512], in0=pt[:, 0:512], scalar=s_[:, 3:4], in1=gpm[:, hf * 512:(hf + 1) * 512],
                    op0=ALU.mult, op1=ALU.mult), reads=[pb_, s_b, gpm_b], writes=[xn_b])
            Sd.op("dve", lambda e: e.tensor_tensor(out=xn_, in0=xn_, in1=x_, op=ALU.add), reads=[xn_b, x_b], writes=[xn_b])
            Sd.dma(lambda e: e.dma_start(out=xa_d[i * 128:(i + 1) * 128, :], in_=xn_), xn_b.name + "s", reads=[xn_b])

        def st_sqx(i):
            xn_, xn_b = xn[i % 3]
            s_, s_b = s3[i % 4]
            junk3, junk3_b = next_junk3()
            Sd.op("act", lambda e: e.activation(out=junk3, in_=xn_, func=AF.Square, accum_out=s_[:, 4:5]),
                  reads=[xn_b], writes=[junk3_b, s_b])
            rstd_ops(s_[:, 4:5], s_[:, 5:6], D, s_b)

        def st_h2(i):
            xn_, xn_b = xn[i % 3]
            h2_, h2_b = h2[i % 2]
            s_, s_b = s3[i % 4]
            Sd.op("dve", lambda e: e.tensor_scalar(out=h2_, in0=xn_, scalar1=s_[:, 5:6], scalar2=None, op0=ALU.mult),
                  reads=[xn_b, s_b], writes=[h2_b])

        def st_tr2(i):
            k = i % 2
            h2_, h2_b = h2[k]
            pv2 = pbf(6 + k)

            def tr2(e):
                ins = None
                for kc in range(8):
                    ins = e.transpose(out=pv2[:, kc * 128:(kc + 1) * 128], in_=h2_[:, kc * 128:(kc + 1) * 128], identity=ident)
                return ins
            Sd.op("pe", tr2, reads=[h2_b, ident_b], writes=[pbank[6 + k][1]])

        def st_h2o(i):
            k = i % 2
            h2o_, h2o_b = h2o[k]
            pv2 = pbf(6 + k)
            Sd.op("act", lambda e: e.activation(out=h2o_, in_=pv2, func=AF.Copy), reads=[pbank[6 + k][1]], writes=[h2o_b])
            Sd.dma(lambda e: e.dma_start(out=h2T_d[i], in_=h2o_), h2o_b.name + "s", reads=[h2o_b])

        def ok(t):
            return 0 <= t < NT

        for i in range(min(3, NT)):
            p3_load(i)
        wd_next = 0
        for j in range(NT + 4):
            if ok(j):
                st_tr(j)
            if ok(j - 2):
                st_ssqadd(j - 2)
                st_rstd1(j - 2)
            if ok(j - 4):
                st_tr2(j - 4)
            if ok(j):
                st_evac(j)
            if ok(j - 3):
                st_sqx(j - 3)
            if ok(j - 1):
                st_mm(j - 1)
            if ok(j - 2):
                st_xn(j - 2)
            if ok(j - 4):
                st_h2o(j - 4)
            if ok(j - 3):
                st_h2(j - 3)
            if ok(j - 1):
                st_sqy(j - 1)
            if j + 3 < NT:
                p3_load(j + 3)
            if j % 2 == 1 and wd_next < 16:
                load_wd_chunk(wd_next)
                wd_next += 1
        while wd_next < 16:
            load_wd_chunk(wd_next, engs=("dve", "act"))
            wd_next += 1
        Sd.barrier()
        if dbg and l == 0:
            Sd.dma(lambda e: e.dma_start(out=dbg_d["xa"], in_=xa_d), "dbg2")
            Sd.barrier()

        A.reset(p4_mark)
        wu, wu_b = A.alloc("wu", 8 * DFF, BF16)
        wu3 = wu.rearrange("p (k n) -> p k n", k=8)
        m4 = A.mark()
        stg4 = [A.alloc("stg4b", 2048, F32) for _ in range(3)]
        ci = 0
        for kc in range(8):
            for c in range(2):
                sg, sg_b = stg4[ci % 3]
                Sd.dma(lambda e, sg=sg, kc=kc, c=c: e.dma_start(out=sg, in_=w_up_d[l][kc * 128:(kc + 1) * 128, c * 2048:(c + 1) * 2048]),
                       sg_b.name, writes=[sg_b])
                if ci % 2 == 0:
                    Sd.op("dve", lambda e, sg=sg, kc=kc, c=c: e.tensor_scalar(out=wu3[:, kc, c * 2048:(c + 1) * 2048], in0=sg, scalar1=gin2[:, kc:kc + 1],
                                                                             scalar2=None, op0=ALU.mult), reads=[sg_b, gin2_b], writes=[wu_b])
                else:
                    Sd.op("act", lambda e, sg=sg, kc=kc, c=c: e.activation(out=wu3[:, kc, c * 2048:(c + 1) * 2048], in_=sg, func=AF.Copy,
                                                                          scale=gin2[:, kc:kc + 1]), reads=[sg_b, gin2_b], writes=[wu_b])
                ci += 1
        Sd.barrier()
        A.reset(m4)
        TB = 4
        NMB = NT // TB
        uT, uT_b = A.alloc("uT", 32 * 512, BF16)
        uT3 = uT.rearrange("p (f t) -> p f t", f=32)
        uR = [A.alloc("uR", 512, BF16) for _ in range(3)]
        hblk, hblk_b = A.alloc("hblk", TB * 1024, BF16)
        hb4 = hblk.rearrange("p (t k x) -> p t k x", t=TB, k=8)
        x4 = [A.alloc("x4", D, F32) for _ in range(4)]
        o4 = [A.alloc("o4", D, F32) for _ in range(2)]
        s4 = [A.alloc("s4", 8, F32) for _ in range(2)]
        junk4s = [A.alloc("junk4", 512, BF16) for _ in range(3)]
        jc4 = [0]
        uTb = [Buf(f"uTpart{j}") for j in range(32)]

        def p4_load_h(b):
            Sd.dma(lambda e: e.dma_start(out=hblk.rearrange("p (t x) -> p t x", t=TB),
                                         in_=h2T_d[TB * b:TB * (b + 1)].rearrange("t p x -> p t x")), hblk_b.name, writes=[hblk_b])

        def p4_load_x(ti):
            x_, x_b = x4[ti % 4]
            Sd.dma(lambda e: e.dma_start(out=x_, in_=xa_d[ti * 128:(ti + 1) * 128, :]), x_b.name, writes=[x_b])

        p4_load_h(0)
        for ti in range(min(4, NT)):
            p4_load_x(ti)
        for b in range(NMB):
            for fc in range(32):
                pt, pb_ = pbank[fc % 2]

                def mm(e, pt=pt, fc=fc):
                    ins = None
                    for kc in range(8):
                        ins = e.matmul(pt[:, 0:512].rearrange("p (t x) -> p t x", t=TB), lhsT=wu3[:, kc, fc * 128:(fc + 1) * 128],
                                       rhs=hb4[:, :, kc, :], start=(kc == 0), stop=(kc == 7))
                    return ins
                Sd.op("pe", mm, reads=[wu_b, hblk_b], writes=[pb_])
                ur, ur_b = uR[fc % 3]
                Sd.op("dve", lambda e, pt=pt, ur=ur: e.tensor_scalar(out=ur, in0=pt[:, 0:512], scalar1=0.0, scalar2=None, op0=ALU.max),
                      reads=[pb_], writes=[ur_b])
                Sd.op("pool", lambda e, ur=ur, fc=fc: e.tensor_tensor(out=uT3[:, fc, :], in0=ur, in1=ur, op=ALU.mult),
                      reads=[ur_b], writes=[uTb[fc]])
            if b + 1 < NMB:
                p4_load_h(b + 1)
            for t in range(TB):
                ti = b * TB + t
                k = ti % 2
                x_, x_b = x4[ti % 4]
                o_, o_b = o4[k]
                s_, s_b = s4[k]
                for hf in range(2):
                    pt, pb_ = pbank[2 + k * 2 + hf]

                    def mm(e, pt=pt, hf=hf, t=t):
                        ins = None
                        for fc in range(32):
                            ins = e.matmul(pt[:, 0:512], lhsT=uT3[:, fc, t * 128:(t + 1) * 128], rhs=wd3[:, fc, hf * 512:(hf + 1) * 512],
                                           start=(fc == 0), stop=(fc == 31))
                        return ins
                    Sd.op("pe", mm, reads=uTb + [wd_b], writes=[pb_])
                    jc4[0] += 1
                    junk4, junk4_b = junk4s[jc4[0] % 3]
                    Sd.op("act", lambda e, pt=pt, s_=s_, hf=hf, junk4=junk4: e.activation(out=junk4, in_=pt[:, 0:512], func=AF.Square, accum_out=s_[:, hf:hf + 1]),
                          reads=[pb_], writes=[junk4_b, s_b])
                Sd.op("dve", lambda e, s_=s_: e.tensor_tensor(out=s_[:, 2:3], in0=s_[:, 0:1], in1=s_[:, 1:2], op=ALU.add), reads=[s_b], writes=[s_b])
                rstd_ops(s_[:, 2:3], s_[:, 3:4], D, s_b)
                for hf in range(2):
                    pt, pb_ = pbank[2 + k * 2 + hf]
                    Sd.op("dve", lambda e, pt=pt, o_=o_, s_=s_, hf=hf: e.scalar_tensor_tensor(
                        out=o_[:, hf * 512:(hf + 1) * 512], in0=pt[:, 0:512], scalar=s_[:, 3:4], in1=gpl[:, hf * 512:(hf + 1) * 512],
                        op0=ALU.mult, op1=ALU.mult), reads=[pb_, s_b, gpl_b], writes=[o_b])
                Sd.op("pool", lambda e, o_=o_, x_=x_: e.tensor_tensor(out=o_, in0=o_, in1=x_, op=ALU.add), reads=[o_b, x_b], writes=[o_b])
                Sd.dma(lambda e, o_=o_, ti=ti: e.dma_start(out=x_dst[ti * 128:(ti + 1) * 128, :], in_=o_), o_b.name + "s", reads=[o_b])
                if ti + 4 < NT:
                    p4_load_x(ti + 4)
        Sd.barrier()

    one_t, one_b = A.alloc("one", 1, F32)
    Sd.op("pool", lambda e: e.memset(one_t, 1.0), writes=[one_b])
    base_mark = A.mark()
    base_names = dict(A.namecnt)
    src = x_d
    for l in range(L):
        dst = out_d if l == L - 1 else xb_d
        layer(l, src, dst)
        src = dst
    Sd.barrier()
    Sd.finalize(nc, es)
    es.close()
    stats = dict(n_ops={e: len(Sd.ops[e]) for e in Sd.ENGS}, n_sems=len(Sd.sems), n_waits=Sd.n_waits, arena_peak=A.peak)
    return nc, stats


def host_consts(S):
    bf = ml_dtypes.bfloat16
    c = {}
    c["c_ident"] = np.eye(128, dtype=np.float32).astype(bf)
    pos = np.arange(S, dtype=np.float32)
    inv_freq = (np.float32(500000.0) ** (-np.arange(0, 16, 2, dtype=np.float32) / np.float32(16))).astype(np.float32)
    ang = (pos[:, None] * inv_freq[None, :]).astype(np.float32)
    cs, sn = np.cos(ang).astype(np.float32), np.sin(ang).astype(np.float32)
    C = np.ones((128, S), np.float32)
    Sg = np.zeros((128, S), np.float32)
    for n in range(2):
        for i in range(8):
            C[n * 64 + i] = cs[:, i]
            C[n * 64 + 8 + i] = cs[:, i]
            Sg[n * 64 + i] = -sn[:, i]
            Sg[n * 64 + 8 + i] = sn[:, i]
    c["c_ropeC"] = C
    c["c_ropeS"] = Sg
    p = np.arange(128)[:, None]
    j = np.arange(128)[None, :]
    c["c_mask_cc"] = np.where((p >= 64) & (j < 64), 0.0, 1.0).astype(np.float32).astype(bf)
    c["c_mask_tri"] = (p < j).astype(np.float32).astype(bf)
    c["c_negU"] = np.where(p >= j, -1.0, 0.0).astype(np.float32).astype(bf)
    c["c_negOnes"] = np.full((128, 128), -1.0, np.float32).astype(bf)
    m0 = np.where((p >= 64) & (j < 64), -30000.0, 0.0).astype(np.float32)
    m4 = np.where((p < 64) & (j >= 64), -30000.0, 0.0).astype(np.float32)
    c["c_maskB"] = np.concatenate([m0, m4], axis=1)
    c["c_antiI"] = np.eye(128, dtype=np.float32)[::-1].copy()
    return c


_CACHE = {}

PARAM_NAMES = ["w_in", "w_out", "w_up", "w_down", "norm_pre_mix", "norm_post_mix", "norm_pre_mlp", "norm_post_mlp",
               "lam_q1", "lam_k1", "lam_q2", "lam_k2", "subln_a", "rel_bias", "gn_b", "gn_c"]


def kernel(**inputs):
    x = np.ascontiguousarray(np.asarray(inputs["x"], dtype=np.float32))
    B, S, _ = x.shape
    L = int(np.asarray(inputs["w_in"]).shape[0])
    key = (S, L)
    if key not in _CACHE:
        _CACHE[key] = build(S, L)[0]
    nc = _CACHE[key]
    consts = host_consts(S)
    shared = {n: np.ascontiguousarray(np.asarray(inputs[n], dtype=np.float32)) for n in PARAM_NAMES}
    shared.update(consts)
    n_cores = 8
    in_maps = []
    for c in range(n_cores):
        m = dict(shared)
        m["x"] = x[c % B]
        in_maps.append(m)
    res = run_bass_kernel_spmd(nc, in_maps, core_ids=list(range(n_cores)))
    out = np.stack([np.asarray(res.results[b]["out"], dtype=np.float32) for b in range(B)], axis=0)
    return out
```
